# Optimizing a Trainium2 kernel written in Bass

```python
import math
import jax, jax.numpy as jnp
from jax import lax
import numpy as np

D_MODEL = 2048
BATCH = 4
SEQ = 2048
DEPTH = 1
DEC_BATCH = 128
DEC_SEQ = 4
PAST_LEN = 16384
PAGE_SIZE = 128

D_RNN = D_MODEL // 2
RNN_BLOCKS = 16
RNN_BW = D_RNN // RNN_BLOCKS
CONV_W = 4
LRU_C = 8.0
D_GMLP = D_MODEL // 2
CHUNK = 128
GMLP_GROUPS = 8
GMLP_GW = D_GMLP // GMLP_GROUPS
D_FF = 5632
ALPHA = (2.0 * DEPTH) ** 0.25
BETA = (8.0 * DEPTH) ** -0.25
LN_EPS = 1e-5
N_MOD = 9
IN_COLS = 2 * D_RNN + 2 * D_GMLP + 2 * D_MODEL

kernel_name = "hybrid_rglru_chunkgmlp_macaron_decode_step"


def layer_norm(x, g, b):
    xf = x.astype(jnp.float32)
    mu = jnp.mean(xf, axis=-1, keepdims=True)
    var = jnp.mean(jnp.square(xf - mu), axis=-1, keepdims=True)
    y = (xf - mu) * lax.rsqrt(var + LN_EPS)
    return (y * g.astype(jnp.float32) + b.astype(jnp.float32)).astype(x.dtype)


def swiglu(u, w_gu, w_down):
    gv = u @ w_gu
    g, v = jnp.split(gv, 2, axis=-1)
    return (jax.nn.silu(g) * v) @ w_down


def causal_conv(xpad, w, b):
    T = xpad.shape[1] - CONV_W + 1
    out = b
    for k in range(CONV_W):
        out = out + xpad[:, k:k + T] * w[k]
    return out


def rg_lru(x, h0, wa, ba, wx, bx, lam, reset_first):
    B, T, _ = x.shape
    f32 = jnp.float32
    xf = x.astype(f32)
    xb = xf.reshape(B, T, RNN_BLOCKS, RNN_BW)
    r = jax.nn.sigmoid(jnp.einsum('btnd,nde->btne', xb, wa.astype(f32)).reshape(B, T, D_RNN) + ba.astype(f32))
    i = jax.nn.sigmoid(jnp.einsum('btnd,nde->btne', xb, wx.astype(f32)).reshape(B, T, D_RNN) + bx.astype(f32))
    log_a = -LRU_C * r * jax.nn.softplus(-lam.astype(f32))
    a = jnp.exp(log_a)
    mult = jnp.sqrt(-jnp.expm1(2.0 * log_a))
    if reset_first:
        mult = jnp.where((jnp.arange(T) == 0)[None, :, None], 1.0, mult)
    bt = mult * i * xf

    def step(h, ab):
        a_t, b_t = ab
        h = a_t * h + b_t
        return h, h

    hT, ys = lax.scan(step, h0.astype(f32), (jnp.swapaxes(a, 0, 1), jnp.swapaxes(bt, 0, 1)))
    return jnp.swapaxes(ys, 0, 1), hT


def chunk_spatial_mix(v, w_s, b_s):
    B, T, _ = v.shape
    n = min(T, CHUNK)
    mask = jnp.tril(jnp.ones((n, n), dtype=bool))
    w = jnp.where(mask[None], w_s[:, :n, :n], 0.0)
    vb = v.reshape(B, T // n, n, GMLP_GROUPS, GMLP_GW)
    s = jnp.einsum('gts,bcsgd->bctgd', w, vb) + jnp.transpose(b_s[:, :n])[None, None, :, :, None]
    return s.reshape(B, T, D_GMLP)


def token_mixing(u, conv_buf, h0, reset_first, p):
    proj = u @ p['w_in']
    o1 = D_RNN
    o2 = o1 + D_RNN
    o3 = o2 + D_GMLP
    o4 = o3 + D_GMLP
    o5 = o4 + D_MODEL
    xr, gr, gu, gv, ga, gb = (proj[..., :o1], proj[..., o1:o2], proj[..., o2:o3],
                              proj[..., o3:o4], proj[..., o4:o5], proj[..., o5:])
    xpad = jnp.concatenate([conv_buf.astype(xr.dtype), xr], axis=1)
    new_buf = xpad[:, -(CONV_W - 1):]
    xc = causal_conv(xpad, p['conv_w'], p['conv_b'])
    y_lru, hT = rg_lru(xc, h0, p['lru_wa'], p['lru_ba'], p['lru_wx'], p['lru_bx'], p['lru_lambda'], reset_first)
    y_a = (y_lru.astype(u.dtype) * jax.nn.gelu(gr)) @ p['w_pa']
    vn = layer_norm(gv, p['gmlp_ln_g'], p['gmlp_ln_b'])
    s = chunk_spatial_mix(vn, p['gmlp_ws'], p['gmlp_bs'])
    y_b = (gu * s) @ p['w_pb']
    m = jax.nn.sigmoid(ga) * y_a + jax.nn.sigmoid(gb) * y_b
    return m @ p['w_out'], new_buf, hT.astype(u.dtype), vn


def decoder_layer(x, c, conv_buf, h0, reset_first, p):
    mod = jax.nn.silu(c) @ p['w_ada'] + p['b_ada']
    mod = mod.reshape(c.shape[0], 1, N_MOD, D_MODEL)
    sh1, sc1, g1, sh2, sc2, g2, sh3, sc3, g3 = [mod[:, :, k] for k in range(N_MOD)]
    ln_g, ln_b = p['ln_g'], p['ln_b']
    u = x * (1.0 + sc1) + sh1
    x = layer_norm(ALPHA * x + 0.5 * g1 * swiglu(u, p['ffn1_w_gu'], p['ffn1_w_down']), ln_g[0], ln_b[0])
    u = x * (1.0 + sc2) + sh2
    mix, new_buf, hT, vn = token_mixing(u, conv_buf, h0, reset_first, p)
    x = layer_norm(ALPHA * x + g2 * mix, ln_g[1], ln_b[1])
    u = x * (1.0 + sc3) + sh3
    x = layer_norm(ALPHA * x + 0.5 * g3 * swiglu(u, p['ffn2_w_gu'], p['ffn2_w_down']), ln_g[2], ln_b[2])
    return x, new_buf, hT, vn


def setup_inputs(seed: int = 0) -> dict:
    key = jax.random.key(seed)
    ks = jax.random.split(key, 32)
    f32 = jnp.float32
    L = DEPTH

    def nrm(k, shape, scale):
        return jax.random.normal(k, shape, f32) * scale

    a8 = jax.random.uniform(ks[20], (L, D_RNN), f32, 0.9, 0.999)
    a = a8 ** (1.0 / LRU_C)
    lam = jnp.log(a) - jnp.log1p(-a)
    return {
        'x_prompt': nrm(ks[0], (BATCH, SEQ, D_MODEL), 1.0),
        'x_sample': nrm(ks[1], (DEC_BATCH, DEC_SEQ, D_MODEL), 1.0),
        'state_conv': nrm(ks[2], (L, DEC_BATCH, CONV_W - 1, D_RNN), 1.0),
        'state_h': nrm(ks[3], (L, DEC_BATCH, D_RNN), 0.5),
        'c_prompt': nrm(ks[4], (BATCH, D_MODEL), 1.0),
        'c_sample': nrm(ks[5], (DEC_BATCH, D_MODEL), 1.0),
        'w_ada': nrm(ks[6], (L, D_MODEL, N_MOD * D_MODEL), 0.5 * D_MODEL ** -0.5),
        'b_ada': nrm(ks[7], (L, N_MOD * D_MODEL), 0.02),
        'ffn1_w_gu': nrm(ks[8], (L, D_MODEL, 2 * D_FF), D_MODEL ** -0.5),
        'ffn1_w_down': nrm(ks[9], (L, D_FF, D_MODEL), BETA * D_FF ** -0.5),
        'ffn2_w_gu': nrm(ks[10], (L, D_MODEL, 2 * D_FF), D_MODEL ** -0.5),
        'ffn2_w_down': nrm(ks[11], (L, D_FF, D_MODEL), BETA * D_FF ** -0.5),
        'w_in': nrm(ks[12], (L, D_MODEL, IN_COLS), D_MODEL ** -0.5),
        'conv_w': nrm(ks[13], (L, CONV_W, D_RNN), CONV_W ** -0.5),
        'conv_b': nrm(ks[14], (L, D_RNN), 0.02),
        'lru_wa': nrm(ks[15], (L, RNN_BLOCKS, RNN_BW, RNN_BW), RNN_BW ** -0.5),
        'lru_ba': nrm(ks[16], (L, D_RNN), 0.02),
        'lru_wx': nrm(ks[17], (L, RNN_BLOCKS, RNN_BW, RNN_BW), RNN_BW ** -0.5),
        'lru_bx': nrm(ks[18], (L, D_RNN), 0.02),
        'lru_lambda': lam,
        'gmlp_ln_g': 1.0 + nrm(ks[21], (L, D_GMLP), 0.02),
        'gmlp_ln_b': nrm(ks[22], (L, D_GMLP), 0.02),
        'gmlp_ws': nrm(ks[23], (L, GMLP_GROUPS, CHUNK, CHUNK), CHUNK ** -0.5),
        'gmlp_bs': 1.0 + nrm(ks[24], (L, GMLP_GROUPS, CHUNK), 0.1),
        'w_pa': nrm(ks[25], (L, D_RNN, D_MODEL), D_RNN ** -0.5),
        'w_pb': nrm(ks[26], (L, D_GMLP, D_MODEL), D_GMLP ** -0.5),
        'w_out': nrm(ks[27], (L, D_MODEL, D_MODEL), BETA * D_MODEL ** -0.5),
        'ln_g': 1.0 + nrm(ks[28], (L, 3, D_MODEL), 0.02),
        'ln_b': nrm(ks[29], (L, 3, D_MODEL), 0.02),
    }


def reference(x_prompt, x_sample, state_conv, state_h, c_prompt, c_sample,
              w_ada, b_ada, ffn1_w_gu, ffn1_w_down, ffn2_w_gu, ffn2_w_down,
              w_in, conv_w, conv_b, lru_wa, lru_ba, lru_wx, lru_bx, lru_lambda,
              gmlp_ln_g, gmlp_ln_b, gmlp_ws, gmlp_bs, w_pa, w_pb, w_out, ln_g, ln_b):
    yp, ys = x_prompt, x_sample
    bp = x_prompt.shape[0]
    conv_p, h_p, conv_s, h_s, v_s = [], [], [], [], []
    for l in range(DEPTH):
        p = {
            'w_ada': w_ada[l], 'b_ada': b_ada[l],
            'ffn1_w_gu': ffn1_w_gu[l], 'ffn1_w_down': ffn1_w_down[l],
            'ffn2_w_gu': ffn2_w_gu[l], 'ffn2_w_down': ffn2_w_down[l],
            'w_in': w_in[l], 'conv_w': conv_w[l], 'conv_b': conv_b[l],
            'lru_wa': lru_wa[l], 'lru_ba': lru_ba[l], 'lru_wx': lru_wx[l], 'lru_bx': lru_bx[l],
            'lru_lambda': lru_lambda[l],
            'gmlp_ln_g': gmlp_ln_g[l], 'gmlp_ln_b': gmlp_ln_b[l],
            'gmlp_ws': gmlp_ws[l], 'gmlp_bs': gmlp_bs[l],
            'w_pa': w_pa[l], 'w_pb': w_pb[l], 'w_out': w_out[l],
            'ln_g': ln_g[l], 'ln_b': ln_b[l],
        }
        zero_buf = jnp.zeros((bp, CONV_W - 1, D_RNN), x_prompt.dtype)
        zero_h = jnp.zeros((bp, D_RNN), jnp.float32)
        yp, cbp, hbp, _ = decoder_layer(yp, c_prompt, zero_buf, zero_h, True, p)
        ys, cbs, hbs, vns = decoder_layer(ys, c_sample, state_conv[l], state_h[l], False, p)
        conv_p.append(cbp)
        h_p.append(hbp)
        conv_s.append(cbs)
        h_s.append(hbs)
        v_s.append(vns)
    return (yp, ys, jnp.stack(conv_p), jnp.stack(h_p), jnp.stack(conv_s), jnp.stack(h_s), jnp.stack(v_s))
```

```python
import contextlib
import numpy as np
import concourse.bass as bass
import concourse.mybir as mybir
from concourse.bass_utils import run_bass_kernel_spmd

F32 = mybir.dt.float32
BF16 = mybir.dt.bfloat16
ALU = mybir.AluOpType
AF = mybir.ActivationFunctionType

NCORES = 8
D = 2048
DFF = 5632
NJ = DFF // 128
NT = 1092
NPL = 1028
SOFF = 1028
TILES = [(0, 512), (512, 512), (1024, 68)]
ALPHA = 2.0 ** 0.25
EPSP = 1e-5 / (ALPHA * ALPHA)
NMODR = 17
GRP = 4
GRAN = 8

PV_BADA, PV_LNG, PV_LNB, PV_CW, PV_CB, PV_BA, PV_BX, PV_LAM = 0, 144, 192, 240, 272, 280, 288, 296
PV_FLG, PV_OH, PV_SCONV, PV_SH0, PV_CT = 304, 308, 316, 700, 828
PV_ID = 828 + 16 * NMODR
NPV = PV_ID + NMODR
PM_BDA, PM_BDX, PM_WST, PM_MASK = 0, 1024, 2048, 3072
NPM = 3200
PM64_WSS, PM64_MASK = 0, 512
NPM64 = 576
PB_G, PB_B, PB_BSP, PB_BSS = 0, 1024, 2048, 3072
NPB = 3584
OS_NCP, OS_NHP, OS_NCS, OS_NHS = 0, 24, 32, 416
NOS = 544


def I(name, *args, **kw):
    return (name, args, kw)


class Op:
    __slots__ = ("eng", "sem", "val", "dma")

    def __init__(self, eng, sem, val, dma):
        self.eng, self.sem, self.val, self.dma = eng, sem, val, dma


class Cell:
    __slots__ = ("w", "r")

    def __init__(self):
        self.w = {}
        self.r = {}


class Eng:
    def __init__(self, name, clock):
        self.name, self.clock, self.seq = name, clock, 0
        self.prog = []
        self.waited = {}


class Sched:
    def __init__(self, nc, stack):
        self.nc = nc
        self.engs = {}
        for n in ("pe", "act", "dve", "pool", "sp"):
            self.engs[n] = Eng(n, stack.enter_context(nc.semaphore("clk_" + n)))
        self.dsems = {}
        for q, cnt in (("pool", 40), ("sp", 24)):
            self.dsems[q] = [[stack.enter_context(nc.semaphore("d%s%d" % (q, i))), 0] for i in range(cnt)]
        self.drr = {"pool": 0, "sp": 0}
        self.ccsem = [stack.enter_context(nc.semaphore("ccsem")), 0]
        self.cells = {}
        self.nops = 0
        self.out_ops = []

    def _cells(self, rng):
        sp, lo, hi = rng
        cs = self.cells
        out = []
        for g in range(lo // GRAN, (hi + GRAN - 1) // GRAN):
            k = (sp, g)
            c = cs.get(k)
            if c is None:
                c = cs[k] = Cell()
            out.append(c)
        return out

    def op(self, en, fn, reads=(), writes=(), dma=False, inc=None):
        E = self.engs[en]
        raw, oth = {}, {}
        rcells = [c for r in reads for c in self._cells(r)]
        wcells = [c for r in writes for c in self._cells(r)]
        for c in rcells:
            for o in c.w.values():
                raw[id(o)] = o
        for c in wcells:
            for o in c.w.values():
                oth[id(o)] = o
            for o in c.r.values():
                oth[id(o)] = o
        waits = []

        def addwait(sem, val):
            k = id(sem)
            if E.waited.get(k, -1) >= val:
                return
            E.waited[k] = val
            waits.append((sem, val))

        for o in raw.values():
            if o.eng == en and not o.dma and not dma and en == "pe":
                continue
            addwait(o.sem, o.val)
        for o in oth.values():
            if id(o) in raw:
                continue
            if o.eng == en and not o.dma and not dma and en == "pe":
                continue
            addwait(o.sem, o.val)
        if dma:
            if inc is not None:
                ent = self.ccsem
            else:
                pool = self.dsems[en]
                i = self.drr[en]
                self.drr[en] = (i + 1) % len(pool)
                ent = pool[i]
            if ent[1] > 0:
                addwait(ent[0], ent[1])
            step = 16 if inc is None else inc
            ent[1] += step
            o = Op(en, ent[0], ent[1], True)
            key = ("d", self.nops)
        else:
            E.seq += 1
            step = 1
            o = Op(en, E.clock, E.seq, False)
            key = en
        self.nops += 1
        for c in rcells:
            c.r[key] = o
        for c in wcells:
            c.w = {key: o}
            c.r = {}
        E.prog.append((waits, fn, o.sem, step))
        return o

    def final_wait(self, en, ops):
        E = self.engs[en]
        waits = []
        for o in ops:
            waits.append((o.sem, o.val))
        E.prog.append((waits, None, None, None))

    def emit_all(self, en, e):
        for waits, fn, sem, step in self.engs[en].prog:
            for s, v in waits:
                e.wait_ge(s, v)
            if fn is None:
                continue
            lst = fn if isinstance(fn, list) else [fn]
            ins = None
            for name, args, kw in lst:
                ins = getattr(e, name)(*args, **kw)
            ins.then_inc(sem, step)


class T:
    def __init__(self, ap, lo, esz, n):
        self.ap, self.lo, self.esz, self.n = ap, lo, esz, n

    def r(self, a=0, b=None):
        if b is None:
            b = self.n
        return ("sb", self.lo + a * self.esz, self.lo + b * self.esz)

    def v3(self, inner):
        return self.ap.rearrange("p (c n) -> p c n", n=inner)


class Arena:
    def __init__(self, ap, nwords):
        self.A, self.nwords, self.pos = ap, nwords, 0

    def alloc(self, n, dt):
        esz = 4 if dt == F32 else 2
        words = (n * esz + 3) // 4
        words = (words + 7) // 8 * 8
        lo = self.pos
        assert lo + words <= self.nwords, "arena overflow %d + %d > %d" % (lo, words, self.nwords)
        self.pos += words
        v = self.A[:, lo:lo + words]
        if dt == BF16:
            v = v.bitcast(BF16)
        v = v[:, 0:n]
        return T(v, lo * 4, esz, n)


def build_program():
    nc = bass.Bass("TRN2", target_bir_lowering=False)

    def din(name, shape):
        return nc.dram_tensor(name, list(shape), F32, kind="ExternalInput").ap()

    def dout(name, shape):
        return nc.dram_tensor(name, list(shape), F32, kind="ExternalOutput").ap()

    xT = din("xT", [16, 128, NT])
    pvec = din("pvec", [128, NPV])
    pmat = din("pmat", [128, NPM])
    pmat64 = din("pmat64", [64, NPM64])
    pbc = din("pbc", [128, NPB])
    wada = din("wada", [36, 128, 8192])
    wgu = [din("wgu1", [NJ, 128, 4096]), din("wgu2", [NJ, 128, 4096])]
    wdn = [din("wd1", [NJ, 128, 2048]), din("wd2", [NJ, 128, 2048])]
    win = din("win", [64, 128, 2048])
    wgv = din("wgv", [128, 16384])
    wpa = din("wpa", [16, 128, 1024])
    wpb = din("wpb", [16, 128, 1024])
    wout = din("wout", [16, 128, 2048])
    yT = dout("yT", [16, 128, NT])
    osmall = dout("osmall", [128, NOS])
    nvs = dout("nvs", [64, 1024])
    xspill = nc.dram_tensor("xspill", [16, 128, NT], F32).ap()
    cin = nc.dram_tensor("cin", [128, 8], F32)
    cout = nc.dram_tensor("cout", [2 * 128, 8], F32)

    stack = contextlib.ExitStack()
    with stack:
        NW = 53184
        arena_t = stack.enter_context(nc.sbuf_tensor("arena", [128, NW], F32))
        ps_t = stack.enter_context(nc.psum_tensor("ps", [128, 8, 512], F32))
        S = Sched(nc, stack)
        AR = Arena(arena_t, NW)

        def PS(b, n0, n1, p0=0, p1=128):
            return ps_t[p0:p1, b, n0:n1]

        def PR(b, n0=0, n1=512):
            return ("ps", b * 2048, (b + 1) * 2048)

        X = AR.alloc(16 * NT, F32)
        U = AR.alloc(16 * NT, BF16)
        NRU = 10
        RING = AR.alloc(NRU * 2048, BF16)
        MOD = AR.alloc(144 * NMODR, F32)
        PV = AR.alloc(NPV, F32)
        PM = AR.alloc(NPM, BF16)
        PM64 = AR.alloc(NPM64, BF16)
        SC = AR.alloc(16 * NMODR, BF16)
        ONES = AR.alloc(128, BF16)
        CL = AR.alloc(8, F32)
        CONSTS = AR.alloc(8, F32)
        OSM = AR.alloc(NOS, F32)
        HIN = AR.alloc(8, F32)
        HLE = AR.alloc(8, F32)
        PEND = AR.alloc(8, F32)
        GATH = AR.alloc(16, F32)
        MROW = AR.alloc(2 * 512, F32)
        TMPS = AR.alloc(64, F32)
        scr_mark = AR.pos
        X3, U3, MOD3 = X.v3(NT), U.v3(NT), MOD.v3(NMODR)
        XA_LO, XA_WORDS = X.lo // 4, 16 * NT

        def xr(c, a=0, b=NT):
            return X.r(c * NT + a, c * NT + b)

        def ur(c, a=0, b=NT):
            return U.r(c * NT + a, c * NT + b)

        def modr(m0, m1):
            return MOD.r(m0 * NMODR, m1 * NMODR)

        ring_pos = [0]

        def ring_load(src, nelem):
            units = (nelem + 2047) // 2048
            if ring_pos[0] + units > NRU:
                ring_pos[0] = 0
            u0 = ring_pos[0]
            ring_pos[0] += units
            dst = RING.ap[:, u0 * 2048:u0 * 2048 + nelem]
            rng = RING.r(u0 * 2048, u0 * 2048 + nelem)
            S.op("pool", I("dma_start", out=dst, in_=src), writes=[rng], dma=True)
            return dst, rng

        def mm_group(out_ap, pairs, reads, wr):
            n = len(pairs)
            fn = [I("matmul", out_ap, l, r, start=(i == 0), stop=(i == n - 1)) for i, (l, r) in enumerate(pairs)]
            return S.op("pe", fn, reads=reads, writes=[wr])

        S.op("sp", I("dma_start", out=PV.ap, in_=pvec), writes=[PV.r()], dma=True)
        for c in range(16):
            S.op("sp", (lambda c: I("dma_start", out=X3[:, c, :], in_=xT[c]))(c), writes=[xr(c)], dma=True)
        S.op("pool", I("dma_start", out=PM.ap, in_=pmat), writes=[PM.r()], dma=True)
        S.op("pool", I("dma_start", out=PM64.ap[0:64, :], in_=pmat64), writes=[PM64.r()], dma=True)
        S.op("dve", I("memset", ONES.ap, 1.0 / D), writes=[ONES.r()])
        S.op("dve", I("memset", CONSTS.ap[:, 0:1], EPSP), writes=[CONSTS.r()])
        S.op("dve", I("memset", CONSTS.ap[:, 1:2], 1e-5), writes=[CONSTS.r()])
        S.op("dve", I("memset", CONSTS.ap[:, 2:3], 1.0), writes=[CONSTS.r()])
        pv = PV.ap
        S.op("act", I("activation", SC.ap, pv[:, PV_CT:PV_CT + 16 * NMODR], AF.Silu),
             reads=[PV.r()], writes=[SC.r()])
        WST3 = PM.ap[:, PM_WST:PM_WST + 1024].rearrange("p (g t) -> p g t", g=8)
        MK = PM.ap[:, PM_MASK:PM_MASK + 128]
        S.op("dve", I("tensor_tensor", WST3, WST3, MK.unsqueeze(1).to_broadcast([128, 8, 128]), ALU.mult),
             reads=[PM.r()], writes=[PM.r()])
        WSS3 = PM64.ap[0:64, PM64_WSS:PM64_WSS + 512].rearrange("p (g t) -> p g t", g=8)
        MK64 = PM64.ap[0:64, PM64_MASK:PM64_MASK + 64]
        S.op("dve", I("tensor_tensor", WSS3, WSS3, MK64.unsqueeze(1).to_broadcast([64, 8, 64]), ALU.mult),
             reads=[PM64.r()], writes=[PM64.r()])
        S.op("act", I("activation", CL.ap, pv[:, PV_LAM:PV_LAM + 8], AF.Exp, scale=-1.0),
             reads=[PV.r()], writes=[CL.r()])
        S.op("act", I("activation", CL.ap, CL.ap, AF.Ln, bias=CONSTS.ap[:, 2:3]),
             reads=[CL.r(), CONSTS.r()], writes=[CL.r()])
        S.op("dve", I("tensor_scalar", CL.ap, CL.ap, -8.0, None, ALU.mult), reads=[CL.r()], writes=[CL.r()])

        BA = pv[:, PV_BADA:PV_BADA + 144]

        ada_slot = [0]
        IDENT = pv[0:NMODR, PV_ID:PV_ID + NMODR]

        def ada_blocks(nb0, nb1, rbank, tbank, stage=None):
            m0, m1 = nb0 * 4, nb1 * 4
            n = m1 - m0
            for bi, nb in enumerate(range(nb0, nb1)):
                if stage is None:
                    w, wrng = ring_load(wada[nb], 8192)
                else:
                    stg, si = stage
                    w = stg.ap[:, si * 8192:(si + 1) * 8192]
                    wrng = stg.r(si * 8192, (si + 1) * 8192)
                    S.op("pool", I("dma_start", out=w, in_=wada[nb]), writes=[wrng], dma=True)
                mm_group(PS(rbank, 0, 512, 0, NMODR),
                         [(SC.ap[:, k * NMODR:(k + 1) * NMODR], w[:, k * 512:(k + 1) * 512]) for k in range(16)],
                         [wrng, SC.r()], PR(rbank))
                sl = ada_slot[0] % 2
                ada_slot[0] += 1
                S.op("act", I("activation", MROW.ap[0:NMODR, sl * 512:(sl + 1) * 512], PS(rbank, 0, 512, 0, NMODR), AF.Copy),
                     reads=[PR(rbank)], writes=[MROW.r(sl * 512, (sl + 1) * 512)])
                fn = [I("transpose", PS(tbank, (bi * 4 + q) * NMODR, (bi * 4 + q + 1) * NMODR),
                        MROW.ap[0:NMODR, sl * 512 + q * 128:sl * 512 + (q + 1) * 128], IDENT) for q in range(4)]
                S.op("pe", fn, reads=[MROW.r(sl * 512, (sl + 1) * 512), PV.r()], writes=[PR(tbank)])
            src = PS(tbank, 0, n * NMODR).rearrange("p (m r) -> p m r", r=NMODR)
            S.op("dve", I("tensor_tensor", MOD3[:, m0:m1, :], src,
                          BA[:, m0:m1].unsqueeze(2).to_broadcast([128, n, NMODR]), ALU.add),
                 reads=[PR(tbank), PV.r()], writes=[modr(m0, m1)])

        def mod_affine(m0, mul, add, n=16):
            v = MOD3[:, m0:m0 + n, :]
            S.op("dve", I("tensor_scalar", v, v, mul, add, ALU.mult, ALU.add),
                 reads=[modr(m0, m0 + n)], writes=[modr(m0, m0 + n)])


        def modulate(c, msc, msh, eng="dve"):
            modulate_big(c, msc, msh, eng)
            modulate_small(c, msc, msh)

        def modulate_big(c, msc, msh, eng="dve"):
            if eng == "act":
                S.op("act", I("activation", U3[:, c, 0:NPL], X3[:, c, 0:NPL], AF.Identity, scale=MOD3[:, msc + c, 0:1],
                              bias=MOD3[:, msh + c, 0:1]),
                     reads=[xr(c, 0, NPL), modr(msc + c, msc + c + 1), modr(msh + c, msh + c + 1)], writes=[ur(c, 0, NPL)])
            else:
                S.op("dve", I("tensor_scalar", U3[:, c, 0:NPL], X3[:, c, 0:NPL], MOD3[:, msc + c, 0:1],
                              MOD3[:, msh + c, 0:1], ALU.mult, ALU.add),
                     reads=[xr(c, 0, NPL), modr(msc + c, msc + c + 1), modr(msh + c, msh + c + 1)], writes=[ur(c, 0, NPL)])

        def modulate_small(c, msc, msh):
            xs = X3[:, c, SOFF:NT].rearrange("p (s t) -> p s t", t=4)
            us = U3[:, c, SOFF:NT].rearrange("p (s t) -> p s t", t=4)
            tm = TMPS.ap.rearrange("p (s t) -> p s t", t=4)
            S.op("dve", I("tensor_tensor", tm, xs, MOD3[:, msc + c, 1:17].unsqueeze(2).to_broadcast([128, 16, 4]), ALU.mult),
                 reads=[xr(c, SOFF, NT), modr(msc + c, msc + c + 1)], writes=[TMPS.r()])
            S.op("dve", I("tensor_tensor", us, tm, MOD3[:, msh + c, 1:17].unsqueeze(2).to_broadcast([128, 16, 4]), ALU.add),
                 reads=[TMPS.r(), modr(msh + c, msh + c + 1)], writes=[ur(c, SOFF, NT)])

        def resid_acc(bank, c, ti, mg):
            t0, tn = TILES[ti]
            mr = modr(mg + c, mg + c + 1)
            if ti < 2:
                S.op("dve", I("scalar_tensor_tensor", X3[:, c, t0:t0 + tn], PS(bank, 0, tn), MOD3[:, mg + c, 0:1],
                                                             X3[:, c, t0:t0 + tn], ALU.mult, ALU.add),
                     reads=[PR(bank, 0, tn), mr, xr(c, t0, t0 + tn)], writes=[xr(c, t0, t0 + tn)])
            else:
                S.op("dve", I("scalar_tensor_tensor", X3[:, c, 1024:NPL], PS(bank, 0, 4), MOD3[:, mg + c, 0:1],
                                                             X3[:, c, 1024:NPL], ALU.mult, ALU.add),
                     reads=[PR(bank, 0, 4), mr, xr(c, 1024, NPL)], writes=[xr(c, 1024, NPL)])
                tm = TMPS.ap.rearrange("p (s t) -> p s t", t=4)
                pss = PS(bank, 4, 68).rearrange("p (s t) -> p s t", t=4)
                xs = X3[:, c, SOFF:NT].rearrange("p (s t) -> p s t", t=4)
                S.op("dve", I("tensor_tensor", tm, pss, MOD3[:, mg + c, 1:17].unsqueeze(2).to_broadcast([128, 16, 4]), ALU.mult),
                     reads=[PR(bank, 4, 68), mr], writes=[TMPS.r()])
                S.op("dve", I("tensor_tensor", xs, xs, tm, ALU.add),
                     reads=[TMPS.r(), xr(c, SOFF, NT)], writes=[xr(c, SOFF, NT)])

        def ffn(fi, mg, hook, first=None):
            AR.pos = scr_mark
            H = AR.alloc(2 * GRP * NT, BF16)
            SG = AR.alloc(3 * 512, F32)
            H3 = H.v3(NT)
            ngr = NJ // GRP
            wds = {}
            slot = [0]

            def gu_chunk(g, jj):
                j = g * GRP + jj
                if j == 0 and first is not None:
                    w, wrng = first
                else:
                    w, wrng = ring_load(wgu[fi][j], 4096)
                hidx = (g % 2) * GRP + jj
                for ti, (t0, tn) in enumerate(TILES):
                    s = slot[0] % 2
                    slot[0] += 1
                    ba, bb = 2 * s, 2 * s + 1
                    ureads = [ur(k, t0, t0 + tn) for k in range(16)]
                    mm_group(PS(ba, 0, tn), [(w[:, k * 256:k * 256 + 128], U3[:, k, t0:t0 + tn]) for k in range(16)],
                             [wrng] + ureads, PR(ba, 0, tn))
                    mm_group(PS(bb, 0, tn), [(w[:, k * 256 + 128:k * 256 + 256], U3[:, k, t0:t0 + tn]) for k in range(16)],
                             [wrng] + ureads, PR(bb, 0, tn))
                    sg = SG.ap[:, s * 512:s * 512 + tn]
                    sgr = SG.r(s * 512, s * 512 + tn)
                    S.op("act", I("activation", sg, PS(ba, 0, tn), AF.Silu), reads=[PR(ba, 0, tn)], writes=[sgr])
                    hr = H.r(hidx * NT + t0, hidx * NT + t0 + tn)
                    S.op("dve", I("tensor_tensor", H3[:, hidx, t0:t0 + tn], PS(bb, 0, tn), sg, ALU.mult),
                         reads=[PR(bb, 0, tn), sgr], writes=[hr])

            def down(g):
                dslot = 0
                for c in range(16):
                    for ti, (t0, tn) in enumerate(TILES):
                        bank = 4 + (dslot % 3)
                        dslot += 1
                        pairs, reads = [], []
                        for jj in range(GRP):
                            w, wrng = wds[(g, jj)]
                            hidx = (g % 2) * GRP + jj
                            pairs.append((w[:, c * 128:(c + 1) * 128], H3[:, hidx, t0:t0 + tn]))
                            reads += [wrng, H.r(hidx * NT + t0, hidx * NT + t0 + tn)]
                        mm_group(PS(bank, 0, tn), pairs, reads, PR(bank, 0, tn))
                        resid_acc(bank, c, ti, mg)

            def load_wd(g):
                for jj in range(GRP):
                    wds[(g, jj)] = ring_load(wdn[fi][g * GRP + jj], 2048)

            for g in range(ngr):
                for jj in range(GRP):
                    gu_chunk(g, jj)
                    if hook is not None:
                        hook(g, jj)
                if g > 0:
                    load_wd(g - 1)
                    down(g - 1)
            load_wd(ngr - 1)
            down(ngr - 1)

        def layernorm(l, msc, msh, last, hook=None):
            AR.pos = scr_mark
            MEAN = AR.alloc(NT, F32)
            RSTD = AR.alloc(NT, F32)
            SQ = AR.alloc(8 * NT, BF16)
            SQ3 = SQ.v3(NT)
            for c in range(16):
                S.op("dve", I("tensor_copy", U3[:, c, :], X3[:, c, :]), reads=[xr(c)], writes=[ur(c)])
                sl = c % 8
                if c % 4 != 3:
                    S.op("act", I("activation", SQ3[:, sl, :], X3[:, c, :], AF.Square), reads=[xr(c)],
                         writes=[SQ.r(sl * NT, (sl + 1) * NT)])
                else:
                    S.op("dve", I("tensor_tensor", SQ3[:, sl, :], X3[:, c, :], X3[:, c, :], ALU.mult), reads=[xr(c)],
                         writes=[SQ.r(sl * NT, (sl + 1) * NT)])
                for ti, (t0, tn) in enumerate(TILES):
                    fn = [I("matmul", PS(ti, 0, tn), ONES.ap, U3[:, c, t0:t0 + tn], start=(c == 0), stop=(c == 15)),
                          I("matmul", PS(3 + ti, 0, tn), ONES.ap, SQ3[:, sl, t0:t0 + tn], start=(c == 0), stop=(c == 15))]
                    S.op("pe", fn, reads=[ONES.r(), ur(c, t0, t0 + tn), SQ.r(sl * NT + t0, sl * NT + t0 + tn)],
                         writes=[PR(ti, 0, tn), PR(3 + ti, 0, tn)])
            for ti, (t0, tn) in enumerate(TILES):
                mn, rs = MEAN.ap[:, t0:t0 + tn], RSTD.ap[:, t0:t0 + tn]
                mr_, rr_ = MEAN.r(t0, t0 + tn), RSTD.r(t0, t0 + tn)
                S.op("act", I("activation", mn, PS(ti, 0, tn), AF.Copy), reads=[PR(ti, 0, tn)], writes=[mr_])
                S.op("dve", I("tensor_tensor", rs, mn, mn, ALU.mult), reads=[mr_], writes=[rr_])
                S.op("dve", I("tensor_tensor", rs, PS(3 + ti, 0, tn), rs, ALU.subtract),
                     reads=[PR(3 + ti, 0, tn), rr_], writes=[rr_])
                S.op("act", I("activation", rs, rs, AF.Ln, bias=CONSTS.ap[:, 0:1]), reads=[rr_, CONSTS.r()], writes=[rr_])
                S.op("act", I("activation", rs, rs, AF.Exp, scale=-0.5), reads=[rr_], writes=[rr_])
            outs = []
            pending = []
            for c in range(16):
                xc = X3[:, c, :]
                seng = "pool" if (last and c % 2 == 1) else "dve"
                S.op(seng, I("tensor_tensor", xc, xc, MEAN.ap, ALU.subtract), reads=[xr(c), MEAN.r()], writes=[xr(c)])
                S.op("dve", I("tensor_tensor", xc, xc, RSTD.ap, ALU.mult), reads=[xr(c), RSTD.r()], writes=[xr(c)])
                aeng = "act" if (last or c % 4 != 0) else "dve"
                if aeng == "act":
                    S.op("act", I("activation", xc, xc, AF.Identity, scale=pv[:, PV_LNG + l * 16 + c:PV_LNG + l * 16 + c + 1],
                                  bias=pv[:, PV_LNB + l * 16 + c:PV_LNB + l * 16 + c + 1]),
                         reads=[xr(c), PV.r()], writes=[xr(c)])
                else:
                    S.op("dve", I("tensor_scalar", xc, xc, pv[:, PV_LNG + l * 16 + c:PV_LNG + l * 16 + c + 1],
                                  pv[:, PV_LNB + l * 16 + c:PV_LNB + l * 16 + c + 1], ALU.mult, ALU.add),
                         reads=[xr(c), PV.r()], writes=[xr(c)])
                if last:
                    outs.append(S.op("sp", I("dma_start", out=yT[c], in_=xc), reads=[xr(c)], writes=[("d_y", c * 32, c * 32 + 32)], dma=True))
                else:
                    modulate_big(c, msc, msh, aeng)
                    if aeng == "act":
                        pending.append(c)
                    else:
                        modulate_small(c, msc, msh)
                        while pending:
                            modulate_small(pending.pop(0), msc, msh)
                if hook is not None and c % 4 == 3:
                    hook(c // 4)
            while pending:
                modulate_small(pending.pop(0), msc, msh)
            return outs

        AR.pos = scr_mark
        STG = AR.alloc(8192, BF16)
        AR.alloc(4096, BF16)
        FG = AR.alloc(4096, BF16)
        S.op("pool", I("dma_start", out=FG.ap, in_=wgu[0][0]), writes=[FG.r()], dma=True)
        first_gu = (FG.ap, FG.r())
        for i in range(4):
            ada_blocks(i, i + 1, 0, 1)
            ada_blocks(4 + i, 5 + i, 2, 3, stage=(STG, 0))
            mod_affine(16 + 4 * i, 1.0, 1.0, 4)
            for c in range(4 * i, 4 * i + 4):
                modulate(c, 16, 0)

        def ada_hook(g, jj):
            if g == 0:
                ada_blocks(8 + jj, 9 + jj, 6, 7)
                if jj == GRP - 1:
                    mod_affine(32, 0.5 / ALPHA, 0.0)
                return
            if g <= 4 and jj in (0, 2):
                nb = 12 + 2 * (g - 1) + jj // 2
                ada_blocks(nb, nb + 1, 6, 7)

        def ln1_hook(i):
            ada_blocks(20 + i, 21 + i, 6, 7)
            if i == 3:
                mod_affine(80, 1.0 / ALPHA, 0.0)

        def ln2_hook(i):
            ada_blocks(32 + i, 33 + i, 6, 7)
            if i == 3:
                mod_affine(128, 0.5 / ALPHA, 0.0)

        ffn(0, 32, ada_hook, first_gu)
        mod_affine(64, 1.0, 1.0)
        layernorm(0, 64, 48, False, ln1_hook)

        for c in range(16):
            S.op("sp", (lambda c: I("dma_start", out=xspill[c], in_=X3[:, c, :]))(c), reads=[xr(c)],
                 writes=[("d_spill", c * 32, c * 32 + 32)], dma=True)

        XAR = Arena(arena_t, XA_LO + XA_WORDS)
        XAR.pos = XA_LO
        AR.pos = scr_mark
        YA = XAR.alloc(8 * NT, BF16)
        PG = XAR.alloc(8 * NT, BF16)
        vn_mark = XAR.pos
        A_ = XAR.alloc(NT, F32)
        GR = XAR.alloc(NT, F32)
        G_ = XAR.alloc(NT, F32)
        XC = XAR.alloc(NT, F32)
        R_ = XAR.alloc(NT, F32)
        I_ = XAR.alloc(NT, F32)
        T1 = XAR.alloc(NT, F32)
        XP = AR.alloc(1027, F32)
        XS = AR.alloc(16 * 7, F32)
        XCb = AR.alloc(NT, BF16)
        B_ = AR.alloc(NT, F32)
        YL = AR.alloc(1024, F32)
        P_ = AR.alloc(1024, F32)
        ZERO = AR.alloc(1024, F32)
        YLS = AR.alloc(64, F32)
        HS = AR.alloc(16, F32)
        YA3 = YA.v3(NT)
        PG3 = PG.ap[:, 0:8 * 1024].rearrange("p (c n) -> p c n", n=1024)
        XS3 = XS.ap.rearrange("p (s k) -> p s k", k=7)
        osm = OSM.ap
        S.op("dve", I("memset", ZERO.ap, 0.0), writes=[ZERO.r()])
        S.op("dve", I("memset", XC.ap[:, 1024:NPL], 0.0), writes=[XC.r(1024, NPL)])
        S.op("dve", I("memset", YA3[:, :, 1024:NPL], 0.0), writes=[YA.r()])
        BDA = PM.ap[:, PM_BDA:PM_BDA + 1024]
        BDX = PM.ap[:, PM_BDX:PM_BDX + 1024]

        def s4(ap2d):
            return ap2d.rearrange("p (s t) -> p s t", t=4)

        XC2 = AR.alloc(NT, F32)
        R2 = AR.alloc(NT, F32)
        I2 = AR.alloc(NT, F32)
        S.op("dve", I("memset", XC2.ap[:, 1024:NPL], 0.0), writes=[XC2.r(1024, NPL)])
        XCs, Gs, Rs, Is = [XC, XC2], [G_, GR], [R_, R2], [I_, I2]

        def ma_front(j):
            sl = j % 2
            wx, wxr = ring_load(win[j], 2048)
            wg, wgr = ring_load(win[8 + j], 2048)
            for ti, (t0, tn) in enumerate(TILES):
                mm_group(PS(ti, 0, tn), [(wx[:, k * 128:(k + 1) * 128], U3[:, k, t0:t0 + tn]) for k in range(16)],
                         [wxr] + [ur(k, t0, t0 + tn) for k in range(16)], PR(ti, 0, tn))
            for ti, (t0, tn) in enumerate(TILES):
                mm_group(PS(3 + ti, 0, tn), [(wg[:, k * 128:(k + 1) * 128], U3[:, k, t0:t0 + tn]) for k in range(16)],
                         [wgr] + [ur(k, t0, t0 + tn) for k in range(16)], PR(3 + ti, 0, tn))
            S.op("act", I("activation", XP.ap[:, 3:515], PS(0, 0, 512), AF.Copy), reads=[PR(0)], writes=[XP.r(3, 515)])
            S.op("act", I("activation", XP.ap[:, 515:1027], PS(1, 0, 512), AF.Copy), reads=[PR(1)], writes=[XP.r(515, 1027)])
            S.op("act", I("activation", XP.ap[:, 0:3], PS(2, 1, 4), AF.Identity, scale=pv[:, PV_FLG + 1:PV_FLG + 2]),
                 reads=[PR(2, 0, 4), PV.r()], writes=[XP.r(0, 3)])
            S.op("act", I("activation", XS3[:, :, 3:7], s4(PS(2, 4, 68)), AF.Copy), reads=[PR(2, 4, 68)], writes=[XS.r()])
            for ti, (t0, tn) in enumerate(TILES):
                S.op("act", (lambda ti, t0, tn: I("activation", Gs[sl].ap[:, t0:t0 + tn], PS(3 + ti, 0, tn), AF.Gelu_apprx_tanh))(ti, t0, tn),
                     reads=[PR(3 + ti, 0, tn)], writes=[Gs[sl].r(t0, t0 + tn)])
            scv = pv[:, PV_SCONV + j * 48:PV_SCONV + (j + 1) * 48].rearrange("p (s k) -> p s k", k=3)
            S.op("dve", I("tensor_copy", XS3[:, :, 0:3], scv), reads=[PV.r()], writes=[XS.r()])

            def cwk(k):
                return pv[:, PV_CW + j * 4 + k:PV_CW + j * 4 + k + 1]
            cbj = pv[:, PV_CB + j:PV_CB + j + 1]
            xcp = XCs[sl].ap[:, 0:1024]
            xcs = s4(XCs[sl].ap[:, SOFF:NT])
            S.op("dve", I("tensor_scalar", xcp, XP.ap[:, 3:1027], cwk(3), cbj, ALU.mult, ALU.add),
                 reads=[XP.r(), PV.r()], writes=[XCs[sl].r(0, 1024)])
            S.op("dve", I("tensor_scalar", xcs, XS3[:, :, 3:7], cwk(3), cbj, ALU.mult, ALU.add),
                 reads=[XS.r(), PV.r()], writes=[XCs[sl].r(SOFF, NT)])
            for k in (2, 1, 0):
                S.op("dve", (lambda k: I("scalar_tensor_tensor", xcp, XP.ap[:, k:k + 1024], cwk(k), xcp, ALU.mult, ALU.add))(k),
                     reads=[XP.r(), PV.r(), XCs[sl].r(0, 1024)], writes=[XCs[sl].r(0, 1024)])
                S.op("dve", (lambda k: I("scalar_tensor_tensor", xcs, XS3[:, :, k:k + 4], cwk(k), xcs, ALU.mult, ALU.add))(k),
                     reads=[XS.r(), PV.r(), XCs[sl].r(SOFF, NT)], writes=[XCs[sl].r(SOFF, NT)])
            S.op("dve", I("tensor_copy", osm[:, OS_NCP + j * 3:OS_NCP + j * 3 + 3], XP.ap[:, 1024:1027]),
                 reads=[XP.r()], writes=[OSM.r(OS_NCP + j * 3, OS_NCP + j * 3 + 3)])
            ncsv = osm[:, OS_NCS + j * 48:OS_NCS + (j + 1) * 48].rearrange("p (s k) -> p s k", k=3)
            S.op("dve", I("tensor_copy", ncsv, XS3[:, :, 4:7]), reads=[XS.r()],
                 writes=[OSM.r(OS_NCS + j * 48, OS_NCS + (j + 1) * 48)])
            S.op("act", I("activation", XCb.ap, XCs[sl].ap, AF.Copy), reads=[XCs[sl].r()], writes=[XCb.r()])
            for ti, (t0, tn) in enumerate(TILES):
                mm_group(PS(6, 0, tn), [(BDA[:, j * 128:(j + 1) * 128], XCb.ap[:, t0:t0 + tn])], [PM.r(), XCb.r(t0, t0 + tn)], PR(6, 0, tn))
                mm_group(PS(7, 0, tn), [(BDX[:, j * 128:(j + 1) * 128], XCb.ap[:, t0:t0 + tn])], [PM.r(), XCb.r(t0, t0 + tn)], PR(7, 0, tn))
                S.op("act", (lambda t0, tn: I("activation", Rs[sl].ap[:, t0:t0 + tn], PS(6, 0, tn), AF.Sigmoid,
                                                                  bias=pv[:, PV_BA + j:PV_BA + j + 1]))(t0, tn),
                     reads=[PR(6, 0, tn), PV.r()], writes=[Rs[sl].r(t0, t0 + tn)])
                S.op("act", (lambda t0, tn: I("activation", Is[sl].ap[:, t0:t0 + tn], PS(7, 0, tn), AF.Sigmoid,
                                                                  bias=pv[:, PV_BX + j:PV_BX + j + 1]))(t0, tn),
                     reads=[PR(7, 0, tn), PV.r()], writes=[Is[sl].r(t0, t0 + tn)])

        def ma_tail_a(j):
            sl = j % 2
            S.op("act", I("activation", A_.ap, Rs[sl].ap, AF.Exp, scale=CL.ap[:, j:j + 1]), reads=[Rs[sl].r(), CL.r()], writes=[A_.r()])
            S.op("dve", I("tensor_tensor", T1.ap, A_.ap, A_.ap, ALU.mult), reads=[A_.r()], writes=[T1.r()])
            S.op("dve", I("tensor_scalar", T1.ap, T1.ap, 1.0, None, ALU.min), reads=[T1.r()], writes=[T1.r()])
            S.op("act", I("activation", T1.ap, T1.ap, AF.Sqrt, bias=CONSTS.ap[:, 2:3], scale=-1.0),
                 reads=[T1.r(), CONSTS.r()], writes=[T1.r()])
            S.op("dve", I("tensor_tensor_scan", P_.ap, A_.ap[:, 0:1024], ZERO.ap, 1.0, ALU.mult, ALU.add),
                 reads=[A_.r(0, 1024), ZERO.r()], writes=[P_.r()])

        def ma_tail_c(j):
            sl = j % 2
            S.op("dve", I("tensor_tensor", T1.ap, T1.ap, Is[sl].ap, ALU.mult), reads=[T1.r(), Is[sl].r()], writes=[T1.r()])
            S.op("dve", I("tensor_scalar", TMPS.ap[:, 0:1], T1.ap[:, 0:1], pv[:, PV_FLG + 2:PV_FLG + 3], None, ALU.mult),
                 reads=[T1.r(0, 1), PV.r()], writes=[TMPS.r(0, 1)])
            S.op("dve", I("scalar_tensor_tensor", T1.ap[:, 0:1], Is[sl].ap[:, 0:1], pv[:, PV_FLG:PV_FLG + 1], TMPS.ap[:, 0:1],
                                                         ALU.mult, ALU.add),
                 reads=[Is[sl].r(0, 1), PV.r(), TMPS.r(0, 1)], writes=[T1.r(0, 1)])
            S.op("dve", I("tensor_tensor", B_.ap, T1.ap, XCs[sl].ap, ALU.mult), reads=[T1.r(), XCs[sl].r()], writes=[B_.r()])
            S.op("dve", I("tensor_tensor_scan", YL.ap, A_.ap[:, 0:1024], B_.ap[:, 0:1024], 0.0, ALU.mult, ALU.add),
                 reads=[A_.r(0, 1024), B_.r(0, 1024)], writes=[YL.r()])
            S.op("dve", I("tensor_copy", HLE.ap[:, j:j + 1], YL.ap[:, 1023:1024]), reads=[YL.r()], writes=[HLE.r(j, j + 1)])
            S.op("dve", I("tensor_copy", PEND.ap[:, j:j + 1], P_.ap[:, 1023:1024]), reads=[P_.r()], writes=[PEND.r(j, j + 1)])
            a_s, b_s, y_s = s4(A_.ap[:, SOFF:NT]), s4(B_.ap[:, SOFF:NT]), s4(YLS.ap)
            for t in range(4):
                prev = pv[:, PV_SH0 + j * 16:PV_SH0 + (j + 1) * 16] if t == 0 else y_s[:, :, t - 1]
                prd = [PV.r()] if t == 0 else [YLS.r()]
                S.op("dve", (lambda t, prev: I("tensor_tensor", HS.ap, a_s[:, :, t], prev, ALU.mult))(t, prev),
                     reads=[A_.r(SOFF, NT)] + prd, writes=[HS.r()])
                S.op("dve", (lambda t: I("tensor_tensor", y_s[:, :, t], HS.ap, b_s[:, :, t], ALU.add))(t),
                     reads=[HS.r(), B_.r(SOFF, NT)], writes=[YLS.r()])
            S.op("dve", I("tensor_copy", osm[:, OS_NHS + j * 16:OS_NHS + (j + 1) * 16], y_s[:, :, 3]),
                 reads=[YLS.r()], writes=[OSM.r(OS_NHS + j * 16, OS_NHS + (j + 1) * 16)])
            S.op("dve", I("tensor_tensor", YA3[:, j, 0:1024], YL.ap, Gs[sl].ap[:, 0:1024], ALU.mult),
                 reads=[YL.r(), Gs[sl].r(0, 1024)], writes=[YA.r(j * NT, j * NT + 1024)])
            S.op("dve", I("tensor_tensor", PG3[:, j, :], P_.ap, Gs[sl].ap[:, 0:1024], ALU.mult),
                 reads=[P_.r(), Gs[sl].r(0, 1024)], writes=[PG.r(j * 1024, (j + 1) * 1024)])
            S.op("dve", I("tensor_tensor", YA3[:, j, SOFF:NT], YLS.ap, Gs[sl].ap[:, SOFF:NT], ALU.mult),
                 reads=[YLS.r(), Gs[sl].r(SOFF, NT)], writes=[YA.r(j * NT + SOFF, (j + 1) * NT)])


        ma_front(0)
        for j in range(8):
            ma_tail_a(j)
            if j < 7:
                ma_front(j + 1)
            ma_tail_c(j)

        S.op("sp", I("dma_start", out=cin.ap(), in_=HLE.ap), reads=[HLE.r()], writes=[("d_cin", 0, 32)], dma=True)
        S.op("pool", I("collective_compute", "AllGather", ALU.bypass, replica_groups=[[0, 1], [2, 3], [4, 5], [6, 7]],
                                                    ins=[cin.ap().opt()], outs=[cout.ap().opt()]),
             reads=[("d_cin", 0, 32)], writes=[("d_cout", 0, 32)], dma=True, inc=1)
        S.op("sp", I("dma_start", out=GATH.ap.rearrange("p (r j) -> p r j", j=8),
                                         in_=cout.ap().rearrange("(r p) j -> p r j", p=128)),
             reads=[("d_cout", 0, 32)], writes=[GATH.r()], dma=True)

        VAR = Arena(arena_t, XA_LO + XA_WORDS)
        VAR.pos = vn_mark
        VNB = VAR.alloc(9 * 1024, BF16)
        SGAB = VAR.alloc(4 * 512, F32)
        VNB3 = VNB.v3(1024)
        AR.pos = scr_mark
        PBC = AR.alloc(NPB, F32)
        VNF = AR.alloc(1024, F32)
        VNS = AR.alloc(1024, F32)
        GU = AR.alloc(NT, F32)
        ST1 = AR.alloc(512, F32)
        BST = AR.alloc(12, F32)
        MV = AR.alloc(2, F32)
        RS1 = AR.alloc(1, F32)
        S.op("sp", I("dma_start", out=PBC.ap, in_=pbc), writes=[PBC.r()], dma=True)
        wv, wvr = ring_load(wgv, 16384)
        nvs_op = None
        for tt in range(9):
            ntok = 128 if tt < 8 else 64
            c0 = tt * 128 if tt < 8 else SOFF
            bA, bB = (0, 1) if tt % 2 == 0 else (2, 3)
            for hb, bank in ((0, bA), (1, bB)):
                mm_group(PS(bank, 0, 512, 0, ntok),
                         [(U3[:, k, c0:c0 + ntok], wv[:, k * 1024 + hb * 512:k * 1024 + (hb + 1) * 512]) for k in range(16)],
                         [wvr] + [ur(k, c0, c0 + ntok) for k in range(16)], PR(bank))
            S.op("dve", I("bn_stats", BST.ap[0:ntok, 0:6], PS(bA, 0, 512, 0, ntok)), reads=[PR(bA)], writes=[BST.r(0, 6)])
            S.op("dve", I("bn_stats", BST.ap[0:ntok, 6:12], PS(bB, 0, 512, 0, ntok)), reads=[PR(bB)], writes=[BST.r(6, 12)])
            S.op("dve", I("bn_aggr", MV.ap[0:ntok, :], BST.ap[0:ntok, :]), reads=[BST.r()], writes=[MV.r()])
            S.op("act", I("activation", RS1.ap[0:ntok, :], MV.ap[0:ntok, 1:2], AF.Sqrt, bias=CONSTS.ap[0:ntok, 1:2]),
                 reads=[MV.r(), CONSTS.r()], writes=[RS1.r()])
            S.op("dve", I("reciprocal", RS1.ap[0:ntok, :], RS1.ap[0:ntok, :]), reads=[RS1.r()], writes=[RS1.r()])
            for hb, bank in ((0, bA), (1, bB)):
                S.op("dve", (lambda hb, bank: I("tensor_scalar", VNF.ap[0:ntok, hb * 512:(hb + 1) * 512], PS(bank, 0, 512, 0, ntok),
                                                                        MV.ap[0:ntok, 0:1], RS1.ap[0:ntok, 0:1], ALU.subtract, ALU.mult))(hb, bank),
                     reads=[PR(bank), MV.r(), RS1.r()], writes=[VNF.r(hb * 512, (hb + 1) * 512)])
            S.op("dve", I("tensor_tensor", VNF.ap[0:ntok, :], VNF.ap[0:ntok, :], PBC.ap[0:ntok, PB_G:PB_G + 1024], ALU.mult),
                 reads=[VNF.r(), PBC.r()], writes=[VNF.r()])
            S.op("dve", I("tensor_tensor", VNB3[0:ntok, tt, :], VNF.ap[0:ntok, :], PBC.ap[0:ntok, PB_B:PB_B + 1024], ALU.add),
                 reads=[VNF.r(), PBC.r()], writes=[VNB.r(tt * 1024, (tt + 1) * 1024)])
            if tt == 8:
                S.op("dve", I("tensor_tensor", VNS.ap[0:64, :], VNF.ap[0:64, :], PBC.ap[0:64, PB_B:PB_B + 1024], ALU.add),
                     reads=[VNF.r(), PBC.r()], writes=[VNS.r()])
                nvs_op = S.op("sp", I("dma_start", out=nvs, in_=VNS.ap[0:64, :]), reads=[VNS.r()], writes=[("d_nvs", 0, 32)], dma=True)

        GA3 = GATH.ap.rearrange("p (r j) -> p r j", j=8)
        for r in range(2):
            ohr = pv[:, PV_OH + r:PV_OH + r + 1]
            if r == 0:
                S.op("dve", I("tensor_scalar", HIN.ap, GA3[:, 0, :], ohr, None, ALU.mult), reads=[GATH.r(), PV.r()], writes=[HIN.r()])
            else:
                S.op("dve", (lambda r, ohr: I("scalar_tensor_tensor", HIN.ap, GA3[:, r, :], ohr, HIN.ap, ALU.mult, ALU.add))(r, ohr),
                     reads=[GATH.r(), PV.r(), HIN.r()], writes=[HIN.r()])
        for j in range(8):
            S.op("dve", (lambda j: I("scalar_tensor_tensor", YA3[:, j, 0:1024], PG3[:, j, :], HIN.ap[:, j:j + 1], YA3[:, j, 0:1024],
                                                                    ALU.mult, ALU.add))(j),
                 reads=[PG.r(j * 1024, (j + 1) * 1024), HIN.r(), YA.r(j * NT, j * NT + 1024)], writes=[YA.r(j * NT, j * NT + 1024)])
        S.op("dve", I("tensor_tensor", osm[:, OS_NHP:OS_NHP + 8], PEND.ap, HIN.ap, ALU.mult),
             reads=[PEND.r(), HIN.r()], writes=[OSM.r(OS_NHP, OS_NHP + 8)])
        S.op("dve", I("tensor_tensor", osm[:, OS_NHP:OS_NHP + 8], osm[:, OS_NHP:OS_NHP + 8], HLE.ap, ALU.add),
             reads=[OSM.r(OS_NHP, OS_NHP + 8), HLE.r()], writes=[OSM.r(OS_NHP, OS_NHP + 8)])
        osm_op = S.op("sp", I("dma_start", out=osmall, in_=osm), reads=[OSM.r()], writes=[("d_osm", 0, 32)], dma=True)

        YBAR = Arena(arena_t, XA_LO + XA_WORDS)
        YBAR.pos = PG.lo // 4
        YB = YBAR.alloc(8 * NT, BF16)
        assert YBAR.pos <= vn_mark
        YB3 = YB.v3(NT)
        S.op("dve", I("memset", YB3[:, :, 1024:NPL], 0.0), writes=[YB.r()])
        WSTm = PM.ap[:, PM_WST:PM_WST + 1024]
        WSSm = PM64.ap[0:64, PM64_WSS:PM64_WSS + 512]
        for g in range(8):
            wu, wur = ring_load(win[16 + g], 2048)
            for ti, (t0, tn) in enumerate(TILES):
                mm_group(PS(3 + ti, 0, tn), [(wu[:, k * 128:(k + 1) * 128], U3[:, k, t0:t0 + tn]) for k in range(16)],
                         [wur] + [ur(k, t0, t0 + tn) for k in range(16)], PR(3 + ti, 0, tn))
                S.op("act", (lambda ti, t0, tn: I("activation", GU.ap[:, t0:t0 + tn], PS(3 + ti, 0, tn), AF.Copy))(ti, t0, tn),
                     reads=[PR(3 + ti, 0, tn)], writes=[GU.r(t0, t0 + tn)])
            for hb in range(2):
                fn = [I("matmul", PS(hb, q * 128, (q + 1) * 128), VNB3[:, hb * 4 + q, g * 128:(g + 1) * 128],
                        WSTm[:, g * 128:(g + 1) * 128], start=True, stop=True) for q in range(4)]
                S.op("pe", fn, reads=[VNB.r(hb * 4096, (hb + 1) * 4096), PM.r()], writes=[PR(hb)])
            S.op("pe", (lambda g: I("matmul", PS(2, 4, 68), VNB3[0:64, 8, g * 128:(g + 1) * 128],
                                                     WSSm[:, g * 64:(g + 1) * 64], start=True, stop=True))(g),
                 reads=[VNB.r(8 * 1024, 9 * 1024), PM64.r()], writes=[PR(2, 4, 68)])
            bsp = PBC.ap[:, PB_BSP + g * 128:PB_BSP + (g + 1) * 128].unsqueeze(1).to_broadcast([128, 4, 128])
            for hb in range(2):
                st3 = ST1.ap.rearrange("p (q t) -> p q t", t=128)
                ps3 = PS(hb, 0, 512).rearrange("p (q t) -> p q t", t=128)
                S.op("dve", (lambda ps3, st3: I("tensor_tensor", st3, ps3, bsp, ALU.add))(ps3, st3),
                     reads=[PR(hb), PBC.r()], writes=[ST1.r()])
                S.op("dve", (lambda hb: I("tensor_tensor", YB3[:, g, hb * 512:(hb + 1) * 512], ST1.ap,
                                                                   GU.ap[:, hb * 512:(hb + 1) * 512], ALU.mult))(hb),
                     reads=[ST1.r(), GU.r(hb * 512, (hb + 1) * 512)], writes=[YB.r(g * NT + hb * 512, g * NT + (hb + 1) * 512)])
            S.op("dve", I("tensor_tensor", ST1.ap[:, 0:64], PS(2, 4, 68), PBC.ap[:, PB_BSS + g * 64:PB_BSS + (g + 1) * 64], ALU.add),
                 reads=[PR(2, 4, 68), PBC.r()], writes=[ST1.r(0, 64)])
            S.op("dve", I("tensor_tensor", YB3[:, g, SOFF:NT], ST1.ap[:, 0:64], GU.ap[:, SOFF:NT], ALU.mult),
                 reads=[ST1.r(0, 64), GU.r(SOFF, NT)], writes=[YB.r(g * NT + SOFF, (g + 1) * NT)])
            if g % 2 == 0:
                ada_blocks(24 + g // 2, 25 + g // 2, 6, 7)

        AR.pos = scr_mark
        M_ = AR.alloc(16 * NT, BF16)
        M3 = M_.v3(NT)
        YA3, YB3 = YA.v3(NT), YB.v3(NT)
        cnt = 0
        for c in range(16):
            wa_, war = ring_load(wpa[c], 1024)
            wb_, wbr = ring_load(wpb[c], 1024)
            wga, wgar = ring_load(win[32 + c], 2048)
            wgb, wgbr = ring_load(win[48 + c], 2048)
            for ti, (t0, tn) in enumerate(TILES):
                s = cnt % 2
                cnt += 1
                b0 = 4 * s
                mm_group(PS(b0, 0, tn), [(wa_[:, k * 128:(k + 1) * 128], YA3[:, k, t0:t0 + tn]) for k in range(8)],
                         [war] + [YA.r(k * NT + t0, k * NT + t0 + tn) for k in range(8)], PR(b0, 0, tn))
                mm_group(PS(b0 + 1, 0, tn), [(wb_[:, k * 128:(k + 1) * 128], YB3[:, k, t0:t0 + tn]) for k in range(8)],
                         [wbr] + [YB.r(k * NT + t0, k * NT + t0 + tn) for k in range(8)], PR(b0 + 1, 0, tn))
                mm_group(PS(b0 + 2, 0, tn), [(wga[:, k * 128:(k + 1) * 128], U3[:, k, t0:t0 + tn]) for k in range(16)],
                         [wgar] + [ur(k, t0, t0 + tn) for k in range(16)], PR(b0 + 2, 0, tn))
                mm_group(PS(b0 + 3, 0, tn), [(wgb[:, k * 128:(k + 1) * 128], U3[:, k, t0:t0 + tn]) for k in range(16)],
                         [wgbr] + [ur(k, t0, t0 + tn) for k in range(16)], PR(b0 + 3, 0, tn))
                sa = SGAB.ap[:, (2 * s) * 512:(2 * s) * 512 + tn]
                sb = SGAB.ap[:, (2 * s + 1) * 512:(2 * s + 1) * 512 + tn]
                sar = SGAB.r((2 * s) * 512, (2 * s) * 512 + tn)
                sbr = SGAB.r((2 * s + 1) * 512, (2 * s + 1) * 512 + tn)
                S.op("act", (lambda sa, b0, tn: I("activation", sa, PS(b0 + 2, 0, tn), AF.Sigmoid))(sa, b0, tn),
                     reads=[PR(b0 + 2, 0, tn)], writes=[sar])
                S.op("act", (lambda sb, b0, tn: I("activation", sb, PS(b0 + 3, 0, tn), AF.Sigmoid))(sb, b0, tn),
                     reads=[PR(b0 + 3, 0, tn)], writes=[sbr])
                S.op("dve", (lambda sa, b0, tn: I("tensor_tensor", sa, sa, PS(b0, 0, tn), ALU.mult))(sa, b0, tn),
                     reads=[sar, PR(b0, 0, tn)], writes=[sar])
                S.op("dve", (lambda sb, b0, tn: I("tensor_tensor", sb, sb, PS(b0 + 1, 0, tn), ALU.mult))(sb, b0, tn),
                     reads=[sbr, PR(b0 + 1, 0, tn)], writes=[sbr])
                S.op("dve", (lambda sa, sb, c, t0, tn: I("tensor_tensor", M3[:, c, t0:t0 + tn], sa, sb, ALU.add))(sa, sb, c, t0, tn),
                     reads=[sar, sbr], writes=[M_.r(c * NT + t0, c * NT + t0 + tn)])

        for c in range(16):
            S.op("sp", (lambda c: I("dma_start", out=X3[:, c, :], in_=xspill[c]))(c), reads=[("d_spill", c * 32, c * 32 + 32)],
                 writes=[xr(c)], dma=True)
        cnt = 0
        for c in range(16):
            wo, wor = ring_load(wout[c], 2048)
            for ti, (t0, tn) in enumerate(TILES):
                bank = cnt % 4
                cnt += 1
                mm_group(PS(bank, 0, tn), [(wo[:, k * 128:(k + 1) * 128], M3[:, k, t0:t0 + tn]) for k in range(16)],
                         [wor] + [M_.r(k * NT + t0, k * NT + t0 + tn) for k in range(16)], PR(bank, 0, tn))
                resid_acc(bank, c, ti, 80)
            if c % 4 == 0:
                ada_blocks(28 + c // 4, 29 + c // 4, 6, 7)
        mod_affine(112, 1.0, 1.0)
        layernorm(1, 112, 96, False, ln2_hook)
        ffn(1, 128, None)
        outs = layernorm(2, 0, 0, True)
        S.final_wait("sp", outs + [osm_op, nvs_op])

        with nc.Block() as block:
            @block.tensor
            def _(e):
                S.emit_all("pe", e)

            @block.scalar
            def _(e):
                S.emit_all("act", e)

            @block.vector
            def _(e):
                S.emit_all("dve", e)

            @block.gpsimd
            def _(e):
                S.emit_all("pool", e)

            @block.sync
            def _(e):
                S.emit_all("sp", e)
    return nc


_NC_CACHE = {}


def _blk(w, kchunks):
    K, N = w.shape
    return np.ascontiguousarray(w.reshape(kchunks, 128, N // 128, 128).transpose(2, 1, 0, 3).reshape(N // 128, 128, kchunks * 128))


def _fm(v):
    return np.ascontiguousarray(v.reshape(-1, 128).T)


def kernel(x_prompt, x_sample, state_conv, state_h, c_prompt, c_sample,
           w_ada, b_ada, ffn1_w_gu, ffn1_w_down, ffn2_w_gu, ffn2_w_down,
           w_in, conv_w, conv_b, lru_wa, lru_ba, lru_wx, lru_bx, lru_lambda,
           gmlp_ln_g, gmlp_ln_b, gmlp_ws, gmlp_bs, w_pa, w_pb, w_out, ln_g, ln_b):
    f = np.float32
    A = lambda a: np.asarray(a, dtype=f)
    x_prompt, x_sample, state_conv, state_h = A(x_prompt), A(x_sample), A(state_conv), A(state_h)
    c_prompt, c_sample = A(c_prompt), A(c_sample)
    w_ada, b_ada, w_in = A(w_ada)[0], A(b_ada)[0], A(w_in)[0]
    gus = [A(ffn1_w_gu)[0], A(ffn2_w_gu)[0]]
    dns = [A(ffn1_w_down)[0], A(ffn2_w_down)[0]]
    conv_w, conv_b = A(conv_w)[0], A(conv_b)[0]
    lru_wa, lru_ba, lru_wx, lru_bx, lru_lambda = A(lru_wa)[0], A(lru_ba)[0], A(lru_wx)[0], A(lru_bx)[0], A(lru_lambda)[0]
    gmlp_ln_g, gmlp_ln_b, gmlp_ws, gmlp_bs = A(gmlp_ln_g)[0], A(gmlp_ln_b)[0], A(gmlp_ws)[0], A(gmlp_bs)[0]
    w_pa, w_pb, w_out, ln_g, ln_b = A(w_pa)[0], A(w_pb)[0], A(w_out)[0], A(ln_g)[0], A(ln_b)[0]

    shared = {}
    shared["wada"] = np.ascontiguousarray(w_ada.reshape(16, 128, 36, 512).transpose(2, 1, 0, 3).reshape(36, 128, 8192))
    for i in range(2):
        g = gus[i][:, :DFF].reshape(16, 128, NJ, 128)
        v = gus[i][:, DFF:].reshape(16, 128, NJ, 128)
        gv = np.stack([g, v], axis=3)
        shared["wgu%d" % (i + 1)] = np.ascontiguousarray(gv.transpose(2, 1, 0, 3, 4).reshape(NJ, 128, 4096))
        shared["wd%d" % (i + 1)] = np.ascontiguousarray(dns[i].reshape(NJ, 128, 2048))
    shared["win"] = _blk(w_in, 16)
    shared["wgv"] = np.ascontiguousarray(w_in[:, 3072:4096].reshape(16, 128, 1024).transpose(1, 0, 2).reshape(128, 16384))
    shared["wpa"] = _blk(w_pa, 8)
    shared["wpb"] = _blk(w_pb, 8)
    shared["wout"] = _blk(w_out, 16)
    pmat = np.zeros((128, NPM), f)
    for j in range(8):
        for h in range(2):
            pmat[h * 64:(h + 1) * 64, PM_BDA + j * 128 + h * 64:PM_BDA + j * 128 + (h + 1) * 64] = lru_wa[2 * j + h]
            pmat[h * 64:(h + 1) * 64, PM_BDX + j * 128 + h * 64:PM_BDX + j * 128 + (h + 1) * 64] = lru_wx[2 * j + h]
    pmat[:, PM_WST:PM_WST + 1024] = gmlp_ws.transpose(2, 0, 1).reshape(128, 1024)
    s_i, t_i = np.arange(128)[:, None], np.arange(128)[None, :]
    pmat[:, PM_MASK:PM_MASK + 128] = (s_i <= t_i).astype(f)
    shared["pmat"] = pmat
    pm64 = np.zeros((64, NPM64), f)
    w4t = gmlp_ws[:, :4, :4].transpose(2, 0, 1)
    pm64[:, PM64_WSS:PM64_WSS + 512] = np.tile(w4t[None, :, :, None, :], (16, 1, 1, 16, 1)).reshape(64, 8 * 64)
    q = np.arange(64)
    pm64[:, PM64_MASK:PM64_MASK + 64] = ((q[:, None] // 4 == q[None, :] // 4) & (q[:, None] % 4 <= q[None, :] % 4)).astype(f)
    shared["pmat64"] = pm64
    pbc = np.zeros((128, NPB), f)
    pbc[:, PB_G:PB_G + 1024] = gmlp_ln_g[None, :]
    pbc[:, PB_B:PB_B + 1024] = gmlp_ln_b[None, :]
    pbc[:, PB_BSP:PB_BSP + 1024] = gmlp_bs.reshape(1, 1024)
    pbc[:, PB_BSS:PB_BSS + 512] = np.tile(gmlp_bs[:, None, :4], (1, 16, 1)).reshape(1, 512)
    shared["pbc"] = pbc

    pv_base = np.zeros((128, NPV), f)
    pv_base[:, PV_BADA:PV_BADA + 144] = _fm(b_ada)
    pv_base[:, PV_LNG:PV_LNG + 48] = ln_g.reshape(3, 16, 128).transpose(2, 0, 1).reshape(128, 48)
    pv_base[:, PV_LNB:PV_LNB + 48] = ln_b.reshape(3, 16, 128).transpose(2, 0, 1).reshape(128, 48)
    pv_base[:, PV_CW:PV_CW + 32] = conv_w.reshape(4, 8, 128).transpose(2, 1, 0).reshape(128, 32)
    pv_base[:, PV_CB:PV_CB + 8] = _fm(conv_b)
    pv_base[:, PV_BA:PV_BA + 8] = _fm(lru_ba)
    pv_base[:, PV_BX:PV_BX + 8] = _fm(lru_bx)
    pv_base[:, PV_LAM:PV_LAM + 8] = _fm(lru_lambda)
    pv_base[0:NMODR, PV_ID:PV_ID + NMODR] = np.eye(NMODR, dtype=f)

    in_maps = []
    for r in range(NCORES):
        b, half = r // 2, r % 2
        xt = np.empty((NT, D), f)
        xt[0:1024] = x_prompt[b, half * 1024:(half + 1) * 1024]
        xt[1024:1028] = x_prompt[b, 1020:1024] if half == 1 else x_prompt[b, 0:4]
        xt[1028:1092] = x_sample[16 * r:16 * (r + 1)].reshape(64, D)
        m = dict(shared)
        m["xT"] = np.ascontiguousarray(xt.T.reshape(16, 128, NT))
        pvr = pv_base.copy()
        pvr[:, PV_FLG + 0] = 1.0 if half == 0 else 0.0
        pvr[:, PV_FLG + 1] = 1.0 if half == 1 else 0.0
        pvr[:, PV_FLG + 2] = 0.0 if half == 0 else 1.0
        if half == 1:
            pvr[:, PV_OH + 0] = 1.0
        sc = state_conv[0, 16 * r:16 * (r + 1)]
        pvr[:, PV_SCONV:PV_SCONV + 384] = sc.reshape(16, 3, 8, 128).transpose(3, 2, 0, 1).reshape(128, 384)
        sh = state_h[0, 16 * r:16 * (r + 1)]
        pvr[:, PV_SH0:PV_SH0 + 128] = sh.reshape(16, 8, 128).transpose(2, 1, 0).reshape(128, 128)
        crow = np.concatenate([c_prompt[b:b + 1], c_sample[16 * r:16 * (r + 1)]], axis=0)
        pvr[:, PV_CT:PV_CT + 16 * NMODR] = crow.reshape(NMODR, 16, 128).transpose(2, 1, 0).reshape(128, 16 * NMODR)
        m["pvec"] = pvr
        in_maps.append(m)

    if "nc" not in _NC_CACHE:
        _NC_CACHE["nc"] = build_program()
    nc = _NC_CACHE["nc"]
    res = run_bass_kernel_spmd(nc, in_maps, core_ids=list(range(NCORES)))
    R = res.results

    y_prompt = np.empty((4, 2048, D), f)
    y_sample = np.empty((128, 4, D), f)
    ncp = np.empty((1, 4, 3, 1024), f)
    nhp = np.empty((1, 4, 1024), f)
    ncs = np.empty((1, 128, 3, 1024), f)
    nhs = np.empty((1, 128, 1024), f)
    nv = np.empty((1, 128, 4, 1024), f)
    for r in range(NCORES):
        b, half = r // 2, r % 2
        yt = np.asarray(R[r]["yT"]).reshape(D, NT).T
        y_prompt[b, half * 1024:(half + 1) * 1024] = yt[0:1024]
        y_sample[16 * r:16 * (r + 1)] = yt[1028:1092].reshape(16, 4, D)
        osm = np.asarray(R[r]["osmall"])
        if half == 1:
            ncp[0, b] = osm[:, OS_NCP:OS_NCP + 24].reshape(128, 8, 3).transpose(2, 1, 0).reshape(3, 1024)
            nhp[0, b] = osm[:, OS_NHP:OS_NHP + 8].T.reshape(1024)
        ncs[0, 16 * r:16 * (r + 1)] = osm[:, OS_NCS:OS_NCS + 384].reshape(128, 8, 16, 3).transpose(2, 3, 1, 0).reshape(16, 3, 1024)
        nhs[0, 16 * r:16 * (r + 1)] = osm[:, OS_NHS:OS_NHS + 128].reshape(128, 8, 16).transpose(2, 1, 0).reshape(16, 1024)
        nv[0, 16 * r:16 * (r + 1)] = np.asarray(R[r]["nvs"]).reshape(16, 4, 1024)
    return (y_prompt, y_sample, ncp, nhp, ncs, nhs, nv)
```

```python
import contextlib
import numpy as np
import concourse.bass as bass
import concourse.mybir as mybir
from concourse.bass_utils import run_bass_kernel_spmd

F32 = mybir.dt.float32
BF16 = mybir.dt.bfloat16
ALU = mybir.AluOpType
AF = mybir.ActivationFunctionType

NCORES = 8
D = 2048
DFF = 5632
NJ = DFF // 128
NT = 1092
NPL = 1028
SOFF = 1028
TILES = [(0, 512), (512, 512), (1024, 68)]
ALPHA = 2.0 ** 0.25
EPSP = 1e-5 / (ALPHA * ALPHA)
NMODR = 17
GRP = 4
GRAN = 8

PV_BADA, PV_LNG, PV_LNB, PV_CW, PV_CB, PV_BA, PV_BX, PV_LAM = 0, 144, 192, 240, 272, 280, 288, 296
PV_FLG, PV_OH, PV_SCONV, PV_SH0, PV_CT = 304, 308, 316, 700, 828
PV_ID = 828 + 16 * NMODR
NPV = PV_ID + NMODR
PM_BDA, PM_BDX, PM_WST, PM_MASK = 0, 1024, 2048, 3072
NPM = 3200
PM64_WSS, PM64_MASK = 0, 512
NPM64 = 576
PB_G, PB_B, PB_BSP, PB_BSS = 0, 1024, 2048, 3072
NPB = 3584
OS_NCP, OS_NHP, OS_NCS, OS_NHS = 0, 24, 32, 416
NOS = 544


def I(name, *args, **kw):
    return (name, args, kw)


class Op:
    __slots__ = ("eng", "sem", "val", "dma")

    def __init__(self, eng, sem, val, dma):
        self.eng, self.sem, self.val, self.dma = eng, sem, val, dma


class Cell:
    __slots__ = ("w", "r")

    def __init__(self):
        self.w = {}
        self.r = {}


class Eng:
    def __init__(self, name, clock):
        self.name, self.clock, self.seq = name, clock, 0
        self.prog = []
        self.waited = {}


class Sched:
    def __init__(self, nc, stack):
        self.nc = nc
        self.engs = {}
        for n in ("pe", "act", "dve", "pool", "sp"):
            self.engs[n] = Eng(n, stack.enter_context(nc.semaphore("clk_" + n)))
        self.dsems = {}
        for q, cnt in (("pool", 40), ("sp", 24)):
            self.dsems[q] = [[stack.enter_context(nc.semaphore("d%s%d" % (q, i))), 0] for i in range(cnt)]
        self.drr = {"pool": 0, "sp": 0}
        self.ccsem = [stack.enter_context(nc.semaphore("ccsem")), 0]
        self.cells = {}
        self.nops = 0
        self.out_ops = []

    def _cells(self, rng):
        sp, lo, hi = rng
        cs = self.cells
        out = []
        for g in range(lo // GRAN, (hi + GRAN - 1) // GRAN):
            k = (sp, g)
            c = cs.get(k)
            if c is None:
                c = cs[k] = Cell()
            out.append(c)
        return out

    def op(self, en, fn, reads=(), writes=(), dma=False, inc=None):
        E = self.engs[en]
        raw, oth = {}, {}
        rcells = [c for r in reads for c in self._cells(r)]
        wcells = [c for r in writes for c in self._cells(r)]
        for c in rcells:
            for o in c.w.values():
                raw[id(o)] = o
        for c in wcells:
            for o in c.w.values():
                oth[id(o)] = o
            for o in c.r.values():
                oth[id(o)] = o
        waits = []

        def addwait(sem, val):
            k = id(sem)
            if E.waited.get(k, -1) >= val:
                return
            E.waited[k] = val
            waits.append((sem, val))

        for o in raw.values():
            if o.eng == en and not o.dma and not dma and en == "pe":
                continue
            addwait(o.sem, o.val)
        for o in oth.values():
            if id(o) in raw:
                continue
            if o.eng == en and not o.dma and not dma and en == "pe":
                continue
            addwait(o.sem, o.val)
        if dma:
            if inc is not None:
                ent = self.ccsem
            else:
                pool = self.dsems[en]
                i = self.drr[en]
                self.drr[en] = (i + 1) % len(pool)
                ent = pool[i]
            if ent[1] > 0:
                addwait(ent[0], ent[1])
            step = 16 if inc is None else inc
            ent[1] += step
            o = Op(en, ent[0], ent[1], True)
            key = ("d", self.nops)
        else:
            E.seq += 1
            step = 1
            o = Op(en, E.clock, E.seq, False)
            key = en
        self.nops += 1
        for c in rcells:
            c.r[key] = o
        for c in wcells:
            c.w = {key: o}
            c.r = {}
        E.prog.append((waits, fn, o.sem, step))
        return o

    def final_wait(self, en, ops):
        E = self.engs[en]
        waits = []
        for o in ops:
            waits.append((o.sem, o.val))
        E.prog.append((waits, None, None, None))

    def emit_all(self, en, e):
        for waits, fn, sem, step in self.engs[en].prog:
            for s, v in waits:
                e.wait_ge(s, v)
            if fn is None:
                continue
            lst = fn if isinstance(fn, list) else [fn]
            ins = None
            for name, args, kw in lst:
                ins = getattr(e, name)(*args, **kw)
            ins.then_inc(sem, step)


class T:
    def __init__(self, ap, lo, esz, n):
        self.ap, self.lo, self.esz, self.n = ap, lo, esz, n

    def r(self, a=0, b=None):
        if b is None:
            b = self.n
        return ("sb", self.lo + a * self.esz, self.lo + b * self.esz)

    def v3(self, inner):
        return self.ap.rearrange("p (c n) -> p c n", n=inner)


class Arena:
    def __init__(self, ap, nwords):
        self.A, self.nwords, self.pos = ap, nwords, 0

    def alloc(self, n, dt):
        esz = 4 if dt == F32 else 2
        words = (n * esz + 3) // 4
        words = (words + 7) // 8 * 8
        lo = self.pos
        assert lo + words <= self.nwords, "arena overflow %d + %d > %d" % (lo, words, self.nwords)
        self.pos += words
        v = self.A[:, lo:lo + words]
        if dt == BF16:
            v = v.bitcast(BF16)
        v = v[:, 0:n]
        return T(v, lo * 4, esz, n)


def build_program():
    nc = bass.Bass("TRN2", target_bir_lowering=False)

    def din(name, shape):
        return nc.dram_tensor(name, list(shape), F32, kind="ExternalInput").ap()

    def dout(name, shape):
        return nc.dram_tensor(name, list(shape), F32, kind="ExternalOutput").ap()

    xT = din("xT", [16, 128, NT])
    pvec = din("pvec", [128, NPV])
    pmat = din("pmat", [128, NPM])
    pmat64 = din("pmat64", [64, NPM64])
    pbc = din("pbc", [128, NPB])
    wada = din("wada", [36, 128, 8192])
    wgu = [din("wgu1", [NJ, 128, 4096]), din("wgu2", [NJ, 128, 4096])]
    wdn = [din("wd1", [NJ, 128, 2048]), din("wd2", [NJ, 128, 2048])]
    win = din("win", [64, 128, 2048])
    wgv = din("wgv", [128, 16384])
    wpa = din("wpa", [16, 128, 1024])
    wpb = din("wpb", [16, 128, 1024])
    wout = din("wout", [16, 128, 2048])
    yT = dout("yT", [16, 128, NT])
    osmall = dout("osmall", [128, NOS])
    nvs = dout("nvs", [64, 1024])
    xspill = nc.dram_tensor("xspill", [16, 128, NT], F32).ap()
    cin = nc.dram_tensor("cin", [128, 8], F32)
    cout = nc.dram_tensor("cout", [2 * 128, 8], F32)

    stack = contextlib.ExitStack()
    with stack:
        NW = 53184
        arena_t = stack.enter_context(nc.sbuf_tensor("arena", [128, NW], F32))
        ps_t = stack.enter_context(nc.psum_tensor("ps", [128, 8, 512], F32))
        S = Sched(nc, stack)
        AR = Arena(arena_t, NW)

        def PS(b, n0, n1, p0=0, p1=128):
            return ps_t[p0:p1, b, n0:n1]

        def PR(b, n0=0, n1=512):
            return ("ps", b * 2048, (b + 1) * 2048)

        X = AR.alloc(16 * NT, F32)
        U = AR.alloc(16 * NT, BF16)
        NRU = 10
        RING = AR.alloc(NRU * 2048, BF16)
        MOD = AR.alloc(144 * NMODR, F32)
        PV = AR.alloc(NPV, F32)
        PM = AR.alloc(NPM, BF16)
        PM64 = AR.alloc(NPM64, BF16)
        SC = AR.alloc(16 * NMODR, BF16)
        ONES = AR.alloc(128, BF16)
        CL = AR.alloc(8, F32)
        CONSTS = AR.alloc(8, F32)
        OSM = AR.alloc(NOS, F32)
        HIN = AR.alloc(8, F32)
        HLE = AR.alloc(8, F32)
        PEND = AR.alloc(8, F32)
        GATH = AR.alloc(16, F32)
        MROW = AR.alloc(2 * 512, F32)
        TMPS = AR.alloc(64, F32)
        scr_mark = AR.pos
        X3, U3, MOD3 = X.v3(NT), U.v3(NT), MOD.v3(NMODR)
        XA_LO, XA_WORDS = X.lo // 4, 16 * NT

        def xr(c, a=0, b=NT):
            return X.r(c * NT + a, c * NT + b)

        def ur(c, a=0, b=NT):
            return U.r(c * NT + a, c * NT + b)

        def modr(m0, m1):
            return MOD.r(m0 * NMODR, m1 * NMODR)

        ring_pos = [0]

        def ring_load(src, nelem):
            units = (nelem + 2047) // 2048
            if ring_pos[0] + units > NRU:
                ring_pos[0] = 0
            u0 = ring_pos[0]
            ring_pos[0] += units
            dst = RING.ap[:, u0 * 2048:u0 * 2048 + nelem]
            rng = RING.r(u0 * 2048, u0 * 2048 + nelem)
            S.op("pool", I("dma_start", out=dst, in_=src), writes=[rng], dma=True)
            return dst, rng

        def mm_group(out_ap, pairs, reads, wr):
            n = len(pairs)
            fn = [I("matmul", out_ap, l, r, start=(i == 0), stop=(i == n - 1)) for i, (l, r) in enumerate(pairs)]
            return S.op("pe", fn, reads=reads, writes=[wr])

        S.op("sp", I("dma_start", out=PV.ap, in_=pvec), writes=[PV.r()], dma=True)
        for c in range(16):
            S.op("sp", (lambda c: I("dma_start", out=X3[:, c, :], in_=xT[c]))(c), writes=[xr(c)], dma=True)
        S.op("pool", I("dma_start", out=PM.ap, in_=pmat), writes=[PM.r()], dma=True)
        S.op("pool", I("dma_start", out=PM64.ap[0:64, :], in_=pmat64), writes=[PM64.r()], dma=True)
        S.op("dve", I("memset", ONES.ap, 1.0 / D), writes=[ONES.r()])
        S.op("dve", I("memset", CONSTS.ap[:, 0:1], EPSP), writes=[CONSTS.r()])
        S.op("dve", I("memset", CONSTS.ap[:, 1:2], 1e-5), writes=[CONSTS.r()])
        S.op("dve", I("memset", CONSTS.ap[:, 2:3], 1.0), writes=[CONSTS.r()])
        pv = PV.ap
        S.op("act", I("activation", SC.ap, pv[:, PV_CT:PV_CT + 16 * NMODR], AF.Silu),
             reads=[PV.r()], writes=[SC.r()])
        WST3 = PM.ap[:, PM_WST:PM_WST + 1024].rearrange("p (g t) -> p g t", g=8)
        MK = PM.ap[:, PM_MASK:PM_MASK + 128]
        S.op("dve", I("tensor_tensor", WST3, WST3, MK.unsqueeze(1).to_broadcast([128, 8, 128]), ALU.mult),
             reads=[PM.r()], writes=[PM.r()])
        WSS3 = PM64.ap[0:64, PM64_WSS:PM64_WSS + 512].rearrange("p (g t) -> p g t", g=8)
        MK64 = PM64.ap[0:64, PM64_MASK:PM64_MASK + 64]
        S.op("dve", I("tensor_tensor", WSS3, WSS3, MK64.unsqueeze(1).to_broadcast([64, 8, 64]), ALU.mult),
             reads=[PM64.r()], writes=[PM64.r()])
        S.op("act", I("activation", CL.ap, pv[:, PV_LAM:PV_LAM + 8], AF.Exp, scale=-1.0),
             reads=[PV.r()], writes=[CL.r()])
        S.op("act", I("activation", CL.ap, CL.ap, AF.Ln, bias=CONSTS.ap[:, 2:3]),
             reads=[CL.r(), CONSTS.r()], writes=[CL.r()])
        S.op("dve", I("tensor_scalar", CL.ap, CL.ap, -8.0, None, ALU.mult), reads=[CL.r()], writes=[CL.r()])

        BA = pv[:, PV_BADA:PV_BADA + 144]

        ada_slot = [0]
        IDENT = pv[0:NMODR, PV_ID:PV_ID + NMODR]

        def ada_blocks(nb0, nb1, rbank, tbank, stage=None):
            m0, m1 = nb0 * 4, nb1 * 4
            n = m1 - m0
            for bi, nb in enumerate(range(nb0, nb1)):
                if stage is None:
                    w, wrng = ring_load(wada[nb], 8192)
                else:
                    stg, si = stage
                    w = stg.ap[:, si * 8192:(si + 1) * 8192]
                    wrng = stg.r(si * 8192, (si + 1) * 8192)
                    S.op("pool", I("dma_start", out=w, in_=wada[nb]), writes=[wrng], dma=True)
                mm_group(PS(rbank, 0, 512, 0, NMODR),
                         [(SC.ap[:, k * NMODR:(k + 1) * NMODR], w[:, k * 512:(k + 1) * 512]) for k in range(16)],
                         [wrng, SC.r()], PR(rbank))
                sl = ada_slot[0] % 2
                ada_slot[0] += 1
                S.op("act", I("activation", MROW.ap[0:NMODR, sl * 512:(sl + 1) * 512], PS(rbank, 0, 512, 0, NMODR), AF.Copy),
                     reads=[PR(rbank)], writes=[MROW.r(sl * 512, (sl + 1) * 512)])
                fn = [I("transpose", PS(tbank, (bi * 4 + q) * NMODR, (bi * 4 + q + 1) * NMODR),
                        MROW.ap[0:NMODR, sl * 512 + q * 128:sl * 512 + (q + 1) * 128], IDENT) for q in range(4)]
                S.op("pe", fn, reads=[MROW.r(sl * 512, (sl + 1) * 512), PV.r()], writes=[PR(tbank)])
            src = PS(tbank, 0, n * NMODR).rearrange("p (m r) -> p m r", r=NMODR)
            S.op("dve", I("tensor_tensor", MOD3[:, m0:m1, :], src,
                          BA[:, m0:m1].unsqueeze(2).to_broadcast([128, n, NMODR]), ALU.add),
                 reads=[PR(tbank), PV.r()], writes=[modr(m0, m1)])

        def mod_affine(m0, mul, add, n=16):
            v = MOD3[:, m0:m0 + n, :]
            S.op("dve", I("tensor_scalar", v, v, mul, add, ALU.mult, ALU.add),
                 reads=[modr(m0, m0 + n)], writes=[modr(m0, m0 + n)])


        def modulate(c, msc, msh, eng="dve"):
            modulate_big(c, msc, msh, eng)
            modulate_small(c, msc, msh)

        def modulate_big(c, msc, msh, eng="dve"):
            if eng == "act":
                S.op("act", I("activation", U3[:, c, 0:NPL], X3[:, c, 0:NPL], AF.Identity, scale=MOD3[:, msc + c, 0:1],
                              bias=MOD3[:, msh + c, 0:1]),
                     reads=[xr(c, 0, NPL), modr(msc + c, msc + c + 1), modr(msh + c, msh + c + 1)], writes=[ur(c, 0, NPL)])
            else:
                S.op("dve", I("tensor_scalar", U3[:, c, 0:NPL], X3[:, c, 0:NPL], MOD3[:, msc + c, 0:1],
                              MOD3[:, msh + c, 0:1], ALU.mult, ALU.add),
                     reads=[xr(c, 0, NPL), modr(msc + c, msc + c + 1), modr(msh + c, msh + c + 1)], writes=[ur(c, 0, NPL)])

        def modulate_small(c, msc, msh):
            xs = X3[:, c, SOFF:NT].rearrange("p (s t) -> p s t", t=4)
            us = U3[:, c, SOFF:NT].rearrange("p (s t) -> p s t", t=4)
            tm = TMPS.ap.rearrange("p (s t) -> p s t", t=4)
            S.op("dve", I("tensor_tensor", tm, xs, MOD3[:, msc + c, 1:17].unsqueeze(2).to_broadcast([128, 16, 4]), ALU.mult),
                 reads=[xr(c, SOFF, NT), modr(msc + c, msc + c + 1)], writes=[TMPS.r()])
            S.op("dve", I("tensor_tensor", us, tm, MOD3[:, msh + c, 1:17].unsqueeze(2).to_broadcast([128, 16, 4]), ALU.add),
                 reads=[TMPS.r(), modr(msh + c, msh + c + 1)], writes=[ur(c, SOFF, NT)])

        def resid_acc(bank, c, ti, mg):
            t0, tn = TILES[ti]
            mr = modr(mg + c, mg + c + 1)
            if ti < 2:
                S.op("dve", I("scalar_tensor_tensor", X3[:, c, t0:t0 + tn], PS(bank, 0, tn), MOD3[:, mg + c, 0:1],
                                                             X3[:, c, t0:t0 + tn], ALU.mult, ALU.add),
                     reads=[PR(bank, 0, tn), mr, xr(c, t0, t0 + tn)], writes=[xr(c, t0, t0 + tn)])
            else:
                S.op("dve", I("scalar_tensor_tensor", X3[:, c, 1024:NPL], PS(bank, 0, 4), MOD3[:, mg + c, 0:1],
                                                             X3[:, c, 1024:NPL], ALU.mult, ALU.add),
                     reads=[PR(bank, 0, 4), mr, xr(c, 1024, NPL)], writes=[xr(c, 1024, NPL)])
                tm = TMPS.ap.rearrange("p (s t) -> p s t", t=4)
                pss = PS(bank, 4, 68).rearrange("p (s t) -> p s t", t=4)
                xs = X3[:, c, SOFF:NT].rearrange("p (s t) -> p s t", t=4)
                S.op("dve", I("tensor_tensor", tm, pss, MOD3[:, mg + c, 1:17].unsqueeze(2).to_broadcast([128, 16, 4]), ALU.mult),
                     reads=[PR(bank, 4, 68), mr], writes=[TMPS.r()])
                S.op("dve", I("tensor_tensor", xs, xs, tm, ALU.add),
                     reads=[TMPS.r(), xr(c, SOFF, NT)], writes=[xr(c, SOFF, NT)])

        def ffn(fi, mg, hook, first=None):
            AR.pos = scr_mark
            H = AR.alloc(2 * GRP * NT, BF16)
            SG = AR.alloc(3 * 512, F32)
            H3 = H.v3(NT)
            ngr = NJ // GRP
            wds = {}
            slot = [0]

            def gu_chunk(g, jj):
                j = g * GRP + jj
                if j == 0 and first is not None:
                    w, wrng = first
                else:
                    w, wrng = ring_load(wgu[fi][j], 4096)
                hidx = (g % 2) * GRP + jj
                for ti, (t0, tn) in enumerate(TILES):
                    s = slot[0] % 2
                    slot[0] += 1
                    ba, bb = 2 * s, 2 * s + 1
                    ureads = [ur(k, t0, t0 + tn) for k in range(16)]
                    mm_group(PS(ba, 0, tn), [(w[:, k * 256:k * 256 + 128], U3[:, k, t0:t0 + tn]) for k in range(16)],
                             [wrng] + ureads, PR(ba, 0, tn))
                    mm_group(PS(bb, 0, tn), [(w[:, k * 256 + 128:k * 256 + 256], U3[:, k, t0:t0 + tn]) for k in range(16)],
                             [wrng] + ureads, PR(bb, 0, tn))
                    sg = SG.ap[:, s * 512:s * 512 + tn]
                    sgr = SG.r(s * 512, s * 512 + tn)
                    S.op("act", I("activation", sg, PS(ba, 0, tn), AF.Silu), reads=[PR(ba, 0, tn)], writes=[sgr])
                    hr = H.r(hidx * NT + t0, hidx * NT + t0 + tn)
                    S.op("dve", I("tensor_tensor", H3[:, hidx, t0:t0 + tn], PS(bb, 0, tn), sg, ALU.mult),
                         reads=[PR(bb, 0, tn), sgr], writes=[hr])

            def down(g):
                dslot = 0
                for c in range(16):
                    for ti, (t0, tn) in enumerate(TILES):
                        bank = 4 + (dslot % 3)
                        dslot += 1
                        pairs, reads = [], []
                        for jj in range(GRP):
                            w, wrng = wds[(g, jj)]
                            hidx = (g % 2) * GRP + jj
                            pairs.append((w[:, c * 128:(c + 1) * 128], H3[:, hidx, t0:t0 + tn]))
                            reads += [wrng, H.r(hidx * NT + t0, hidx * NT + t0 + tn)]
                        mm_group(PS(bank, 0, tn), pairs, reads, PR(bank, 0, tn))
                        resid_acc(bank, c, ti, mg)

            def load_wd(g):
                for jj in range(GRP):
                    wds[(g, jj)] = ring_load(wdn[fi][g * GRP + jj], 2048)

            for g in range(ngr):
                for jj in range(GRP):
                    gu_chunk(g, jj)
                    if hook is not None:
                        hook(g, jj)
                if g > 0:
                    load_wd(g - 1)
                    down(g - 1)
            load_wd(ngr - 1)
            down(ngr - 1)

        def layernorm(l, msc, msh, last, hook=None):
            AR.pos = scr_mark
            MEAN = AR.alloc(NT, F32)
            RSTD = AR.alloc(NT, F32)
            SQ = AR.alloc(8 * NT, BF16)
            SQ3 = SQ.v3(NT)
            for c in range(16):
                S.op("dve", I("tensor_copy", U3[:, c, :], X3[:, c, :]), reads=[xr(c)], writes=[ur(c)])
                sl = c % 8
                if c % 4 != 3:
                    S.op("act", I("activation", SQ3[:, sl, :], X3[:, c, :], AF.Square), reads=[xr(c)],
                         writes=[SQ.r(sl * NT, (sl + 1) * NT)])
                else:
                    S.op("dve", I("tensor_tensor", SQ3[:, sl, :], X3[:, c, :], X3[:, c, :], ALU.mult), reads=[xr(c)],
                         writes=[SQ.r(sl * NT, (sl + 1) * NT)])
                for ti, (t0, tn) in enumerate(TILES):
                    fn = [I("matmul", PS(ti, 0, tn), ONES.ap, U3[:, c, t0:t0 + tn], start=(c == 0), stop=(c == 15)),
                          I("matmul", PS(3 + ti, 0, tn), ONES.ap, SQ3[:, sl, t0:t0 + tn], start=(c == 0), stop=(c == 15))]
                    S.op("pe", fn, reads=[ONES.r(), ur(c, t0, t0 + tn), SQ.r(sl * NT + t0, sl * NT + t0 + tn)],
                         writes=[PR(ti, 0, tn), PR(3 + ti, 0, tn)])
            for ti, (t0, tn) in enumerate(TILES):
                mn, rs = MEAN.ap[:, t0:t0 + tn], RSTD.ap[:, t0:t0 + tn]
                mr_, rr_ = MEAN.r(t0, t0 + tn), RSTD.r(t0, t0 + tn)
                S.op("act", I("activation", mn, PS(ti, 0, tn), AF.Copy), reads=[PR(ti, 0, tn)], writes=[mr_])
                S.op("dve", I("tensor_tensor", rs, mn, mn, ALU.mult), reads=[mr_], writes=[rr_])
                S.op("dve", I("tensor_tensor", rs, PS(3 + ti, 0, tn), rs, ALU.subtract),
                     reads=[PR(3 + ti, 0, tn), rr_], writes=[rr_])
                S.op("act", I("activation", rs, rs, AF.Ln, bias=CONSTS.ap[:, 0:1]), reads=[rr_, CONSTS.r()], writes=[rr_])
                S.op("act", I("activation", rs, rs, AF.Exp, scale=-0.5), reads=[rr_], writes=[rr_])
            outs = []
            pending = []
            for c in range(16):
                xc = X3[:, c, :]
                seng = "dve"
                S.op(seng, I("tensor_tensor", xc, xc, MEAN.ap, ALU.subtract), reads=[xr(c), MEAN.r()], writes=[xr(c)])
                S.op("dve", I("tensor_tensor", xc, xc, RSTD.ap, ALU.mult), reads=[xr(c), RSTD.r()], writes=[xr(c)])
                aeng = "act" if (last or c % 4 != 0) else "dve"
                if aeng == "act":
                    S.op("act", I("activation", xc, xc, AF.Identity, scale=pv[:, PV_LNG + l * 16 + c:PV_LNG + l * 16 + c + 1],
                                  bias=pv[:, PV_LNB + l * 16 + c:PV_LNB + l * 16 + c + 1]),
                         reads=[xr(c), PV.r()], writes=[xr(c)])
                else:
                    S.op("dve", I("tensor_scalar", xc, xc, pv[:, PV_LNG + l * 16 + c:PV_LNG + l * 16 + c + 1],
                                  pv[:, PV_LNB + l * 16 + c:PV_LNB + l * 16 + c + 1], ALU.mult, ALU.add),
                         reads=[xr(c), PV.r()], writes=[xr(c)])
                if last:
                    outs.append(S.op("sp", I("dma_start", out=yT[c], in_=xc), reads=[xr(c)], writes=[("d_y", c * 32, c * 32 + 32)], dma=True))
                else:
                    modulate_big(c, msc, msh, aeng)
                    if aeng == "act":
                        pending.append(c)
                    else:
                        modulate_small(c, msc, msh)
                        while pending:
                            modulate_small(pending.pop(0), msc, msh)
                if hook is not None and c % 4 == 3:
                    hook(c // 4)
            while pending:
                modulate_small(pending.pop(0), msc, msh)
            return outs

        AR.pos = scr_mark
        STG = AR.alloc(8192, BF16)
        AR.alloc(4096, BF16)
        FG = AR.alloc(4096, BF16)
        S.op("pool", I("dma_start", out=FG.ap, in_=wgu[0][0]), writes=[FG.r()], dma=True)
        first_gu = (FG.ap, FG.r())
        for i in range(4):
            ada_blocks(i, i + 1, 0, 1)
            ada_blocks(4 + i, 5 + i, 2, 3, stage=(STG, 0))
            mod_affine(16 + 4 * i, 1.0, 1.0, 4)
            for c in range(4 * i, 4 * i + 4):
                modulate(c, 16, 0)

        def ada_hook(g, jj):
            if g == 0:
                ada_blocks(8 + jj, 9 + jj, 6, 7)
                if jj == GRP - 1:
                    mod_affine(32, 0.5 / ALPHA, 0.0)
                return
            if g <= 4 and jj in (0, 2):
                nb = 12 + 2 * (g - 1) + jj // 2
                ada_blocks(nb, nb + 1, 6, 7)

        def ln1_hook(i):
            ada_blocks(20 + i, 21 + i, 6, 7)
            if i == 3:
                mod_affine(80, 1.0 / ALPHA, 0.0)

        def ln2_hook(i):
            ada_blocks(32 + i, 33 + i, 6, 7)
            if i == 3:
                mod_affine(128, 0.5 / ALPHA, 0.0)

        ffn(0, 32, ada_hook, first_gu)
        mod_affine(64, 1.0, 1.0)
        layernorm(0, 64, 48, False, ln1_hook)

        for c in range(16):
            S.op("sp", (lambda c: I("dma_start", out=xspill[c], in_=X3[:, c, :]))(c), reads=[xr(c)],
                 writes=[("d_spill", c * 32, c * 32 + 32)], dma=True)

        XAR = Arena(arena_t, XA_LO + XA_WORDS)
        XAR.pos = XA_LO
        AR.pos = scr_mark
        YA = XAR.alloc(8 * NT, BF16)
        PG = XAR.alloc(8 * NT, BF16)
        vn_mark = XAR.pos
        A_ = XAR.alloc(NT, F32)
        GR = XAR.alloc(NT, F32)
        G_ = XAR.alloc(NT, F32)
        XC = XAR.alloc(NT, F32)
        R_ = XAR.alloc(NT, F32)
        I_ = XAR.alloc(NT, F32)
        T1 = XAR.alloc(NT, F32)
        XP = AR.alloc(1027, F32)
        XS = AR.alloc(16 * 7, F32)
        XCb = AR.alloc(NT, BF16)
        B_ = AR.alloc(NT, F32)
        YL = AR.alloc(1024, F32)
        P_ = AR.alloc(1024, F32)
        ZERO = AR.alloc(1024, F32)
        YLS = AR.alloc(64, F32)
        HS = AR.alloc(16, F32)
        YA3 = YA.v3(NT)
        PG3 = PG.ap[:, 0:8 * 1024].rearrange("p (c n) -> p c n", n=1024)
        XS3 = XS.ap.rearrange("p (s k) -> p s k", k=7)
        osm = OSM.ap
        S.op("dve", I("memset", ZERO.ap, 0.0), writes=[ZERO.r()])
        S.op("dve", I("memset", XC.ap[:, 1024:NPL], 0.0), writes=[XC.r(1024, NPL)])
        S.op("dve", I("memset", YA3[:, :, 1024:NPL], 0.0), writes=[YA.r()])
        BDA = PM.ap[:, PM_BDA:PM_BDA + 1024]
        BDX = PM.ap[:, PM_BDX:PM_BDX + 1024]

        def s4(ap2d):
            return ap2d.rearrange("p (s t) -> p s t", t=4)

        XC2 = AR.alloc(NT, F32)
        R2 = AR.alloc(NT, F32)
        I2 = AR.alloc(NT, F32)
        S.op("dve", I("memset", XC2.ap[:, 1024:NPL], 0.0), writes=[XC2.r(1024, NPL)])
        XCs, Gs, Rs, Is = [XC, XC2], [G_, GR], [R_, R2], [I_, I2]

        def ma_front(j):
            sl = j % 2
            wx, wxr = ring_load(win[j], 2048)
            wg, wgr = ring_load(win[8 + j], 2048)
            for ti, (t0, tn) in enumerate(TILES):
                mm_group(PS(ti, 0, tn), [(wx[:, k * 128:(k + 1) * 128], U3[:, k, t0:t0 + tn]) for k in range(16)],
                         [wxr] + [ur(k, t0, t0 + tn) for k in range(16)], PR(ti, 0, tn))
            for ti, (t0, tn) in enumerate(TILES):
                mm_group(PS(3 + ti, 0, tn), [(wg[:, k * 128:(k + 1) * 128], U3[:, k, t0:t0 + tn]) for k in range(16)],
                         [wgr] + [ur(k, t0, t0 + tn) for k in range(16)], PR(3 + ti, 0, tn))
            S.op("act", I("activation", XP.ap[:, 3:515], PS(0, 0, 512), AF.Copy), reads=[PR(0)], writes=[XP.r(3, 515)])
            S.op("act", I("activation", XP.ap[:, 515:1027], PS(1, 0, 512), AF.Copy), reads=[PR(1)], writes=[XP.r(515, 1027)])
            S.op("act", I("activation", XP.ap[:, 0:3], PS(2, 1, 4), AF.Identity, scale=pv[:, PV_FLG + 1:PV_FLG + 2]),
                 reads=[PR(2, 0, 4), PV.r()], writes=[XP.r(0, 3)])
            S.op("act", I("activation", XS3[:, :, 3:7], s4(PS(2, 4, 68)), AF.Copy), reads=[PR(2, 4, 68)], writes=[XS.r()])
            for ti, (t0, tn) in enumerate(TILES):
                S.op("act", (lambda ti, t0, tn: I("activation", Gs[sl].ap[:, t0:t0 + tn], PS(3 + ti, 0, tn), AF.Gelu_apprx_tanh))(ti, t0, tn),
                     reads=[PR(3 + ti, 0, tn)], writes=[Gs[sl].r(t0, t0 + tn)])
            scv = pv[:, PV_SCONV + j * 48:PV_SCONV + (j + 1) * 48].rearrange("p (s k) -> p s k", k=3)
            S.op("dve", I("tensor_copy", XS3[:, :, 0:3], scv), reads=[PV.r()], writes=[XS.r()])

            def cwk(k):
                return pv[:, PV_CW + j * 4 + k:PV_CW + j * 4 + k + 1]
            cbj = pv[:, PV_CB + j:PV_CB + j + 1]
            xcp = XCs[sl].ap[:, 0:1024]
            xcs = s4(XCs[sl].ap[:, SOFF:NT])
            S.op("dve", I("tensor_scalar", xcp, XP.ap[:, 3:1027], cwk(3), cbj, ALU.mult, ALU.add),
                 reads=[XP.r(), PV.r()], writes=[XCs[sl].r(0, 1024)])
            S.op("dve", I("tensor_scalar", xcs, XS3[:, :, 3:7], cwk(3), cbj, ALU.mult, ALU.add),
                 reads=[XS.r(), PV.r()], writes=[XCs[sl].r(SOFF, NT)])
            for k in (2, 1, 0):
                S.op("dve", (lambda k: I("scalar_tensor_tensor", xcp, XP.ap[:, k:k + 1024], cwk(k), xcp, ALU.mult, ALU.add))(k),
                     reads=[XP.r(), PV.r(), XCs[sl].r(0, 1024)], writes=[XCs[sl].r(0, 1024)])
                S.op("dve", (lambda k: I("scalar_tensor_tensor", xcs, XS3[:, :, k:k + 4], cwk(k), xcs, ALU.mult, ALU.add))(k),
                     reads=[XS.r(), PV.r(), XCs[sl].r(SOFF, NT)], writes=[XCs[sl].r(SOFF, NT)])
            S.op("dve", I("tensor_copy", osm[:, OS_NCP + j * 3:OS_NCP + j * 3 + 3], XP.ap[:, 1024:1027]),
                 reads=[XP.r()], writes=[OSM.r(OS_NCP + j * 3, OS_NCP + j * 3 + 3)])
            ncsv = osm[:, OS_NCS + j * 48:OS_NCS + (j + 1) * 48].rearrange("p (s k) -> p s k", k=3)
            S.op("dve", I("tensor_copy", ncsv, XS3[:, :, 4:7]), reads=[XS.r()],
                 writes=[OSM.r(OS_NCS + j * 48, OS_NCS + (j + 1) * 48)])
            S.op("act", I("activation", XCb.ap, XCs[sl].ap, AF.Copy), reads=[XCs[sl].r()], writes=[XCb.r()])
            for ti, (t0, tn) in enumerate(TILES):
                mm_group(PS(6, 0, tn), [(BDA[:, j * 128:(j + 1) * 128], XCb.ap[:, t0:t0 + tn])], [PM.r(), XCb.r(t0, t0 + tn)], PR(6, 0, tn))
                mm_group(PS(7, 0, tn), [(BDX[:, j * 128:(j + 1) * 128], XCb.ap[:, t0:t0 + tn])], [PM.r(), XCb.r(t0, t0 + tn)], PR(7, 0, tn))
                S.op("act", (lambda t0, tn: I("activation", Rs[sl].ap[:, t0:t0 + tn], PS(6, 0, tn), AF.Sigmoid,
                                                                  bias=pv[:, PV_BA + j:PV_BA + j + 1]))(t0, tn),
                     reads=[PR(6, 0, tn), PV.r()], writes=[Rs[sl].r(t0, t0 + tn)])
                S.op("act", (lambda t0, tn: I("activation", Is[sl].ap[:, t0:t0 + tn], PS(7, 0, tn), AF.Sigmoid,
                                                                  bias=pv[:, PV_BX + j:PV_BX + j + 1]))(t0, tn),
                     reads=[PR(7, 0, tn), PV.r()], writes=[Is[sl].r(t0, t0 + tn)])

        def ma_tail_a(j):
            sl = j % 2
            S.op("act", I("activation", A_.ap, Rs[sl].ap, AF.Exp, scale=CL.ap[:, j:j + 1]), reads=[Rs[sl].r(), CL.r()], writes=[A_.r()])
            S.op("dve", I("tensor_tensor", T1.ap, A_.ap, A_.ap, ALU.mult), reads=[A_.r()], writes=[T1.r()])
            S.op("dve", I("tensor_scalar", T1.ap, T1.ap, 1.0, None, ALU.min), reads=[T1.r()], writes=[T1.r()])
            S.op("act", I("activation", T1.ap, T1.ap, AF.Sqrt, bias=CONSTS.ap[:, 2:3], scale=-1.0),
                 reads=[T1.r(), CONSTS.r()], writes=[T1.r()])
            S.op("dve", I("tensor_tensor_scan", P_.ap, A_.ap[:, 0:1024], ZERO.ap, 1.0, ALU.mult, ALU.add),
                 reads=[A_.r(0, 1024), ZERO.r()], writes=[P_.r()])

        def ma_tail_c(j):
            sl = j % 2
            S.op("dve", I("tensor_tensor", T1.ap, T1.ap, Is[sl].ap, ALU.mult), reads=[T1.r(), Is[sl].r()], writes=[T1.r()])
            S.op("dve", I("tensor_scalar", TMPS.ap[:, 0:1], T1.ap[:, 0:1], pv[:, PV_FLG + 2:PV_FLG + 3], None, ALU.mult),
                 reads=[T1.r(0, 1), PV.r()], writes=[TMPS.r(0, 1)])
            S.op("dve", I("scalar_tensor_tensor", T1.ap[:, 0:1], Is[sl].ap[:, 0:1], pv[:, PV_FLG:PV_FLG + 1], TMPS.ap[:, 0:1],
                                                         ALU.mult, ALU.add),
                 reads=[Is[sl].r(0, 1), PV.r(), TMPS.r(0, 1)], writes=[T1.r(0, 1)])
            S.op("dve", I("tensor_tensor", B_.ap, T1.ap, XCs[sl].ap, ALU.mult), reads=[T1.r(), XCs[sl].r()], writes=[B_.r()])
            S.op("dve", I("tensor_tensor_scan", YL.ap, A_.ap[:, 0:1024], B_.ap[:, 0:1024], 0.0, ALU.mult, ALU.add),
                 reads=[A_.r(0, 1024), B_.r(0, 1024)], writes=[YL.r()])
            S.op("dve", I("tensor_copy", HLE.ap[:, j:j + 1], YL.ap[:, 1023:1024]), reads=[YL.r()], writes=[HLE.r(j, j + 1)])
            S.op("dve", I("tensor_copy", PEND.ap[:, j:j + 1], P_.ap[:, 1023:1024]), reads=[P_.r()], writes=[PEND.r(j, j + 1)])
            a_s, b_s, y_s = s4(A_.ap[:, SOFF:NT]), s4(B_.ap[:, SOFF:NT]), s4(YLS.ap)
            for t in range(4):
                prev = pv[:, PV_SH0 + j * 16:PV_SH0 + (j + 1) * 16] if t == 0 else y_s[:, :, t - 1]
                prd = [PV.r()] if t == 0 else [YLS.r()]
                S.op("dve", (lambda t, prev: I("tensor_tensor", HS.ap, a_s[:, :, t], prev, ALU.mult))(t, prev),
                     reads=[A_.r(SOFF, NT)] + prd, writes=[HS.r()])
                S.op("dve", (lambda t: I("tensor_tensor", y_s[:, :, t], HS.ap, b_s[:, :, t], ALU.add))(t),
                     reads=[HS.r(), B_.r(SOFF, NT)], writes=[YLS.r()])
            S.op("dve", I("tensor_copy", osm[:, OS_NHS + j * 16:OS_NHS + (j + 1) * 16], y_s[:, :, 3]),
                 reads=[YLS.r()], writes=[OSM.r(OS_NHS + j * 16, OS_NHS + (j + 1) * 16)])
            S.op("dve", I("tensor_tensor", YA3[:, j, 0:1024], YL.ap, Gs[sl].ap[:, 0:1024], ALU.mult),
                 reads=[YL.r(), Gs[sl].r(0, 1024)], writes=[YA.r(j * NT, j * NT + 1024)])
            S.op("dve", I("tensor_tensor", PG3[:, j, :], P_.ap, Gs[sl].ap[:, 0:1024], ALU.mult),
                 reads=[P_.r(), Gs[sl].r(0, 1024)], writes=[PG.r(j * 1024, (j + 1) * 1024)])
            S.op("dve", I("tensor_tensor", YA3[:, j, SOFF:NT], YLS.ap, Gs[sl].ap[:, SOFF:NT], ALU.mult),
                 reads=[YLS.r(), Gs[sl].r(SOFF, NT)], writes=[YA.r(j * NT + SOFF, (j + 1) * NT)])


        ma_front(0)
        for j in range(8):
            ma_tail_a(j)
            if j < 7:
                ma_front(j + 1)
            ma_tail_c(j)

        S.op("sp", I("dma_start", out=cin.ap(), in_=HLE.ap), reads=[HLE.r()], writes=[("d_cin", 0, 32)], dma=True)
        S.op("pool", I("collective_compute", "AllGather", ALU.bypass, replica_groups=[[0, 1], [2, 3], [4, 5], [6, 7]],
                                                    ins=[cin.ap().opt()], outs=[cout.ap().opt()]),
             reads=[("d_cin", 0, 32)], writes=[("d_cout", 0, 32)], dma=True, inc=1)
        S.op("sp", I("dma_start", out=GATH.ap.rearrange("p (r j) -> p r j", j=8),
                                         in_=cout.ap().rearrange("(r p) j -> p r j", p=128)),
             reads=[("d_cout", 0, 32)], writes=[GATH.r()], dma=True)

        VAR = Arena(arena_t, XA_LO + XA_WORDS)
        VAR.pos = vn_mark
        VNB = VAR.alloc(9 * 1024, BF16)
        SGAB = VAR.alloc(4 * 512, F32)
        VNB3 = VNB.v3(1024)
        AR.pos = scr_mark
        PBC = AR.alloc(NPB, F32)
        VNF = AR.alloc(1024, F32)
        VNS = AR.alloc(1024, F32)
        GU = AR.alloc(NT, F32)
        ST1 = AR.alloc(512, F32)
        BST = AR.alloc(12, F32)
        MV = AR.alloc(2, F32)
        RS1 = AR.alloc(1, F32)
        S.op("sp", I("dma_start", out=PBC.ap, in_=pbc), writes=[PBC.r()], dma=True)
        wv, wvr = ring_load(wgv, 16384)
        nvs_op = None
        for tt in range(9):
            ntok = 128 if tt < 8 else 64
            c0 = tt * 128 if tt < 8 else SOFF
            bA, bB = (0, 1) if tt % 2 == 0 else (2, 3)
            for hb, bank in ((0, bA), (1, bB)):
                mm_group(PS(bank, 0, 512, 0, ntok),
                         [(U3[:, k, c0:c0 + ntok], wv[:, k * 1024 + hb * 512:k * 1024 + (hb + 1) * 512]) for k in range(16)],
                         [wvr] + [ur(k, c0, c0 + ntok) for k in range(16)], PR(bank))
            S.op("dve", I("bn_stats", BST.ap[0:ntok, 0:6], PS(bA, 0, 512, 0, ntok)), reads=[PR(bA)], writes=[BST.r(0, 6)])
            S.op("dve", I("bn_stats", BST.ap[0:ntok, 6:12], PS(bB, 0, 512, 0, ntok)), reads=[PR(bB)], writes=[BST.r(6, 12)])
            S.op("dve", I("bn_aggr", MV.ap[0:ntok, :], BST.ap[0:ntok, :]), reads=[BST.r()], writes=[MV.r()])
            S.op("act", I("activation", RS1.ap[0:ntok, :], MV.ap[0:ntok, 1:2], AF.Sqrt, bias=CONSTS.ap[0:ntok, 1:2]),
                 reads=[MV.r(), CONSTS.r()], writes=[RS1.r()])
            S.op("dve", I("reciprocal", RS1.ap[0:ntok, :], RS1.ap[0:ntok, :]), reads=[RS1.r()], writes=[RS1.r()])
            for hb, bank in ((0, bA), (1, bB)):
                S.op("dve", (lambda hb, bank: I("tensor_scalar", VNF.ap[0:ntok, hb * 512:(hb + 1) * 512], PS(bank, 0, 512, 0, ntok),
                                                                        MV.ap[0:ntok, 0:1], RS1.ap[0:ntok, 0:1], ALU.subtract, ALU.mult))(hb, bank),
                     reads=[PR(bank), MV.r(), RS1.r()], writes=[VNF.r(hb * 512, (hb + 1) * 512)])
            S.op("dve", I("tensor_tensor", VNF.ap[0:ntok, :], VNF.ap[0:ntok, :], PBC.ap[0:ntok, PB_G:PB_G + 1024], ALU.mult),
                 reads=[VNF.r(), PBC.r()], writes=[VNF.r()])
            S.op("dve", I("tensor_tensor", VNB3[0:ntok, tt, :], VNF.ap[0:ntok, :], PBC.ap[0:ntok, PB_B:PB_B + 1024], ALU.add),
                 reads=[VNF.r(), PBC.r()], writes=[VNB.r(tt * 1024, (tt + 1) * 1024)])
            if tt == 8:
                S.op("dve", I("tensor_tensor", VNS.ap[0:64, :], VNF.ap[0:64, :], PBC.ap[0:64, PB_B:PB_B + 1024], ALU.add),
                     reads=[VNF.r(), PBC.r()], writes=[VNS.r()])
                nvs_op = S.op("sp", I("dma_start", out=nvs, in_=VNS.ap[0:64, :]), reads=[VNS.r()], writes=[("d_nvs", 0, 32)], dma=True)

        GA3 = GATH.ap.rearrange("p (r j) -> p r j", j=8)
        for r in range(2):
            ohr = pv[:, PV_OH + r:PV_OH + r + 1]
            if r == 0:
                S.op("dve", I("tensor_scalar", HIN.ap, GA3[:, 0, :], ohr, None, ALU.mult), reads=[GATH.r(), PV.r()], writes=[HIN.r()])
            else:
                S.op("dve", (lambda r, ohr: I("scalar_tensor_tensor", HIN.ap, GA3[:, r, :], ohr, HIN.ap, ALU.mult, ALU.add))(r, ohr),
                     reads=[GATH.r(), PV.r(), HIN.r()], writes=[HIN.r()])
        for j in range(8):
            S.op("dve", (lambda j: I("scalar_tensor_tensor", YA3[:, j, 0:1024], PG3[:, j, :], HIN.ap[:, j:j + 1], YA3[:, j, 0:1024],
                                                                    ALU.mult, ALU.add))(j),
                 reads=[PG.r(j * 1024, (j + 1) * 1024), HIN.r(), YA.r(j * NT, j * NT + 1024)], writes=[YA.r(j * NT, j * NT + 1024)])
        S.op("dve", I("tensor_tensor", osm[:, OS_NHP:OS_NHP + 8], PEND.ap, HIN.ap, ALU.mult),
             reads=[PEND.r(), HIN.r()], writes=[OSM.r(OS_NHP, OS_NHP + 8)])
        S.op("dve", I("tensor_tensor", osm[:, OS_NHP:OS_NHP + 8], osm[:, OS_NHP:OS_NHP + 8], HLE.ap, ALU.add),
             reads=[OSM.r(OS_NHP, OS_NHP + 8), HLE.r()], writes=[OSM.r(OS_NHP, OS_NHP + 8)])
        osm_op = S.op("sp", I("dma_start", out=osmall, in_=osm), reads=[OSM.r()], writes=[("d_osm", 0, 32)], dma=True)

        YBAR = Arena(arena_t, XA_LO + XA_WORDS)
        YBAR.pos = PG.lo // 4
        YB = YBAR.alloc(8 * NT, BF16)
        assert YBAR.pos <= vn_mark
        YB3 = YB.v3(NT)
        S.op("dve", I("memset", YB3[:, :, 1024:NPL], 0.0), writes=[YB.r()])
        WSTm = PM.ap[:, PM_WST:PM_WST + 1024]
        WSSm = PM64.ap[0:64, PM64_WSS:PM64_WSS + 512]
        for g in range(8):
            wu, wur = ring_load(win[16 + g], 2048)
            for ti, (t0, tn) in enumerate(TILES):
                mm_group(PS(3 + ti, 0, tn), [(wu[:, k * 128:(k + 1) * 128], U3[:, k, t0:t0 + tn]) for k in range(16)],
                         [wur] + [ur(k, t0, t0 + tn) for k in range(16)], PR(3 + ti, 0, tn))
                S.op("act", (lambda ti, t0, tn: I("activation", GU.ap[:, t0:t0 + tn], PS(3 + ti, 0, tn), AF.Copy))(ti, t0, tn),
                     reads=[PR(3 + ti, 0, tn)], writes=[GU.r(t0, t0 + tn)])
            for hb in range(2):
                fn = [I("matmul", PS(hb, q * 128, (q + 1) * 128), VNB3[:, hb * 4 + q, g * 128:(g + 1) * 128],
                        WSTm[:, g * 128:(g + 1) * 128], start=True, stop=True) for q in range(4)]
                S.op("pe", fn, reads=[VNB.r(hb * 4096, (hb + 1) * 4096), PM.r()], writes=[PR(hb)])
            S.op("pe", (lambda g: I("matmul", PS(2, 4, 68), VNB3[0:64, 8, g * 128:(g + 1) * 128],
                                                     WSSm[:, g * 64:(g + 1) * 64], start=True, stop=True))(g),
                 reads=[VNB.r(8 * 1024, 9 * 1024), PM64.r()], writes=[PR(2, 4, 68)])
            bsp = PBC.ap[:, PB_BSP + g * 128:PB_BSP + (g + 1) * 128].unsqueeze(1).to_broadcast([128, 4, 128])
            for hb in range(2):
                st3 = ST1.ap.rearrange("p (q t) -> p q t", t=128)
                ps3 = PS(hb, 0, 512).rearrange("p (q t) -> p q t", t=128)
                S.op("dve", (lambda ps3, st3: I("tensor_tensor", st3, ps3, bsp, ALU.add))(ps3, st3),
                     reads=[PR(hb), PBC.r()], writes=[ST1.r()])
                S.op("dve", (lambda hb: I("tensor_tensor", YB3[:, g, hb * 512:(hb + 1) * 512], ST1.ap,
                                                                   GU.ap[:, hb * 512:(hb + 1) * 512], ALU.mult))(hb),
                     reads=[ST1.r(), GU.r(hb * 512, (hb + 1) * 512)], writes=[YB.r(g * NT + hb * 512, g * NT + (hb + 1) * 512)])
            S.op("dve", I("tensor_tensor", ST1.ap[:, 0:64], PS(2, 4, 68), PBC.ap[:, PB_BSS + g * 64:PB_BSS + (g + 1) * 64], ALU.add),
                 reads=[PR(2, 4, 68), PBC.r()], writes=[ST1.r(0, 64)])
            S.op("dve", I("tensor_tensor", YB3[:, g, SOFF:NT], ST1.ap[:, 0:64], GU.ap[:, SOFF:NT], ALU.mult),
                 reads=[ST1.r(0, 64), GU.r(SOFF, NT)], writes=[YB.r(g * NT + SOFF, (g + 1) * NT)])
            if g % 2 == 0:
                ada_blocks(24 + g // 2, 25 + g // 2, 6, 7)

        AR.pos = scr_mark
        M_ = AR.alloc(16 * NT, BF16)
        M3 = M_.v3(NT)
        YA3, YB3 = YA.v3(NT), YB.v3(NT)
        cnt = 0
        for c in range(16):
            wa_, war = ring_load(wpa[c], 1024)
            wb_, wbr = ring_load(wpb[c], 1024)
            wga, wgar = ring_load(win[32 + c], 2048)
            wgb, wgbr = ring_load(win[48 + c], 2048)
            for ti, (t0, tn) in enumerate(TILES):
                s = cnt % 2
                cnt += 1
                b0 = 4 * s
                mm_group(PS(b0, 0, tn), [(wa_[:, k * 128:(k + 1) * 128], YA3[:, k, t0:t0 + tn]) for k in range(8)],
                         [war] + [YA.r(k * NT + t0, k * NT + t0 + tn) for k in range(8)], PR(b0, 0, tn))
                mm_group(PS(b0 + 1, 0, tn), [(wb_[:, k * 128:(k + 1) * 128], YB3[:, k, t0:t0 + tn]) for k in range(8)],
                         [wbr] + [YB.r(k * NT + t0, k * NT + t0 + tn) for k in range(8)], PR(b0 + 1, 0, tn))
                mm_group(PS(b0 + 2, 0, tn), [(wga[:, k * 128:(k + 1) * 128], U3[:, k, t0:t0 + tn]) for k in range(16)],
                         [wgar] + [ur(k, t0, t0 + tn) for k in range(16)], PR(b0 + 2, 0, tn))
                mm_group(PS(b0 + 3, 0, tn), [(wgb[:, k * 128:(k + 1) * 128], U3[:, k, t0:t0 + tn]) for k in range(16)],
                         [wgbr] + [ur(k, t0, t0 + tn) for k in range(16)], PR(b0 + 3, 0, tn))
                sa = SGAB.ap[:, (2 * s) * 512:(2 * s) * 512 + tn]
                sb = SGAB.ap[:, (2 * s + 1) * 512:(2 * s + 1) * 512 + tn]
                sar = SGAB.r((2 * s) * 512, (2 * s) * 512 + tn)
                sbr = SGAB.r((2 * s + 1) * 512, (2 * s + 1) * 512 + tn)
                S.op("act", (lambda sa, b0, tn: I("activation", sa, PS(b0 + 2, 0, tn), AF.Sigmoid))(sa, b0, tn),
                     reads=[PR(b0 + 2, 0, tn)], writes=[sar])
                S.op("act", (lambda sb, b0, tn: I("activation", sb, PS(b0 + 3, 0, tn), AF.Sigmoid))(sb, b0, tn),
                     reads=[PR(b0 + 3, 0, tn)], writes=[sbr])
                S.op("dve", (lambda sa, b0, tn: I("tensor_tensor", sa, sa, PS(b0, 0, tn), ALU.mult))(sa, b0, tn),
                     reads=[sar, PR(b0, 0, tn)], writes=[sar])
                S.op("dve", (lambda sb, b0, tn: I("tensor_tensor", sb, sb, PS(b0 + 1, 0, tn), ALU.mult))(sb, b0, tn),
                     reads=[sbr, PR(b0 + 1, 0, tn)], writes=[sbr])
                S.op("dve", (lambda sa, sb, c, t0, tn: I("tensor_tensor", M3[:, c, t0:t0 + tn], sa, sb, ALU.add))(sa, sb, c, t0, tn),
                     reads=[sar, sbr], writes=[M_.r(c * NT + t0, c * NT + t0 + tn)])

        for c in range(16):
            S.op("sp", (lambda c: I("dma_start", out=X3[:, c, :], in_=xspill[c]))(c), reads=[("d_spill", c * 32, c * 32 + 32)],
                 writes=[xr(c)], dma=True)
        cnt = 0
        for c in range(16):
            wo, wor = ring_load(wout[c], 2048)
            for ti, (t0, tn) in enumerate(TILES):
                bank = cnt % 4
                cnt += 1
                mm_group(PS(bank, 0, tn), [(wo[:, k * 128:(k + 1) * 128], M3[:, k, t0:t0 + tn]) for k in range(16)],
                         [wor] + [M_.r(k * NT + t0, k * NT + t0 + tn) for k in range(16)], PR(bank, 0, tn))
                resid_acc(bank, c, ti, 80)
            if c % 4 == 0:
                ada_blocks(28 + c // 4, 29 + c // 4, 6, 7)
        mod_affine(112, 1.0, 1.0)
        layernorm(1, 112, 96, False, ln2_hook)
        ffn(1, 128, None)
        outs = layernorm(2, 0, 0, True)
        S.final_wait("sp", outs + [osm_op, nvs_op])

        with nc.Block() as block:
            @block.tensor
            def _(e):
                S.emit_all("pe", e)

            @block.scalar
            def _(e):
                S.emit_all("act", e)

            @block.vector
            def _(e):
                S.emit_all("dve", e)

            @block.gpsimd
            def _(e):
                S.emit_all("pool", e)

            @block.sync
            def _(e):
                S.emit_all("sp", e)
    return nc


_NC_CACHE = {}


def _blk(w, kchunks):
    K, N = w.shape
    return np.ascontiguousarray(w.reshape(kchunks, 128, N // 128, 128).transpose(2, 1, 0, 3).reshape(N // 128, 128, kchunks * 128))


def _fm(v):
    return np.ascontiguousarray(v.reshape(-1, 128).T)


def kernel(x_prompt, x_sample, state_conv, state_h, c_prompt, c_sample,
           w_ada, b_ada, ffn1_w_gu, ffn1_w_down, ffn2_w_gu, ffn2_w_down,
           w_in, conv_w, conv_b, lru_wa, lru_ba, lru_wx, lru_bx, lru_lambda,
           gmlp_ln_g, gmlp_ln_b, gmlp_ws, gmlp_bs, w_pa, w_pb, w_out, ln_g, ln_b):
    f = np.float32
    A = lambda a: np.asarray(a, dtype=f)
    x_prompt, x_sample, state_conv, state_h = A(x_prompt), A(x_sample), A(state_conv), A(state_h)
    c_prompt, c_sample = A(c_prompt), A(c_sample)
    w_ada, b_ada, w_in = A(w_ada)[0], A(b_ada)[0], A(w_in)[0]
    gus = [A(ffn1_w_gu)[0], A(ffn2_w_gu)[0]]
    dns = [A(ffn1_w_down)[0], A(ffn2_w_down)[0]]
    conv_w, conv_b = A(conv_w)[0], A(conv_b)[0]
    lru_wa, lru_ba, lru_wx, lru_bx, lru_lambda = A(lru_wa)[0], A(lru_ba)[0], A(lru_wx)[0], A(lru_bx)[0], A(lru_lambda)[0]
    gmlp_ln_g, gmlp_ln_b, gmlp_ws, gmlp_bs = A(gmlp_ln_g)[0], A(gmlp_ln_b)[0], A(gmlp_ws)[0], A(gmlp_bs)[0]
    w_pa, w_pb, w_out, ln_g, ln_b = A(w_pa)[0], A(w_pb)[0], A(w_out)[0], A(ln_g)[0], A(ln_b)[0]

    shared = {}
    shared["wada"] = np.ascontiguousarray(w_ada.reshape(16, 128, 36, 512).transpose(2, 1, 0, 3).reshape(36, 128, 8192))
    for i in range(2):
        g = gus[i][:, :DFF].reshape(16, 128, NJ, 128)
        v = gus[i][:, DFF:].reshape(16, 128, NJ, 128)
        gv = np.stack([g, v], axis=3)
        shared["wgu%d" % (i + 1)] = np.ascontiguousarray(gv.transpose(2, 1, 0, 3, 4).reshape(NJ, 128, 4096))
        shared["wd%d" % (i + 1)] = np.ascontiguousarray(dns[i].reshape(NJ, 128, 2048))
    shared["win"] = _blk(w_in, 16)
    shared["wgv"] = np.ascontiguousarray(w_in[:, 3072:4096].reshape(16, 128, 1024).transpose(1, 0, 2).reshape(128, 16384))
    shared["wpa"] = _blk(w_pa, 8)
    shared["wpb"] = _blk(w_pb, 8)
    shared["wout"] = _blk(w_out, 16)
    pmat = np.zeros((128, NPM), f)
    for j in range(8):
        for h in range(2):
            pmat[h * 64:(h + 1) * 64, PM_BDA + j * 128 + h * 64:PM_BDA + j * 128 + (h + 1) * 64] = lru_wa[2 * j + h]
            pmat[h * 64:(h + 1) * 64, PM_BDX + j * 128 + h * 64:PM_BDX + j * 128 + (h + 1) * 64] = lru_wx[2 * j + h]
    pmat[:, PM_WST:PM_WST + 1024] = gmlp_ws.transpose(2, 0, 1).reshape(128, 1024)
    s_i, t_i = np.arange(128)[:, None], np.arange(128)[None, :]
    pmat[:, PM_MASK:PM_MASK + 128] = (s_i <= t_i).astype(f)
    shared["pmat"] = pmat
    pm64 = np.zeros((64, NPM64), f)
    w4t = gmlp_ws[:, :4, :4].transpose(2, 0, 1)
    pm64[:, PM64_WSS:PM64_WSS + 512] = np.tile(w4t[None, :, :, None, :], (16, 1, 1, 16, 1)).reshape(64, 8 * 64)
    q = np.arange(64)
    pm64[:, PM64_MASK:PM64_MASK + 64] = ((q[:, None] // 4 == q[None, :] // 4) & (q[:, None] % 4 <= q[None, :] % 4)).astype(f)
    shared["pmat64"] = pm64
    pbc = np.zeros((128, NPB), f)
    pbc[:, PB_G:PB_G + 1024] = gmlp_ln_g[None, :]
    pbc[:, PB_B:PB_B + 1024] = gmlp_ln_b[None, :]
    pbc[:, PB_BSP:PB_BSP + 1024] = gmlp_bs.reshape(1, 1024)
    pbc[:, PB_BSS:PB_BSS + 512] = np.tile(gmlp_bs[:, None, :4], (1, 16, 1)).reshape(1, 512)
    shared["pbc"] = pbc

    pv_base = np.zeros((128, NPV), f)
    pv_base[:, PV_BADA:PV_BADA + 144] = _fm(b_ada)
    pv_base[:, PV_LNG:PV_LNG + 48] = ln_g.reshape(3, 16, 128).transpose(2, 0, 1).reshape(128, 48)
    pv_base[:, PV_LNB:PV_LNB + 48] = ln_b.reshape(3, 16, 128).transpose(2, 0, 1).reshape(128, 48)
    pv_base[:, PV_CW:PV_CW + 32] = conv_w.reshape(4, 8, 128).transpose(2, 1, 0).reshape(128, 32)
    pv_base[:, PV_CB:PV_CB + 8] = _fm(conv_b)
    pv_base[:, PV_BA:PV_BA + 8] = _fm(lru_ba)
    pv_base[:, PV_BX:PV_BX + 8] = _fm(lru_bx)
    pv_base[:, PV_LAM:PV_LAM + 8] = _fm(lru_lambda)
    pv_base[0:NMODR, PV_ID:PV_ID + NMODR] = np.eye(NMODR, dtype=f)

    in_maps = []
    for r in range(NCORES):
        b, half = r // 2, r % 2
        xt = np.empty((NT, D), f)
        xt[0:1024] = x_prompt[b, half * 1024:(half + 1) * 1024]
        xt[1024:1028] = x_prompt[b, 1020:1024] if half == 1 else x_prompt[b, 0:4]
        xt[1028:1092] = x_sample[16 * r:16 * (r + 1)].reshape(64, D)
        m = dict(shared)
        m["xT"] = np.ascontiguousarray(xt.T.reshape(16, 128, NT))
        pvr = pv_base.copy()
        pvr[:, PV_FLG + 0] = 1.0 if half == 0 else 0.0
        pvr[:, PV_FLG + 1] = 1.0 if half == 1 else 0.0
        pvr[:, PV_FLG + 2] = 0.0 if half == 0 else 1.0
        if half == 1:
            pvr[:, PV_OH + 0] = 1.0
        sc = state_conv[0, 16 * r:16 * (r + 1)]
        pvr[:, PV_SCONV:PV_SCONV + 384] = sc.reshape(16, 3, 8, 128).transpose(3, 2, 0, 1).reshape(128, 384)
        sh = state_h[0, 16 * r:16 * (r + 1)]
        pvr[:, PV_SH0:PV_SH0 + 128] = sh.reshape(16, 8, 128).transpose(2, 1, 0).reshape(128, 128)
        crow = np.concatenate([c_prompt[b:b + 1], c_sample[16 * r:16 * (r + 1)]], axis=0)
        pvr[:, PV_CT:PV_CT + 16 * NMODR] = crow.reshape(NMODR, 16, 128).transpose(2, 1, 0).reshape(128, 16 * NMODR)
        m["pvec"] = pvr
        in_maps.append(m)

    if "nc" not in _NC_CACHE:
        _NC_CACHE["nc"] = build_program()
    nc = _NC_CACHE["nc"]
    res = run_bass_kernel_spmd(nc, in_maps, core_ids=list(range(NCORES)))
    R = res.results

    y_prompt = np.empty((4, 2048, D), f)
    y_sample = np.empty((128, 4, D), f)
    ncp = np.empty((1, 4, 3, 1024), f)
    nhp = np.empty((1, 4, 1024), f)
    ncs = np.empty((1, 128, 3, 1024), f)
    nhs = np.empty((1, 128, 1024), f)
    nv = np.empty((1, 128, 4, 1024), f)
    for r in range(NCORES):
        b, half = r // 2, r % 2
        yt = np.asarray(R[r]["yT"]).reshape(D, NT).T
        y_prompt[b, half * 1024:(half + 1) * 1024] = yt[0:1024]
        y_sample[16 * r:16 * (r + 1)] = yt[1028:1092].reshape(16, 4, D)
        osm = np.asarray(R[r]["osmall"])
        if half == 1:
            ncp[0, b] = osm[:, OS_NCP:OS_NCP + 24].reshape(128, 8, 3).transpose(2, 1, 0).reshape(3, 1024)
            nhp[0, b] = osm[:, OS_NHP:OS_NHP + 8].T.reshape(1024)
        ncs[0, 16 * r:16 * (r + 1)] = osm[:, OS_NCS:OS_NCS + 384].reshape(128, 8, 16, 3).transpose(2, 3, 1, 0).reshape(16, 3, 1024)
        nhs[0, 16 * r:16 * (r + 1)] = osm[:, OS_NHS:OS_NHS + 128].reshape(128, 8, 16).transpose(2, 1, 0).reshape(16, 1024)
        nv[0, 16 * r:16 * (r + 1)] = np.asarray(R[r]["nvs"]).reshape(16, 4, 1024)
    return (y_prompt, y_sample, ncp, nhp, ncs, nhs, nv)
```

```python
import contextlib
import numpy as np
import concourse.bass as bass
import concourse.mybir as mybir
from concourse.bass_utils import run_bass_kernel_spmd

F32 = mybir.dt.float32
BF16 = mybir.dt.bfloat16
ALU = mybir.AluOpType
AF = mybir.ActivationFunctionType

NCORES = 8
D = 2048
DFF = 5632
NJ = DFF // 128
NT = 1092
NPL = 1028
SOFF = 1028
TILES = [(0, 512), (512, 512), (1024, 68)]
ALPHA = 2.0 ** 0.25
EPSP = 1e-5 / (ALPHA * ALPHA)
NMODR = 17
GRP = 4
GRAN = 8

PV_BADA, PV_LNG, PV_LNB, PV_CW, PV_CB, PV_BA, PV_BX, PV_LAM = 0, 144, 192, 240, 272, 280, 288, 296
PV_FLG, PV_OH, PV_SCONV, PV_SH0, PV_CT = 304, 308, 316, 700, 828
PV_ID = 828 + 16 * NMODR
NPV = PV_ID + NMODR
PM_BDA, PM_BDX, PM_WST, PM_MASK = 0, 1024, 2048, 3072
NPM = 3200
PM64_WSS, PM64_MASK = 0, 512
NPM64 = 576
PB_G, PB_B, PB_BSP, PB_BSS = 0, 1024, 2048, 3072
NPB = 3584
OS_NCP, OS_NHP, OS_NCS, OS_NHS = 0, 24, 32, 416
NOS = 544


def I(name, *args, **kw):
    return (name, args, kw)


class Op:
    __slots__ = ("eng", "sem", "val", "dma")

    def __init__(self, eng, sem, val, dma):
        self.eng, self.sem, self.val, self.dma = eng, sem, val, dma


class Cell:
    __slots__ = ("w", "r")

    def __init__(self):
        self.w = {}
        self.r = {}


class Eng:
    def __init__(self, name, clock):
        self.name, self.clock, self.seq = name, clock, 0
        self.prog = []
        self.waited = {}


class Sched:
    def __init__(self, nc, stack):
        self.nc = nc
        self.engs = {}
        for n in ("pe", "act", "dve", "pool", "sp"):
            self.engs[n] = Eng(n, stack.enter_context(nc.semaphore("clk_" + n)))
        self.dsems = {}
        for q, cnt in (("pool", 40), ("sp", 24)):
            self.dsems[q] = [[stack.enter_context(nc.semaphore("d%s%d" % (q, i))), 0] for i in range(cnt)]
        self.drr = {"pool": 0, "sp": 0}
        self.ccsem = [stack.enter_context(nc.semaphore("ccsem")), 0]
        self.cells = {}
        self.nops = 0
        self.out_ops = []

    def _cells(self, rng):
        sp, lo, hi = rng
        cs = self.cells
        out = []
        for g in range(lo // GRAN, (hi + GRAN - 1) // GRAN):
            k = (sp, g)
            c = cs.get(k)
            if c is None:
                c = cs[k] = Cell()
            out.append(c)
        return out

    def op(self, en, fn, reads=(), writes=(), dma=False, inc=None):
        E = self.engs[en]
        raw, oth = {}, {}
        rcells = [c for r in reads for c in self._cells(r)]
        wcells = [c for r in writes for c in self._cells(r)]
        for c in rcells:
            for o in c.w.values():
                raw[id(o)] = o
        for c in wcells:
            for o in c.w.values():
                oth[id(o)] = o
            for o in c.r.values():
                oth[id(o)] = o
        waits = []

        def addwait(sem, val):
            k = id(sem)
            if E.waited.get(k, -1) >= val:
                return
            E.waited[k] = val
            waits.append((sem, val))

        for o in raw.values():
            if o.eng == en and not o.dma and not dma and en == "pe":
                continue
            addwait(o.sem, o.val)
        for o in oth.values():
            if id(o) in raw:
                continue
            if o.eng == en and not o.dma and not dma and en == "pe":
                continue
            addwait(o.sem, o.val)
        if dma:
            if inc is not None:
                ent = self.ccsem
            else:
                pool = self.dsems[en]
                i = self.drr[en]
                self.drr[en] = (i + 1) % len(pool)
                ent = pool[i]
            if ent[1] > 0:
                addwait(ent[0], ent[1])
            step = 16 if inc is None else inc
            ent[1] += step
            o = Op(en, ent[0], ent[1], True)
            key = ("d", self.nops)
        else:
            E.seq += 1
            step = 1
            o = Op(en, E.clock, E.seq, False)
            key = en
        self.nops += 1
        for c in rcells:
            c.r[key] = o
        for c in wcells:
            c.w = {key: o}
            c.r = {}
        E.prog.append((waits, fn, o.sem, step))
        return o

    def final_wait(self, en, ops):
        E = self.engs[en]
        waits = []
        for o in ops:
            waits.append((o.sem, o.val))
        E.prog.append((waits, None, None, None))

    def emit_all(self, en, e):
        for waits, fn, sem, step in self.engs[en].prog:
            for s, v in waits:
                e.wait_ge(s, v)
            if fn is None:
                continue
            lst = fn if isinstance(fn, list) else [fn]
            ins = None
            for name, args, kw in lst:
                ins = getattr(e, name)(*args, **kw)
            ins.then_inc(sem, step)


class T:
    def __init__(self, ap, lo, esz, n):
        self.ap, self.lo, self.esz, self.n = ap, lo, esz, n

    def r(self, a=0, b=None):
        if b is None:
            b = self.n
        return ("sb", self.lo + a * self.esz, self.lo + b * self.esz)

    def v3(self, inner):
        return self.ap.rearrange("p (c n) -> p c n", n=inner)


class Arena:
    def __init__(self, ap, nwords):
        self.A, self.nwords, self.pos = ap, nwords, 0

    def alloc(self, n, dt):
        esz = 4 if dt == F32 else 2
        words = (n * esz + 3) // 4
        words = (words + 7) // 8 * 8
        lo = self.pos
        assert lo + words <= self.nwords, "arena overflow %d + %d > %d" % (lo, words, self.nwords)
        self.pos += words
        v = self.A[:, lo:lo + words]
        if dt == BF16:
            v = v.bitcast(BF16)
        v = v[:, 0:n]
        return T(v, lo * 4, esz, n)


def build_program():
    nc = bass.Bass("TRN2", target_bir_lowering=False)

    def din(name, shape):
        return nc.dram_tensor(name, list(shape), F32, kind="ExternalInput").ap()

    def dout(name, shape):
        return nc.dram_tensor(name, list(shape), F32, kind="ExternalOutput").ap()

    xT = din("xT", [16, 128, NT])
    pvec = din("pvec", [128, NPV])
    pmat = din("pmat", [128, NPM])
    pmat64 = din("pmat64", [64, NPM64])
    pbc = din("pbc", [128, NPB])
    wada = din("wada", [36, 128, 8192])
    wgu = [din("wgu1", [NJ, 128, 4096]), din("wgu2", [NJ, 128, 4096])]
    wdn = [din("wd1", [NJ, 128, 2048]), din("wd2", [NJ, 128, 2048])]
    win = din("win", [64, 128, 2048])
    wgv = din("wgv", [128, 16384])
    wpa = din("wpa", [16, 128, 1024])
    wpb = din("wpb", [16, 128, 1024])
    wout = din("wout", [16, 128, 2048])
    yT = dout("yT", [16, 128, NT])
    osmall = dout("osmall", [128, NOS])
    nvs = dout("nvs", [64, 1024])
    xspill = nc.dram_tensor("xspill", [16, 128, NT], F32).ap()
    cin = nc.dram_tensor("cin", [128, 8], F32)
    cout = nc.dram_tensor("cout", [2 * 128, 8], F32)

    stack = contextlib.ExitStack()
    with stack:
        NW = 53184
        arena_t = stack.enter_context(nc.sbuf_tensor("arena", [128, NW], F32))
        ps_t = stack.enter_context(nc.psum_tensor("ps", [128, 8, 512], F32))
        S = Sched(nc, stack)
        AR = Arena(arena_t, NW)

        def PS(b, n0, n1, p0=0, p1=128):
            return ps_t[p0:p1, b, n0:n1]

        def PR(b, n0=0, n1=512):
            return ("ps", b * 2048, (b + 1) * 2048)

        X = AR.alloc(16 * NT, F32)
        U = AR.alloc(16 * NT, BF16)
        NRU = 10
        RING = AR.alloc(NRU * 2048, BF16)
        MOD = AR.alloc(144 * NMODR, F32)
        PV = AR.alloc(NPV, F32)
        PM = AR.alloc(NPM, BF16)
        PM64 = AR.alloc(NPM64, BF16)
        SC = AR.alloc(16 * NMODR, BF16)
        ONES = AR.alloc(128, BF16)
        CL = AR.alloc(8, F32)
        CONSTS = AR.alloc(8, F32)
        OSM = AR.alloc(NOS, F32)
        HIN = AR.alloc(8, F32)
        HLE = AR.alloc(8, F32)
        PEND = AR.alloc(8, F32)
        GATH = AR.alloc(16, F32)
        MROW = AR.alloc(2 * 512, F32)
        TMPS = AR.alloc(64, F32)
        scr_mark = AR.pos
        X3, U3, MOD3 = X.v3(NT), U.v3(NT), MOD.v3(NMODR)
        XA_LO, XA_WORDS = X.lo // 4, 16 * NT

        def xr(c, a=0, b=NT):
            return X.r(c * NT + a, c * NT + b)

        def ur(c, a=0, b=NT):
            return U.r(c * NT + a, c * NT + b)

        def modr(m0, m1):
            return MOD.r(m0 * NMODR, m1 * NMODR)

        ring_pos = [0]

        def ring_load(src, nelem):
            units = (nelem + 2047) // 2048
            if ring_pos[0] + units > NRU:
                ring_pos[0] = 0
            u0 = ring_pos[0]
            ring_pos[0] += units
            dst = RING.ap[:, u0 * 2048:u0 * 2048 + nelem]
            rng = RING.r(u0 * 2048, u0 * 2048 + nelem)
            S.op("pool", I("dma_start", out=dst, in_=src), writes=[rng], dma=True)
            return dst, rng

        def mm_group(out_ap, pairs, reads, wr):
            n = len(pairs)
            fn = [I("matmul", out_ap, l, r, start=(i == 0), stop=(i == n - 1)) for i, (l, r) in enumerate(pairs)]
            return S.op("pe", fn, reads=reads, writes=[wr])

        S.op("sp", I("dma_start", out=PV.ap, in_=pvec), writes=[PV.r()], dma=True)
        for c in range(16):
            S.op("sp", (lambda c: I("dma_start", out=X3[:, c, :], in_=xT[c]))(c), writes=[xr(c)], dma=True)
        S.op("pool", I("dma_start", out=PM.ap, in_=pmat), writes=[PM.r()], dma=True)
        S.op("pool", I("dma_start", out=PM64.ap[0:64, :], in_=pmat64), writes=[PM64.r()], dma=True)
        S.op("dve", I("memset", ONES.ap, 1.0 / D), writes=[ONES.r()])
        S.op("dve", I("memset", CONSTS.ap[:, 0:1], EPSP), writes=[CONSTS.r()])
        S.op("dve", I("memset", CONSTS.ap[:, 1:2], 1e-5), writes=[CONSTS.r()])
        S.op("dve", I("memset", CONSTS.ap[:, 2:3], 1.0), writes=[CONSTS.r()])
        pv = PV.ap
        S.op("act", I("activation", SC.ap, pv[:, PV_CT:PV_CT + 16 * NMODR], AF.Silu),
             reads=[PV.r()], writes=[SC.r()])
        WST3 = PM.ap[:, PM_WST:PM_WST + 1024].rearrange("p (g t) -> p g t", g=8)
        MK = PM.ap[:, PM_MASK:PM_MASK + 128]
        S.op("dve", I("tensor_tensor", WST3, WST3, MK.unsqueeze(1).to_broadcast([128, 8, 128]), ALU.mult),
             reads=[PM.r()], writes=[PM.r()])
        WSS3 = PM64.ap[0:64, PM64_WSS:PM64_WSS + 512].rearrange("p (g t) -> p g t", g=8)
        MK64 = PM64.ap[0:64, PM64_MASK:PM64_MASK + 64]
        S.op("dve", I("tensor_tensor", WSS3, WSS3, MK64.unsqueeze(1).to_broadcast([64, 8, 64]), ALU.mult),
             reads=[PM64.r()], writes=[PM64.r()])
        S.op("act", I("activation", CL.ap, pv[:, PV_LAM:PV_LAM + 8], AF.Exp, scale=-1.0),
             reads=[PV.r()], writes=[CL.r()])
        S.op("act", I("activation", CL.ap, CL.ap, AF.Ln, bias=CONSTS.ap[:, 2:3]),
             reads=[CL.r(), CONSTS.r()], writes=[CL.r()])
        S.op("dve", I("tensor_scalar", CL.ap, CL.ap, -8.0, None, ALU.mult), reads=[CL.r()], writes=[CL.r()])

        BA = pv[:, PV_BADA:PV_BADA + 144]

        ada_slot = [0]
        IDENT = pv[0:NMODR, PV_ID:PV_ID + NMODR]

        def ada_blocks(nb0, nb1, rbank, tbank, stage=None):
            m0, m1 = nb0 * 4, nb1 * 4
            n = m1 - m0
            for bi, nb in enumerate(range(nb0, nb1)):
                if stage is None:
                    w, wrng = ring_load(wada[nb], 8192)
                else:
                    stg, si = stage
                    w = stg.ap[:, si * 8192:(si + 1) * 8192]
                    wrng = stg.r(si * 8192, (si + 1) * 8192)
                    S.op("pool", I("dma_start", out=w, in_=wada[nb]), writes=[wrng], dma=True)
                mm_group(PS(rbank, 0, 512, 0, NMODR),
                         [(SC.ap[:, k * NMODR:(k + 1) * NMODR], w[:, k * 512:(k + 1) * 512]) for k in range(16)],
                         [wrng, SC.r()], PR(rbank))
                sl = ada_slot[0] % 2
                ada_slot[0] += 1
                S.op("act", I("activation", MROW.ap[0:NMODR, sl * 512:(sl + 1) * 512], PS(rbank, 0, 512, 0, NMODR), AF.Copy),
                     reads=[PR(rbank)], writes=[MROW.r(sl * 512, (sl + 1) * 512)])
                fn = [I("transpose", PS(tbank, (bi * 4 + q) * NMODR, (bi * 4 + q + 1) * NMODR),
                        MROW.ap[0:NMODR, sl * 512 + q * 128:sl * 512 + (q + 1) * 128], IDENT) for q in range(4)]
                S.op("pe", fn, reads=[MROW.r(sl * 512, (sl + 1) * 512), PV.r()], writes=[PR(tbank)])
            src = PS(tbank, 0, n * NMODR).rearrange("p (m r) -> p m r", r=NMODR)
            S.op("dve", I("tensor_tensor", MOD3[:, m0:m1, :], src,
                          BA[:, m0:m1].unsqueeze(2).to_broadcast([128, n, NMODR]), ALU.add),
                 reads=[PR(tbank), PV.r()], writes=[modr(m0, m1)])

        def mod_affine(m0, mul, add, n=16):
            v = MOD3[:, m0:m0 + n, :]
            S.op("dve", I("tensor_scalar", v, v, mul, add, ALU.mult, ALU.add),
                 reads=[modr(m0, m0 + n)], writes=[modr(m0, m0 + n)])


        def modulate(c, msc, msh, eng="dve"):
            modulate_big(c, msc, msh, eng)
            modulate_small(c, msc, msh)

        def modulate_big(c, msc, msh, eng="dve"):
            if eng == "act":
                S.op("act", I("activation", U3[:, c, 0:NPL], X3[:, c, 0:NPL], AF.Identity, scale=MOD3[:, msc + c, 0:1],
                              bias=MOD3[:, msh + c, 0:1]),
                     reads=[xr(c, 0, NPL), modr(msc + c, msc + c + 1), modr(msh + c, msh + c + 1)], writes=[ur(c, 0, NPL)])
            else:
                S.op("dve", I("tensor_scalar", U3[:, c, 0:NPL], X3[:, c, 0:NPL], MOD3[:, msc + c, 0:1],
                              MOD3[:, msh + c, 0:1], ALU.mult, ALU.add),
                     reads=[xr(c, 0, NPL), modr(msc + c, msc + c + 1), modr(msh + c, msh + c + 1)], writes=[ur(c, 0, NPL)])

        def modulate_small(c, msc, msh):
            xs = X3[:, c, SOFF:NT].rearrange("p (s t) -> p s t", t=4)
            us = U3[:, c, SOFF:NT].rearrange("p (s t) -> p s t", t=4)
            tm = TMPS.ap.rearrange("p (s t) -> p s t", t=4)
            S.op("dve", I("tensor_tensor", tm, xs, MOD3[:, msc + c, 1:17].unsqueeze(2).to_broadcast([128, 16, 4]), ALU.mult),
                 reads=[xr(c, SOFF, NT), modr(msc + c, msc + c + 1)], writes=[TMPS.r()])
            S.op("dve", I("tensor_tensor", us, tm, MOD3[:, msh + c, 1:17].unsqueeze(2).to_broadcast([128, 16, 4]), ALU.add),
                 reads=[TMPS.r(), modr(msh + c, msh + c + 1)], writes=[ur(c, SOFF, NT)])

        def resid_acc(bank, c, ti, mg):
            t0, tn = TILES[ti]
            mr = modr(mg + c, mg + c + 1)
            if ti < 2:
                S.op("dve", I("scalar_tensor_tensor", X3[:, c, t0:t0 + tn], PS(bank, 0, tn), MOD3[:, mg + c, 0:1],
                                                             X3[:, c, t0:t0 + tn], ALU.mult, ALU.add),
                     reads=[PR(bank, 0, tn), mr, xr(c, t0, t0 + tn)], writes=[xr(c, t0, t0 + tn)])
            else:
                S.op("dve", I("scalar_tensor_tensor", X3[:, c, 1024:NPL], PS(bank, 0, 4), MOD3[:, mg + c, 0:1],
                                                             X3[:, c, 1024:NPL], ALU.mult, ALU.add),
                     reads=[PR(bank, 0, 4), mr, xr(c, 1024, NPL)], writes=[xr(c, 1024, NPL)])
                tm = TMPS.ap.rearrange("p (s t) -> p s t", t=4)
                pss = PS(bank, 4, 68).rearrange("p (s t) -> p s t", t=4)
                xs = X3[:, c, SOFF:NT].rearrange("p (s t) -> p s t", t=4)
                S.op("dve", I("tensor_tensor", tm, pss, MOD3[:, mg + c, 1:17].unsqueeze(2).to_broadcast([128, 16, 4]), ALU.mult),
                     reads=[PR(bank, 4, 68), mr], writes=[TMPS.r()])
                S.op("dve", I("tensor_tensor", xs, xs, tm, ALU.add),
                     reads=[TMPS.r(), xr(c, SOFF, NT)], writes=[xr(c, SOFF, NT)])

        def ffn(fi, mg, hook, first=None):
            AR.pos = scr_mark
            H = AR.alloc(2 * GRP * NT, BF16)
            SG = AR.alloc(3 * 512, F32)
            H3 = H.v3(NT)
            ngr = NJ // GRP
            wds = {}
            slot = [0]

            def gu_chunk(g, jj):
                j = g * GRP + jj
                if j == 0 and first is not None:
                    w, wrng = first
                else:
                    w, wrng = ring_load(wgu[fi][j], 4096)
                hidx = (g % 2) * GRP + jj
                for ti, (t0, tn) in enumerate(TILES):
                    s = slot[0] % 2
                    slot[0] += 1
                    ba, bb = 2 * s, 2 * s + 1
                    ureads = [ur(k, t0, t0 + tn) for k in range(16)]
                    mm_group(PS(ba, 0, tn), [(w[:, k * 256:k * 256 + 128], U3[:, k, t0:t0 + tn]) for k in range(16)],
                             [wrng] + ureads, PR(ba, 0, tn))
                    mm_group(PS(bb, 0, tn), [(w[:, k * 256 + 128:k * 256 + 256], U3[:, k, t0:t0 + tn]) for k in range(16)],
                             [wrng] + ureads, PR(bb, 0, tn))
                    sg = SG.ap[:, s * 512:s * 512 + tn]
                    sgr = SG.r(s * 512, s * 512 + tn)
                    S.op("act", I("activation", sg, PS(ba, 0, tn), AF.Silu), reads=[PR(ba, 0, tn)], writes=[sgr])
                    hr = H.r(hidx * NT + t0, hidx * NT + t0 + tn)
                    S.op("dve", I("tensor_tensor", H3[:, hidx, t0:t0 + tn], PS(bb, 0, tn), sg, ALU.mult),
                         reads=[PR(bb, 0, tn), sgr], writes=[hr])

            def down(g):
                dslot = 0
                for c in range(16):
                    for ti, (t0, tn) in enumerate(TILES):
                        bank = 4 + (dslot % 3)
                        dslot += 1
                        pairs, reads = [], []
                        for jj in range(GRP):
                            w, wrng = wds[(g, jj)]
                            hidx = (g % 2) * GRP + jj
                            pairs.append((w[:, c * 128:(c + 1) * 128], H3[:, hidx, t0:t0 + tn]))
                            reads += [wrng, H.r(hidx * NT + t0, hidx * NT + t0 + tn)]
                        mm_group(PS(bank, 0, tn), pairs, reads, PR(bank, 0, tn))
                        resid_acc(bank, c, ti, mg)

            def load_wd(g):
                for jj in range(GRP):
                    wds[(g, jj)] = ring_load(wdn[fi][g * GRP + jj], 2048)

            for g in range(ngr):
                for jj in range(GRP):
                    gu_chunk(g, jj)
                    if hook is not None:
                        hook(g, jj)
                if g > 0:
                    load_wd(g - 1)
                    down(g - 1)
            load_wd(ngr - 1)
            down(ngr - 1)

        def layernorm(l, msc, msh, last, hook=None):
            AR.pos = scr_mark
            MEAN = AR.alloc(NT, F32)
            RSTD = AR.alloc(NT, F32)
            SQ = AR.alloc(8 * NT, BF16)
            SQ3 = SQ.v3(NT)
            for c in range(16):
                S.op("dve", I("tensor_copy", U3[:, c, :], X3[:, c, :]), reads=[xr(c)], writes=[ur(c)])
                sl = c % 8
                if c % 4 != 3:
                    S.op("act", I("activation", SQ3[:, sl, :], X3[:, c, :], AF.Square), reads=[xr(c)],
                         writes=[SQ.r(sl * NT, (sl + 1) * NT)])
                else:
                    S.op("dve", I("tensor_tensor", SQ3[:, sl, :], X3[:, c, :], X3[:, c, :], ALU.mult), reads=[xr(c)],
                         writes=[SQ.r(sl * NT, (sl + 1) * NT)])
                for ti, (t0, tn) in enumerate(TILES):
                    fn = [I("matmul", PS(ti, 0, tn), ONES.ap, U3[:, c, t0:t0 + tn], start=(c == 0), stop=(c == 15)),
                          I("matmul", PS(3 + ti, 0, tn), ONES.ap, SQ3[:, sl, t0:t0 + tn], start=(c == 0), stop=(c == 15))]
                    S.op("pe", fn, reads=[ONES.r(), ur(c, t0, t0 + tn), SQ.r(sl * NT + t0, sl * NT + t0 + tn)],
                         writes=[PR(ti, 0, tn), PR(3 + ti, 0, tn)])
            for ti, (t0, tn) in enumerate(TILES):
                mn, rs = MEAN.ap[:, t0:t0 + tn], RSTD.ap[:, t0:t0 + tn]
                mr_, rr_ = MEAN.r(t0, t0 + tn), RSTD.r(t0, t0 + tn)
                S.op("act", I("activation", mn, PS(ti, 0, tn), AF.Copy), reads=[PR(ti, 0, tn)], writes=[mr_])
                S.op("dve", I("tensor_tensor", rs, mn, mn, ALU.mult), reads=[mr_], writes=[rr_])
                S.op("dve", I("tensor_tensor", rs, PS(3 + ti, 0, tn), rs, ALU.subtract),
                     reads=[PR(3 + ti, 0, tn), rr_], writes=[rr_])
                S.op("act", I("activation", rs, rs, AF.Ln, bias=CONSTS.ap[:, 0:1]), reads=[rr_, CONSTS.r()], writes=[rr_])
                S.op("act", I("activation", rs, rs, AF.Exp, scale=-0.5), reads=[rr_], writes=[rr_])
            outs = []
            pending = []
            for c in range(16):
                xc = X3[:, c, :]
                seng = "dve"
                S.op(seng, I("tensor_tensor", xc, xc, MEAN.ap, ALU.subtract), reads=[xr(c), MEAN.r()], writes=[xr(c)])
                S.op("dve", I("tensor_tensor", xc, xc, RSTD.ap, ALU.mult), reads=[xr(c), RSTD.r()], writes=[xr(c)])
                aeng = "act"
                if aeng == "act":
                    S.op("act", I("activation", xc, xc, AF.Identity, scale=pv[:, PV_LNG + l * 16 + c:PV_LNG + l * 16 + c + 1],
                                  bias=pv[:, PV_LNB + l * 16 + c:PV_LNB + l * 16 + c + 1]),
                         reads=[xr(c), PV.r()], writes=[xr(c)])
                else:
                    S.op("dve", I("tensor_scalar", xc, xc, pv[:, PV_LNG + l * 16 + c:PV_LNG + l * 16 + c + 1],
                                  pv[:, PV_LNB + l * 16 + c:PV_LNB + l * 16 + c + 1], ALU.mult, ALU.add),
                         reads=[xr(c), PV.r()], writes=[xr(c)])
                if last:
                    outs.append(S.op("sp", I("dma_start", out=yT[c], in_=xc), reads=[xr(c)], writes=[("d_y", c * 32, c * 32 + 32)], dma=True))
                else:
                    modulate_big(c, msc, msh, aeng)
                    if aeng == "act":
                        pending.append(c)
                    else:
                        modulate_small(c, msc, msh)
                        while pending:
                            modulate_small(pending.pop(0), msc, msh)
                if hook is not None and c % 4 == 3:
                    hook(c // 4)
            while pending:
                modulate_small(pending.pop(0), msc, msh)
            return outs

        AR.pos = scr_mark
        STG = AR.alloc(8192, BF16)
        AR.alloc(4096, BF16)
        FG = AR.alloc(4096, BF16)
        S.op("pool", I("dma_start", out=FG.ap, in_=wgu[0][0]), writes=[FG.r()], dma=True)
        first_gu = (FG.ap, FG.r())
        for i in range(4):
            ada_blocks(i, i + 1, 0, 1)
            ada_blocks(4 + i, 5 + i, 2, 3, stage=(STG, 0))
            mod_affine(16 + 4 * i, 1.0, 1.0, 4)
            for c in range(4 * i, 4 * i + 4):
                modulate(c, 16, 0)

        def ada_hook(g, jj):
            if g == 0:
                ada_blocks(8 + jj, 9 + jj, 6, 7)
                if jj == GRP - 1:
                    mod_affine(32, 0.5 / ALPHA, 0.0)
                return
            if g <= 4 and jj in (0, 2):
                nb = 12 + 2 * (g - 1) + jj // 2
                ada_blocks(nb, nb + 1, 6, 7)

        def ln1_hook(i):
            ada_blocks(20 + i, 21 + i, 6, 7)
            if i == 3:
                mod_affine(80, 1.0 / ALPHA, 0.0)

        def ln2_hook(i):
            ada_blocks(32 + i, 33 + i, 6, 7)
            if i == 3:
                mod_affine(128, 0.5 / ALPHA, 0.0)

        ffn(0, 32, ada_hook, first_gu)
        mod_affine(64, 1.0, 1.0)
        layernorm(0, 64, 48, False, ln1_hook)

        for c in range(16):
            S.op("sp", (lambda c: I("dma_start", out=xspill[c], in_=X3[:, c, :]))(c), reads=[xr(c)],
                 writes=[("d_spill", c * 32, c * 32 + 32)], dma=True)

        XAR = Arena(arena_t, XA_LO + XA_WORDS)
        XAR.pos = XA_LO
        AR.pos = scr_mark
        YA = XAR.alloc(8 * NT, BF16)
        PG = XAR.alloc(8 * NT, BF16)
        vn_mark = XAR.pos
        A_ = XAR.alloc(NT, F32)
        GR = XAR.alloc(NT, F32)
        G_ = XAR.alloc(NT, F32)
        XC = XAR.alloc(NT, F32)
        R_ = XAR.alloc(NT, F32)
        I_ = XAR.alloc(NT, F32)
        T1 = XAR.alloc(NT, F32)
        XP = AR.alloc(1027, F32)
        XS = AR.alloc(16 * 7, F32)
        XCb = AR.alloc(NT, BF16)
        B_ = AR.alloc(NT, F32)
        YL = AR.alloc(1024, F32)
        P_ = AR.alloc(1024, F32)
        ZERO = AR.alloc(1024, F32)
        YLS = AR.alloc(64, F32)
        HS = AR.alloc(16, F32)
        YA3 = YA.v3(NT)
        PG3 = PG.ap[:, 0:8 * 1024].rearrange("p (c n) -> p c n", n=1024)
        XS3 = XS.ap.rearrange("p (s k) -> p s k", k=7)
        osm = OSM.ap
        S.op("dve", I("memset", ZERO.ap, 0.0), writes=[ZERO.r()])
        S.op("dve", I("memset", XC.ap[:, 1024:NPL], 0.0), writes=[XC.r(1024, NPL)])
        S.op("dve", I("memset", YA3[:, :, 1024:NPL], 0.0), writes=[YA.r()])
        BDA = PM.ap[:, PM_BDA:PM_BDA + 1024]
        BDX = PM.ap[:, PM_BDX:PM_BDX + 1024]

        def s4(ap2d):
            return ap2d.rearrange("p (s t) -> p s t", t=4)

        XC2 = AR.alloc(NT, F32)
        R2 = AR.alloc(NT, F32)
        I2 = AR.alloc(NT, F32)
        S.op("dve", I("memset", XC2.ap[:, 1024:NPL], 0.0), writes=[XC2.r(1024, NPL)])
        XCs, Gs, Rs, Is = [XC, XC2], [G_, GR], [R_, R2], [I_, I2]

        def ma_front(j):
            sl = j % 2
            wx, wxr = ring_load(win[j], 2048)
            wg, wgr = ring_load(win[8 + j], 2048)
            for ti, (t0, tn) in enumerate(TILES):
                mm_group(PS(ti, 0, tn), [(wx[:, k * 128:(k + 1) * 128], U3[:, k, t0:t0 + tn]) for k in range(16)],
                         [wxr] + [ur(k, t0, t0 + tn) for k in range(16)], PR(ti, 0, tn))
            for ti, (t0, tn) in enumerate(TILES):
                mm_group(PS(3 + ti, 0, tn), [(wg[:, k * 128:(k + 1) * 128], U3[:, k, t0:t0 + tn]) for k in range(16)],
                         [wgr] + [ur(k, t0, t0 + tn) for k in range(16)], PR(3 + ti, 0, tn))
            S.op("act", I("activation", XP.ap[:, 3:515], PS(0, 0, 512), AF.Copy), reads=[PR(0)], writes=[XP.r(3, 515)])
            S.op("act", I("activation", XP.ap[:, 515:1027], PS(1, 0, 512), AF.Copy), reads=[PR(1)], writes=[XP.r(515, 1027)])
            S.op("act", I("activation", XP.ap[:, 0:3], PS(2, 1, 4), AF.Identity, scale=pv[:, PV_FLG + 1:PV_FLG + 2]),
                 reads=[PR(2, 0, 4), PV.r()], writes=[XP.r(0, 3)])
            S.op("act", I("activation", XS3[:, :, 3:7], s4(PS(2, 4, 68)), AF.Copy), reads=[PR(2, 4, 68)], writes=[XS.r()])
            for ti, (t0, tn) in enumerate(TILES):
                S.op("act", (lambda ti, t0, tn: I("activation", Gs[sl].ap[:, t0:t0 + tn], PS(3 + ti, 0, tn), AF.Gelu_apprx_tanh))(ti, t0, tn),
                     reads=[PR(3 + ti, 0, tn)], writes=[Gs[sl].r(t0, t0 + tn)])
            scv = pv[:, PV_SCONV + j * 48:PV_SCONV + (j + 1) * 48].rearrange("p (s k) -> p s k", k=3)
            S.op("dve", I("tensor_copy", XS3[:, :, 0:3], scv), reads=[PV.r()], writes=[XS.r()])

            def cwk(k):
                return pv[:, PV_CW + j * 4 + k:PV_CW + j * 4 + k + 1]
            cbj = pv[:, PV_CB + j:PV_CB + j + 1]
            xcp = XCs[sl].ap[:, 0:1024]
            xcs = s4(XCs[sl].ap[:, SOFF:NT])
            S.op("dve", I("tensor_scalar", xcp, XP.ap[:, 3:1027], cwk(3), cbj, ALU.mult, ALU.add),
                 reads=[XP.r(), PV.r()], writes=[XCs[sl].r(0, 1024)])
            S.op("dve", I("tensor_scalar", xcs, XS3[:, :, 3:7], cwk(3), cbj, ALU.mult, ALU.add),
                 reads=[XS.r(), PV.r()], writes=[XCs[sl].r(SOFF, NT)])
            for k in (2, 1, 0):
                S.op("dve", (lambda k: I("scalar_tensor_tensor", xcp, XP.ap[:, k:k + 1024], cwk(k), xcp, ALU.mult, ALU.add))(k),
                     reads=[XP.r(), PV.r(), XCs[sl].r(0, 1024)], writes=[XCs[sl].r(0, 1024)])
                S.op("dve", (lambda k: I("scalar_tensor_tensor", xcs, XS3[:, :, k:k + 4], cwk(k), xcs, ALU.mult, ALU.add))(k),
                     reads=[XS.r(), PV.r(), XCs[sl].r(SOFF, NT)], writes=[XCs[sl].r(SOFF, NT)])
            S.op("dve", I("tensor_copy", osm[:, OS_NCP + j * 3:OS_NCP + j * 3 + 3], XP.ap[:, 1024:1027]),
                 reads=[XP.r()], writes=[OSM.r(OS_NCP + j * 3, OS_NCP + j * 3 + 3)])
            ncsv = osm[:, OS_NCS + j * 48:OS_NCS + (j + 1) * 48].rearrange("p (s k) -> p s k", k=3)
            S.op("dve", I("tensor_copy", ncsv, XS3[:, :, 4:7]), reads=[XS.r()],
                 writes=[OSM.r(OS_NCS + j * 48, OS_NCS + (j + 1) * 48)])
            S.op("act", I("activation", XCb.ap, XCs[sl].ap, AF.Copy), reads=[XCs[sl].r()], writes=[XCb.r()])
            for ti, (t0, tn) in enumerate(TILES):
                mm_group(PS(6, 0, tn), [(BDA[:, j * 128:(j + 1) * 128], XCb.ap[:, t0:t0 + tn])], [PM.r(), XCb.r(t0, t0 + tn)], PR(6, 0, tn))
                mm_group(PS(7, 0, tn), [(BDX[:, j * 128:(j + 1) * 128], XCb.ap[:, t0:t0 + tn])], [PM.r(), XCb.r(t0, t0 + tn)], PR(7, 0, tn))
                S.op("act", (lambda t0, tn: I("activation", Rs[sl].ap[:, t0:t0 + tn], PS(6, 0, tn), AF.Sigmoid,
                                                                  bias=pv[:, PV_BA + j:PV_BA + j + 1]))(t0, tn),
                     reads=[PR(6, 0, tn), PV.r()], writes=[Rs[sl].r(t0, t0 + tn)])
                S.op("act", (lambda t0, tn: I("activation", Is[sl].ap[:, t0:t0 + tn], PS(7, 0, tn), AF.Sigmoid,
                                                                  bias=pv[:, PV_BX + j:PV_BX + j + 1]))(t0, tn),
                     reads=[PR(7, 0, tn), PV.r()], writes=[Is[sl].r(t0, t0 + tn)])

        def ma_tail_a(j):
            sl = j % 2
            S.op("act", I("activation", A_.ap, Rs[sl].ap, AF.Exp, scale=CL.ap[:, j:j + 1]), reads=[Rs[sl].r(), CL.r()], writes=[A_.r()])
            S.op("dve", I("tensor_tensor", T1.ap, A_.ap, A_.ap, ALU.mult), reads=[A_.r()], writes=[T1.r()])
            S.op("dve", I("tensor_scalar", T1.ap, T1.ap, 1.0, None, ALU.min), reads=[T1.r()], writes=[T1.r()])
            S.op("act", I("activation", T1.ap, T1.ap, AF.Sqrt, bias=CONSTS.ap[:, 2:3], scale=-1.0),
                 reads=[T1.r(), CONSTS.r()], writes=[T1.r()])
            S.op("dve", I("tensor_tensor_scan", P_.ap, A_.ap[:, 0:1024], ZERO.ap, 1.0, ALU.mult, ALU.add),
                 reads=[A_.r(0, 1024), ZERO.r()], writes=[P_.r()])

        def ma_tail_c(j):
            sl = j % 2
            S.op("dve", I("tensor_tensor", T1.ap, T1.ap, Is[sl].ap, ALU.mult), reads=[T1.r(), Is[sl].r()], writes=[T1.r()])
            S.op("dve", I("tensor_scalar", TMPS.ap[:, 0:1], T1.ap[:, 0:1], pv[:, PV_FLG + 2:PV_FLG + 3], None, ALU.mult),
                 reads=[T1.r(0, 1), PV.r()], writes=[TMPS.r(0, 1)])
            S.op("dve", I("scalar_tensor_tensor", T1.ap[:, 0:1], Is[sl].ap[:, 0:1], pv[:, PV_FLG:PV_FLG + 1], TMPS.ap[:, 0:1],
                                                         ALU.mult, ALU.add),
                 reads=[Is[sl].r(0, 1), PV.r(), TMPS.r(0, 1)], writes=[T1.r(0, 1)])
            S.op("dve", I("tensor_tensor", B_.ap, T1.ap, XCs[sl].ap, ALU.mult), reads=[T1.r(), XCs[sl].r()], writes=[B_.r()])
            S.op("dve", I("tensor_tensor_scan", YL.ap, A_.ap[:, 0:1024], B_.ap[:, 0:1024], 0.0, ALU.mult, ALU.add),
                 reads=[A_.r(0, 1024), B_.r(0, 1024)], writes=[YL.r()])
            S.op("dve", I("tensor_copy", HLE.ap[:, j:j + 1], YL.ap[:, 1023:1024]), reads=[YL.r()], writes=[HLE.r(j, j + 1)])
            S.op("dve", I("tensor_copy", PEND.ap[:, j:j + 1], P_.ap[:, 1023:1024]), reads=[P_.r()], writes=[PEND.r(j, j + 1)])
            a_s, b_s, y_s = s4(A_.ap[:, SOFF:NT]), s4(B_.ap[:, SOFF:NT]), s4(YLS.ap)
            for t in range(4):
                prev = pv[:, PV_SH0 + j * 16:PV_SH0 + (j + 1) * 16] if t == 0 else y_s[:, :, t - 1]
                prd = [PV.r()] if t == 0 else [YLS.r()]
                S.op("dve", (lambda t, prev: I("tensor_tensor", HS.ap, a_s[:, :, t], prev, ALU.mult))(t, prev),
                     reads=[A_.r(SOFF, NT)] + prd, writes=[HS.r()])
                S.op("dve", (lambda t: I("tensor_tensor", y_s[:, :, t], HS.ap, b_s[:, :, t], ALU.add))(t),
                     reads=[HS.r(), B_.r(SOFF, NT)], writes=[YLS.r()])
            S.op("dve", I("tensor_copy", osm[:, OS_NHS + j * 16:OS_NHS + (j + 1) * 16], y_s[:, :, 3]),
                 reads=[YLS.r()], writes=[OSM.r(OS_NHS + j * 16, OS_NHS + (j + 1) * 16)])
            S.op("dve", I("tensor_tensor", YA3[:, j, 0:1024], YL.ap, Gs[sl].ap[:, 0:1024], ALU.mult),
                 reads=[YL.r(), Gs[sl].r(0, 1024)], writes=[YA.r(j * NT, j * NT + 1024)])
            S.op("dve", I("tensor_tensor", PG3[:, j, :], P_.ap, Gs[sl].ap[:, 0:1024], ALU.mult),
                 reads=[P_.r(), Gs[sl].r(0, 1024)], writes=[PG.r(j * 1024, (j + 1) * 1024)])
            S.op("dve", I("tensor_tensor", YA3[:, j, SOFF:NT], YLS.ap, Gs[sl].ap[:, SOFF:NT], ALU.mult),
                 reads=[YLS.r(), Gs[sl].r(SOFF, NT)], writes=[YA.r(j * NT + SOFF, (j + 1) * NT)])


        ma_front(0)
        for j in range(8):
            ma_tail_a(j)
            if j < 7:
                ma_front(j + 1)
            ma_tail_c(j)

        S.op("sp", I("dma_start", out=cin.ap(), in_=HLE.ap), reads=[HLE.r()], writes=[("d_cin", 0, 32)], dma=True)
        S.op("pool", I("collective_compute", "AllGather", ALU.bypass, replica_groups=[[0, 1], [2, 3], [4, 5], [6, 7]],
                                                    ins=[cin.ap().opt()], outs=[cout.ap().opt()]),
             reads=[("d_cin", 0, 32)], writes=[("d_cout", 0, 32)], dma=True, inc=1)
        S.op("sp", I("dma_start", out=GATH.ap.rearrange("p (r j) -> p r j", j=8),
                                         in_=cout.ap().rearrange("(r p) j -> p r j", p=128)),
             reads=[("d_cout", 0, 32)], writes=[GATH.r()], dma=True)

        VAR = Arena(arena_t, XA_LO + XA_WORDS)
        VAR.pos = vn_mark
        VNB = VAR.alloc(9 * 1024, BF16)
        SGAB = VAR.alloc(4 * 512, F32)
        VNB3 = VNB.v3(1024)
        AR.pos = scr_mark
        PBC = AR.alloc(NPB, F32)
        VNF = AR.alloc(1024, F32)
        VNS = AR.alloc(1024, F32)
        GU = AR.alloc(NT, F32)
        ST1 = AR.alloc(512, F32)
        BST = AR.alloc(12, F32)
        MV = AR.alloc(2, F32)
        RS1 = AR.alloc(1, F32)
        S.op("sp", I("dma_start", out=PBC.ap, in_=pbc), writes=[PBC.r()], dma=True)
        wv, wvr = ring_load(wgv, 16384)
        nvs_op = None
        for tt in range(9):
            ntok = 128 if tt < 8 else 64
            c0 = tt * 128 if tt < 8 else SOFF
            bA, bB = (0, 1) if tt % 2 == 0 else (2, 3)
            for hb, bank in ((0, bA), (1, bB)):
                mm_group(PS(bank, 0, 512, 0, ntok),
                         [(U3[:, k, c0:c0 + ntok], wv[:, k * 1024 + hb * 512:k * 1024 + (hb + 1) * 512]) for k in range(16)],
                         [wvr] + [ur(k, c0, c0 + ntok) for k in range(16)], PR(bank))
            S.op("dve", I("bn_stats", BST.ap[0:ntok, 0:6], PS(bA, 0, 512, 0, ntok)), reads=[PR(bA)], writes=[BST.r(0, 6)])
            S.op("dve", I("bn_stats", BST.ap[0:ntok, 6:12], PS(bB, 0, 512, 0, ntok)), reads=[PR(bB)], writes=[BST.r(6, 12)])
            S.op("dve", I("bn_aggr", MV.ap[0:ntok, :], BST.ap[0:ntok, :]), reads=[BST.r()], writes=[MV.r()])
            S.op("act", I("activation", RS1.ap[0:ntok, :], MV.ap[0:ntok, 1:2], AF.Sqrt, bias=CONSTS.ap[0:ntok, 1:2]),
                 reads=[MV.r(), CONSTS.r()], writes=[RS1.r()])
            S.op("dve", I("reciprocal", RS1.ap[0:ntok, :], RS1.ap[0:ntok, :]), reads=[RS1.r()], writes=[RS1.r()])
            for hb, bank in ((0, bA), (1, bB)):
                S.op("dve", (lambda hb, bank: I("tensor_scalar", VNF.ap[0:ntok, hb * 512:(hb + 1) * 512], PS(bank, 0, 512, 0, ntok),
                                                                        MV.ap[0:ntok, 0:1], RS1.ap[0:ntok, 0:1], ALU.subtract, ALU.mult))(hb, bank),
                     reads=[PR(bank), MV.r(), RS1.r()], writes=[VNF.r(hb * 512, (hb + 1) * 512)])
            S.op("dve", I("tensor_tensor", VNF.ap[0:ntok, :], VNF.ap[0:ntok, :], PBC.ap[0:ntok, PB_G:PB_G + 1024], ALU.mult),
                 reads=[VNF.r(), PBC.r()], writes=[VNF.r()])
            S.op("dve", I("tensor_tensor", VNB3[0:ntok, tt, :], VNF.ap[0:ntok, :], PBC.ap[0:ntok, PB_B:PB_B + 1024], ALU.add),
                 reads=[VNF.r(), PBC.r()], writes=[VNB.r(tt * 1024, (tt + 1) * 1024)])
            if tt == 8:
                S.op("dve", I("tensor_tensor", VNS.ap[0:64, :], VNF.ap[0:64, :], PBC.ap[0:64, PB_B:PB_B + 1024], ALU.add),
                     reads=[VNF.r(), PBC.r()], writes=[VNS.r()])
                nvs_op = S.op("sp", I("dma_start", out=nvs, in_=VNS.ap[0:64, :]), reads=[VNS.r()], writes=[("d_nvs", 0, 32)], dma=True)

        GA3 = GATH.ap.rearrange("p (r j) -> p r j", j=8)
        for r in range(2):
            ohr = pv[:, PV_OH + r:PV_OH + r + 1]
            if r == 0:
                S.op("dve", I("tensor_scalar", HIN.ap, GA3[:, 0, :], ohr, None, ALU.mult), reads=[GATH.r(), PV.r()], writes=[HIN.r()])
            else:
                S.op("dve", (lambda r, ohr: I("scalar_tensor_tensor", HIN.ap, GA3[:, r, :], ohr, HIN.ap, ALU.mult, ALU.add))(r, ohr),
                     reads=[GATH.r(), PV.r(), HIN.r()], writes=[HIN.r()])
        for j in range(8):
            S.op("dve", (lambda j: I("scalar_tensor_tensor", YA3[:, j, 0:1024], PG3[:, j, :], HIN.ap[:, j:j + 1], YA3[:, j, 0:1024],
                                                                    ALU.mult, ALU.add))(j),
                 reads=[PG.r(j * 1024, (j + 1) * 1024), HIN.r(), YA.r(j * NT, j * NT + 1024)], writes=[YA.r(j * NT, j * NT + 1024)])
        S.op("dve", I("tensor_tensor", osm[:, OS_NHP:OS_NHP + 8], PEND.ap, HIN.ap, ALU.mult),
             reads=[PEND.r(), HIN.r()], writes=[OSM.r(OS_NHP, OS_NHP + 8)])
        S.op("dve", I("tensor_tensor", osm[:, OS_NHP:OS_NHP + 8], osm[:, OS_NHP:OS_NHP + 8], HLE.ap, ALU.add),
             reads=[OSM.r(OS_NHP, OS_NHP + 8), HLE.r()], writes=[OSM.r(OS_NHP, OS_NHP + 8)])
        osm_op = S.op("sp", I("dma_start", out=osmall, in_=osm), reads=[OSM.r()], writes=[("d_osm", 0, 32)], dma=True)

        YBAR = Arena(arena_t, XA_LO + XA_WORDS)
        YBAR.pos = PG.lo // 4
        YB = YBAR.alloc(8 * NT, BF16)
        assert YBAR.pos <= vn_mark
        YB3 = YB.v3(NT)
        S.op("dve", I("memset", YB3[:, :, 1024:NPL], 0.0), writes=[YB.r()])
        WSTm = PM.ap[:, PM_WST:PM_WST + 1024]
        WSSm = PM64.ap[0:64, PM64_WSS:PM64_WSS + 512]
        for g in range(8):
            wu, wur = ring_load(win[16 + g], 2048)
            for ti, (t0, tn) in enumerate(TILES):
                mm_group(PS(3 + ti, 0, tn), [(wu[:, k * 128:(k + 1) * 128], U3[:, k, t0:t0 + tn]) for k in range(16)],
                         [wur] + [ur(k, t0, t0 + tn) for k in range(16)], PR(3 + ti, 0, tn))
                S.op("act", (lambda ti, t0, tn: I("activation", GU.ap[:, t0:t0 + tn], PS(3 + ti, 0, tn), AF.Copy))(ti, t0, tn),
                     reads=[PR(3 + ti, 0, tn)], writes=[GU.r(t0, t0 + tn)])
            for hb in range(2):
                fn = [I("matmul", PS(hb, q * 128, (q + 1) * 128), VNB3[:, hb * 4 + q, g * 128:(g + 1) * 128],
                        WSTm[:, g * 128:(g + 1) * 128], start=True, stop=True) for q in range(4)]
                S.op("pe", fn, reads=[VNB.r(hb * 4096, (hb + 1) * 4096), PM.r()], writes=[PR(hb)])
            S.op("pe", (lambda g: I("matmul", PS(2, 4, 68), VNB3[0:64, 8, g * 128:(g + 1) * 128],
                                                     WSSm[:, g * 64:(g + 1) * 64], start=True, stop=True))(g),
                 reads=[VNB.r(8 * 1024, 9 * 1024), PM64.r()], writes=[PR(2, 4, 68)])
            bsp = PBC.ap[:, PB_BSP + g * 128:PB_BSP + (g + 1) * 128].unsqueeze(1).to_broadcast([128, 4, 128])
            for hb in range(2):
                st3 = ST1.ap.rearrange("p (q t) -> p q t", t=128)
                ps3 = PS(hb, 0, 512).rearrange("p (q t) -> p q t", t=128)
                S.op("dve", (lambda ps3, st3: I("tensor_tensor", st3, ps3, bsp, ALU.add))(ps3, st3),
                     reads=[PR(hb), PBC.r()], writes=[ST1.r()])
                S.op("dve", (lambda hb: I("tensor_tensor", YB3[:, g, hb * 512:(hb + 1) * 512], ST1.ap,
                                                                   GU.ap[:, hb * 512:(hb + 1) * 512], ALU.mult))(hb),
                     reads=[ST1.r(), GU.r(hb * 512, (hb + 1) * 512)], writes=[YB.r(g * NT + hb * 512, g * NT + (hb + 1) * 512)])
            S.op("dve", I("tensor_tensor", ST1.ap[:, 0:64], PS(2, 4, 68), PBC.ap[:, PB_BSS + g * 64:PB_BSS + (g + 1) * 64], ALU.add),
                 reads=[PR(2, 4, 68), PBC.r()], writes=[ST1.r(0, 64)])
            S.op("dve", I("tensor_tensor", YB3[:, g, SOFF:NT], ST1.ap[:, 0:64], GU.ap[:, SOFF:NT], ALU.mult),
                 reads=[ST1.r(0, 64), GU.r(SOFF, NT)], writes=[YB.r(g * NT + SOFF, (g + 1) * NT)])
            if g % 2 == 0:
                ada_blocks(24 + g // 2, 25 + g // 2, 6, 7)

        AR.pos = scr_mark
        M_ = AR.alloc(16 * NT, BF16)
        M3 = M_.v3(NT)
        YA3, YB3 = YA.v3(NT), YB.v3(NT)
        cnt = 0
        for c in range(16):
            wa_, war = ring_load(wpa[c], 1024)
            wb_, wbr = ring_load(wpb[c], 1024)
            wga, wgar = ring_load(win[32 + c], 2048)
            wgb, wgbr = ring_load(win[48 + c], 2048)
            for ti, (t0, tn) in enumerate(TILES):
                s = cnt % 2
                cnt += 1
                b0 = 4 * s
                mm_group(PS(b0, 0, tn), [(wa_[:, k * 128:(k + 1) * 128], YA3[:, k, t0:t0 + tn]) for k in range(8)],
                         [war] + [YA.r(k * NT + t0, k * NT + t0 + tn) for k in range(8)], PR(b0, 0, tn))
                mm_group(PS(b0 + 1, 0, tn), [(wb_[:, k * 128:(k + 1) * 128], YB3[:, k, t0:t0 + tn]) for k in range(8)],
                         [wbr] + [YB.r(k * NT + t0, k * NT + t0 + tn) for k in range(8)], PR(b0 + 1, 0, tn))
                mm_group(PS(b0 + 2, 0, tn), [(wga[:, k * 128:(k + 1) * 128], U3[:, k, t0:t0 + tn]) for k in range(16)],
                         [wgar] + [ur(k, t0, t0 + tn) for k in range(16)], PR(b0 + 2, 0, tn))
                mm_group(PS(b0 + 3, 0, tn), [(wgb[:, k * 128:(k + 1) * 128], U3[:, k, t0:t0 + tn]) for k in range(16)],
                         [wgbr] + [ur(k, t0, t0 + tn) for k in range(16)], PR(b0 + 3, 0, tn))
                sa = SGAB.ap[:, (2 * s) * 512:(2 * s) * 512 + tn]
                sb = SGAB.ap[:, (2 * s + 1) * 512:(2 * s + 1) * 512 + tn]
                sar = SGAB.r((2 * s) * 512, (2 * s) * 512 + tn)
                sbr = SGAB.r((2 * s + 1) * 512, (2 * s + 1) * 512 + tn)
                S.op("act", (lambda sa, b0, tn: I("activation", sa, PS(b0 + 2, 0, tn), AF.Sigmoid))(sa, b0, tn),
                     reads=[PR(b0 + 2, 0, tn)], writes=[sar])
                S.op("act", (lambda sb, b0, tn: I("activation", sb, PS(b0 + 3, 0, tn), AF.Sigmoid))(sb, b0, tn),
                     reads=[PR(b0 + 3, 0, tn)], writes=[sbr])
                S.op("dve", (lambda sa, b0, tn: I("tensor_tensor", sa, sa, PS(b0, 0, tn), ALU.mult))(sa, b0, tn),
                     reads=[sar, PR(b0, 0, tn)], writes=[sar])
                S.op("dve", (lambda sb, b0, tn: I("tensor_tensor", sb, sb, PS(b0 + 1, 0, tn), ALU.mult))(sb, b0, tn),
                     reads=[sbr, PR(b0 + 1, 0, tn)], writes=[sbr])
                S.op("dve", (lambda sa, sb, c, t0, tn: I("tensor_tensor", M3[:, c, t0:t0 + tn], sa, sb, ALU.add))(sa, sb, c, t0, tn),
                     reads=[sar, sbr], writes=[M_.r(c * NT + t0, c * NT + t0 + tn)])

        for c in range(16):
            S.op("sp", (lambda c: I("dma_start", out=X3[:, c, :], in_=xspill[c]))(c), reads=[("d_spill", c * 32, c * 32 + 32)],
                 writes=[xr(c)], dma=True)
        cnt = 0
        for c in range(16):
            wo, wor = ring_load(wout[c], 2048)
            for ti, (t0, tn) in enumerate(TILES):
                bank = cnt % 4
                cnt += 1
                mm_group(PS(bank, 0, tn), [(wo[:, k * 128:(k + 1) * 128], M3[:, k, t0:t0 + tn]) for k in range(16)],
                         [wor] + [M_.r(k * NT + t0, k * NT + t0 + tn) for k in range(16)], PR(bank, 0, tn))
                resid_acc(bank, c, ti, 80)
            if c % 4 == 0:
                ada_blocks(28 + c // 4, 29 + c // 4, 6, 7)
        mod_affine(112, 1.0, 1.0)
        layernorm(1, 112, 96, False, ln2_hook)
        ffn(1, 128, None)
        outs = layernorm(2, 0, 0, True)
        S.final_wait("sp", outs + [osm_op, nvs_op])

        with nc.Block() as block:
            @block.tensor
            def _(e):
                S.emit_all("pe", e)

            @block.scalar
            def _(e):
                S.emit_all("act", e)

            @block.vector
            def _(e):
                S.emit_all("dve", e)

            @block.gpsimd
            def _(e):
                S.emit_all("pool", e)

            @block.sync
            def _(e):
                S.emit_all("sp", e)
    return nc


_NC_CACHE = {}


def _blk(w, kchunks):
    K, N = w.shape
    return np.ascontiguousarray(w.reshape(kchunks, 128, N // 128, 128).transpose(2, 1, 0, 3).reshape(N // 128, 128, kchunks * 128))


def _fm(v):
    return np.ascontiguousarray(v.reshape(-1, 128).T)


def kernel(x_prompt, x_sample, state_conv, state_h, c_prompt, c_sample,
           w_ada, b_ada, ffn1_w_gu, ffn1_w_down, ffn2_w_gu, ffn2_w_down,
           w_in, conv_w, conv_b, lru_wa, lru_ba, lru_wx, lru_bx, lru_lambda,
           gmlp_ln_g, gmlp_ln_b, gmlp_ws, gmlp_bs, w_pa, w_pb, w_out, ln_g, ln_b):
    f = np.float32
    A = lambda a: np.asarray(a, dtype=f)
    x_prompt, x_sample, state_conv, state_h = A(x_prompt), A(x_sample), A(state_conv), A(state_h)
    c_prompt, c_sample = A(c_prompt), A(c_sample)
    w_ada, b_ada, w_in = A(w_ada)[0], A(b_ada)[0], A(w_in)[0]
    gus = [A(ffn1_w_gu)[0], A(ffn2_w_gu)[0]]
    dns = [A(ffn1_w_down)[0], A(ffn2_w_down)[0]]
    conv_w, conv_b = A(conv_w)[0], A(conv_b)[0]
    lru_wa, lru_ba, lru_wx, lru_bx, lru_lambda = A(lru_wa)[0], A(lru_ba)[0], A(lru_wx)[0], A(lru_bx)[0], A(lru_lambda)[0]
    gmlp_ln_g, gmlp_ln_b, gmlp_ws, gmlp_bs = A(gmlp_ln_g)[0], A(gmlp_ln_b)[0], A(gmlp_ws)[0], A(gmlp_bs)[0]
    w_pa, w_pb, w_out, ln_g, ln_b = A(w_pa)[0], A(w_pb)[0], A(w_out)[0], A(ln_g)[0], A(ln_b)[0]

    shared = {}
    shared["wada"] = np.ascontiguousarray(w_ada.reshape(16, 128, 36, 512).transpose(2, 1, 0, 3).reshape(36, 128, 8192))
    for i in range(2):
        g = gus[i][:, :DFF].reshape(16, 128, NJ, 128)
        v = gus[i][:, DFF:].reshape(16, 128, NJ, 128)
        gv = np.stack([g, v], axis=3)
        shared["wgu%d" % (i + 1)] = np.ascontiguousarray(gv.transpose(2, 1, 0, 3, 4).reshape(NJ, 128, 4096))
        shared["wd%d" % (i + 1)] = np.ascontiguousarray(dns[i].reshape(NJ, 128, 2048))
    shared["win"] = _blk(w_in, 16)
    shared["wgv"] = np.ascontiguousarray(w_in[:, 3072:4096].reshape(16, 128, 1024).transpose(1, 0, 2).reshape(128, 16384))
    shared["wpa"] = _blk(w_pa, 8)
    shared["wpb"] = _blk(w_pb, 8)
    shared["wout"] = _blk(w_out, 16)
    pmat = np.zeros((128, NPM), f)
    for j in range(8):
        for h in range(2):
            pmat[h * 64:(h + 1) * 64, PM_BDA + j * 128 + h * 64:PM_BDA + j * 128 + (h + 1) * 64] = lru_wa[2 * j + h]
            pmat[h * 64:(h + 1) * 64, PM_BDX + j * 128 + h * 64:PM_BDX + j * 128 + (h + 1) * 64] = lru_wx[2 * j + h]
    pmat[:, PM_WST:PM_WST + 1024] = gmlp_ws.transpose(2, 0, 1).reshape(128, 1024)
    s_i, t_i = np.arange(128)[:, None], np.arange(128)[None, :]
    pmat[:, PM_MASK:PM_MASK + 128] = (s_i <= t_i).astype(f)
    shared["pmat"] = pmat
    pm64 = np.zeros((64, NPM64), f)
    w4t = gmlp_ws[:, :4, :4].transpose(2, 0, 1)
    pm64[:, PM64_WSS:PM64_WSS + 512] = np.tile(w4t[None, :, :, None, :], (16, 1, 1, 16, 1)).reshape(64, 8 * 64)
    q = np.arange(64)
    pm64[:, PM64_MASK:PM64_MASK + 64] = ((q[:, None] // 4 == q[None, :] // 4) & (q[:, None] % 4 <= q[None, :] % 4)).astype(f)
    shared["pmat64"] = pm64
    pbc = np.zeros((128, NPB), f)
    pbc[:, PB_G:PB_G + 1024] = gmlp_ln_g[None, :]
    pbc[:, PB_B:PB_B + 1024] = gmlp_ln_b[None, :]
    pbc[:, PB_BSP:PB_BSP + 1024] = gmlp_bs.reshape(1, 1024)
    pbc[:, PB_BSS:PB_BSS + 512] = np.tile(gmlp_bs[:, None, :4], (1, 16, 1)).reshape(1, 512)
    shared["pbc"] = pbc

    pv_base = np.zeros((128, NPV), f)
    pv_base[:, PV_BADA:PV_BADA + 144] = _fm(b_ada)
    pv_base[:, PV_LNG:PV_LNG + 48] = ln_g.reshape(3, 16, 128).transpose(2, 0, 1).reshape(128, 48)
    pv_base[:, PV_LNB:PV_LNB + 48] = ln_b.reshape(3, 16, 128).transpose(2, 0, 1).reshape(128, 48)
    pv_base[:, PV_CW:PV_CW + 32] = conv_w.reshape(4, 8, 128).transpose(2, 1, 0).reshape(128, 32)
    pv_base[:, PV_CB:PV_CB + 8] = _fm(conv_b)
    pv_base[:, PV_BA:PV_BA + 8] = _fm(lru_ba)
    pv_base[:, PV_BX:PV_BX + 8] = _fm(lru_bx)
    pv_base[:, PV_LAM:PV_LAM + 8] = _fm(lru_lambda)
    pv_base[0:NMODR, PV_ID:PV_ID + NMODR] = np.eye(NMODR, dtype=f)

    in_maps = []
    for r in range(NCORES):
        b, half = r // 2, r % 2
        xt = np.empty((NT, D), f)
        xt[0:1024] = x_prompt[b, half * 1024:(half + 1) * 1024]
        xt[1024:1028] = x_prompt[b, 1020:1024] if half == 1 else x_prompt[b, 0:4]
        xt[1028:1092] = x_sample[16 * r:16 * (r + 1)].reshape(64, D)
        m = dict(shared)
        m["xT"] = np.ascontiguousarray(xt.T.reshape(16, 128, NT))
        pvr = pv_base.copy()
        pvr[:, PV_FLG + 0] = 1.0 if half == 0 else 0.0
        pvr[:, PV_FLG + 1] = 1.0 if half == 1 else 0.0
        pvr[:, PV_FLG + 2] = 0.0 if half == 0 else 1.0
        if half == 1:
            pvr[:, PV_OH + 0] = 1.0
        sc = state_conv[0, 16 * r:16 * (r + 1)]
        pvr[:, PV_SCONV:PV_SCONV + 384] = sc.reshape(16, 3, 8, 128).transpose(3, 2, 0, 1).reshape(128, 384)
        sh = state_h[0, 16 * r:16 * (r + 1)]
        pvr[:, PV_SH0:PV_SH0 + 128] = sh.reshape(16, 8, 128).transpose(2, 1, 0).reshape(128, 128)
        crow = np.concatenate([c_prompt[b:b + 1], c_sample[16 * r:16 * (r + 1)]], axis=0)
        pvr[:, PV_CT:PV_CT + 16 * NMODR] = crow.reshape(NMODR, 16, 128).transpose(2, 1, 0).reshape(128, 16 * NMODR)
        m["pvec"] = pvr
        in_maps.append(m)

    if "nc" not in _NC_CACHE:
        _NC_CACHE["nc"] = build_program()
    nc = _NC_CACHE["nc"]
    res = run_bass_kernel_spmd(nc, in_maps, core_ids=list(range(NCORES)))
    R = res.results

    y_prompt = np.empty((4, 2048, D), f)
    y_sample = np.empty((128, 4, D), f)
    ncp = np.empty((1, 4, 3, 1024), f)
    nhp = np.empty((1, 4, 1024), f)
    ncs = np.empty((1, 128, 3, 1024), f)
    nhs = np.empty((1, 128, 1024), f)
    nv = np.empty((1, 128, 4, 1024), f)
    for r in range(NCORES):
        b, half = r // 2, r % 2
        yt = np.asarray(R[r]["yT"]).reshape(D, NT).T
        y_prompt[b, half * 1024:(half + 1) * 1024] = yt[0:1024]
        y_sample[16 * r:16 * (r + 1)] = yt[1028:1092].reshape(16, 4, D)
        osm = np.asarray(R[r]["osmall"])
        if half == 1:
            ncp[0, b] = osm[:, OS_NCP:OS_NCP + 24].reshape(128, 8, 3).transpose(2, 1, 0).reshape(3, 1024)
            nhp[0, b] = osm[:, OS_NHP:OS_NHP + 8].T.reshape(1024)
        ncs[0, 16 * r:16 * (r + 1)] = osm[:, OS_NCS:OS_NCS + 384].reshape(128, 8, 16, 3).transpose(2, 3, 1, 0).reshape(16, 3, 1024)
        nhs[0, 16 * r:16 * (r + 1)] = osm[:, OS_NHS:OS_NHS + 128].reshape(128, 8, 16).transpose(2, 1, 0).reshape(16, 1024)
        nv[0, 16 * r:16 * (r + 1)] = np.asarray(R[r]["nvs"]).reshape(16, 4, 1024)
    return (y_prompt, y_sample, ncp, nhp, ncs, nhs, nv)
```
